# Optimizing a Trainium2 kernel written in Bass

```python
import jax
import jax.numpy as jnp
from jax import lax
import numpy as np


D_MODEL = 1024
BATCH = 4
SEQ = 8192
DEPTH = 1

PLE_DIM = 256
EXPAND = 2
D_MIX = EXPAND * D_MODEL
GLA_WIDTH = D_MIX // 2
GLA_HEADS = 4
GLA_KEY_WIDTH = GLA_WIDTH // 2
GLA_HEAD_K = GLA_KEY_WIDTH // GLA_HEADS
GLA_HEAD_V = GLA_WIDTH // GLA_HEADS
GLA_GATE_RANK = 16
GLA_GATE_NORMALIZER = 16.0
GLA_CHUNK = 64
SSD_WIDTH = D_MIX - GLA_WIDTH
SSD_HEAD_DIM = 64
SSD_HEADS = SSD_WIDTH // SSD_HEAD_DIM
SSD_GROUPS = 2
SSD_HEADS_PER_GROUP = SSD_HEADS // SSD_GROUPS
SSD_STATE = 128
SSD_CONV = 4
SSD_CHUNK = 64
SSD_CONV_DIM = SSD_WIDTH + 2 * SSD_GROUPS * SSD_STATE
EPS = 1e-6

SPLIT_SIZES = (GLA_KEY_WIDTH, GLA_KEY_WIDTH, GLA_WIDTH, GLA_WIDTH, GLA_GATE_RANK, SSD_WIDTH, SSD_CONV_DIM, SSD_HEADS)
D_IN_PROJ = sum(SPLIT_SIZES)

kernel_name = 'hymba_gla_ssd_hybrid_block'


def rmsnorm(x, w):
    xf = x.astype(jnp.float32)
    y = xf * lax.rsqrt(jnp.mean(xf * xf, axis=-1, keepdims=True) + EPS)
    return (y * w.astype(jnp.float32)).astype(x.dtype)


def split_columns(t, sizes):
    offs = np.cumsum(sizes)[:-1].tolist()
    return jnp.split(t, offs, axis=-1)


def causal_depthwise_conv(x, w, b):
    k_width = w.shape[0]
    t_len = x.shape[1]
    xp = jnp.pad(x, ((0, 0), (k_width - 1, 0), (0, 0)))
    out = b
    for j in range(k_width):
        out = out + xp[:, j:j + t_len] * w[j]
    return out


def gla_chunked(q, k, v, log_a):
    b_, t_, h_, dk = q.shape
    dv = v.shape[-1]
    n = t_ // GLA_CHUNK

    def to_chunks(a):
        return a.reshape(b_, n, GLA_CHUNK, h_, a.shape[-1]).transpose(1, 0, 3, 2, 4)

    causal = jnp.tril(jnp.ones((GLA_CHUNK, GLA_CHUNK), dtype=bool))

    def step(state, inp):
        qi, ki, vi, gi = inp
        b = jnp.cumsum(gi, axis=2)
        b_last = b[:, :, -1:, :]
        q_dec = qi * jnp.exp(b)
        scores = jnp.einsum('bhik,bhjk->bhij', q_dec, ki * jnp.exp(-b))
        scores = jnp.where(causal, scores, 0.0)
        o = jnp.einsum('bhij,bhjv->bhiv', scores, vi) + jnp.einsum('bhik,bhkv->bhiv', q_dec, state)
        k_dec = ki * jnp.exp(b_last - b)
        state = state * jnp.exp(b_last[:, :, 0, :])[..., None] + jnp.einsum('bhjk,bhjv->bhkv', k_dec, vi)
        return state, o

    state0 = jnp.zeros((b_, h_, dk, dv), q.dtype)
    _, o = lax.scan(step, state0, (to_chunks(q * dk ** -0.5), to_chunks(k), to_chunks(v), to_chunks(log_a)))
    return o.transpose(1, 0, 3, 2, 4).reshape(b_, t_, h_, dv)


def ssd_chunked(x, dt, a_neg, bm, cm):
    b_, t_, h_, p_ = x.shape
    n = t_ // SSD_CHUNK
    g_, hg, ns = SSD_GROUPS, SSD_HEADS_PER_GROUP, SSD_STATE
    xc = x.reshape(b_, n, SSD_CHUNK, g_, hg, p_).transpose(1, 0, 2, 3, 4, 5)
    dtc = dt.reshape(b_, n, SSD_CHUNK, g_, hg).transpose(1, 0, 2, 3, 4)
    bc = bm.reshape(b_, n, SSD_CHUNK, g_, ns).transpose(1, 0, 2, 3, 4)
    cc = cm.reshape(b_, n, SSD_CHUNK, g_, ns).transpose(1, 0, 2, 3, 4)
    a_g = a_neg.reshape(g_, hg)
    causal = jnp.tril(jnp.ones((SSD_CHUNK, SSD_CHUNK), dtype=bool))[None, :, :, None, None]

    def step(state, inp):
        xi, dti, bi, ci = inp
        cum = jnp.cumsum(dti * a_g, axis=1)
        seg = cum[:, :, None] - cum[:, None, :]
        decay_ij = jnp.where(causal, jnp.exp(jnp.minimum(seg, 0.0)), 0.0)
        cb = jnp.einsum('bign,bjgn->bijg', ci, bi)
        xdt = xi * dti[..., None]
        y = jnp.einsum('bijgh,bjghp->bighp', cb[..., None] * decay_ij, xdt)
        y = y + jnp.einsum('bign,bghpn->bighp', ci, state) * jnp.exp(cum)[..., None]
        decay_to_end = jnp.exp(cum[:, -1:] - cum)
        state = state * jnp.exp(cum[:, -1])[..., None, None] + jnp.einsum('bjgn,bjghp->bghpn', bi, xdt * decay_to_end[..., None])
        return state, y

    state0 = jnp.zeros((b_, g_, hg, p_, ns), x.dtype)
    _, y = lax.scan(step, state0, (xc, dtc, bc, cc))
    return y.transpose(1, 0, 2, 3, 4, 5).reshape(b_, t_, h_, p_)


def hybrid_layer(h, p_i, norm_w, w_in, gla_gate_up, gla_gate_b, gla_norm_w, conv_w, conv_b,
                 dt_bias, a_log, d_skip, ssd_norm_w, w_out, w_pe, w_pe_gate, pe_norm_w):
    b_, t_, _ = h.shape
    f32 = jnp.float32
    u = rmsnorm(h, norm_w)
    proj = jnp.matmul(u, w_in).astype(f32)
    q, k, v, g, gate_lr, z, xbc, dt_raw = split_columns(proj, SPLIT_SIZES)

    log_a = jax.nn.log_sigmoid(jnp.matmul(gate_lr, gla_gate_up.astype(f32)) + gla_gate_b.astype(f32)) / GLA_GATE_NORMALIZER
    heads_k = lambda a: a.reshape(b_, t_, GLA_HEADS, GLA_HEAD_K)
    o = gla_chunked(heads_k(q), heads_k(k), v.reshape(b_, t_, GLA_HEADS, GLA_HEAD_V), heads_k(log_a))
    gla_out = rmsnorm(o, gla_norm_w).reshape(b_, t_, GLA_WIDTH) * jax.nn.silu(g)

    xbc = jax.nn.silu(causal_depthwise_conv(xbc, conv_w.astype(f32), conv_b.astype(f32)))
    xs, bm, cm = split_columns(xbc, (SSD_WIDTH, SSD_GROUPS * SSD_STATE, SSD_GROUPS * SSD_STATE))
    dt = jax.nn.softplus(dt_raw + dt_bias.astype(f32))
    a_neg = -jnp.exp(a_log.astype(f32))
    xs_h = xs.reshape(b_, t_, SSD_HEADS, SSD_HEAD_DIM)
    y = ssd_chunked(xs_h, dt, a_neg, bm.reshape(b_, t_, SSD_GROUPS, SSD_STATE), cm.reshape(b_, t_, SSD_GROUPS, SSD_STATE))
    y = (y + xs_h * d_skip.astype(f32)[:, None]).reshape(b_, t_, SSD_WIDTH) * jax.nn.silu(z)
    y = rmsnorm(y.reshape(b_, t_, SSD_GROUPS, SSD_WIDTH // SSD_GROUPS), ssd_norm_w.reshape(SSD_GROUPS, -1)).reshape(b_, t_, SSD_WIDTH)

    mixed = jnp.concatenate([gla_out, y], axis=-1)
    h = h + jnp.matmul(mixed, w_out.astype(f32)).astype(h.dtype)

    gate = jax.nn.sigmoid(jnp.matmul(rmsnorm(h, pe_norm_w), w_pe_gate).astype(f32))
    h = h + (gate * jnp.matmul(p_i, w_pe).astype(f32)).astype(h.dtype)
    return h


def setup_inputs(seed: int = 0) -> dict:
    key = jax.random.key(seed)
    ks = jax.random.split(key, 20)
    f32 = jnp.float32
    nrm = lambda k, shape, s: jax.random.normal(k, shape, f32) * s
    x = jax.random.normal(ks[0], (BATCH, SEQ, D_MODEL), f32)
    p = jax.random.normal(ks[1], (DEPTH, BATCH, SEQ, PLE_DIM), f32)
    norm_w = 1.0 + nrm(ks[2], (DEPTH, D_MODEL), 0.02)
    w_in = nrm(ks[3], (DEPTH, D_MODEL, D_IN_PROJ), D_MODEL ** -0.5)
    gla_gate_up = nrm(ks[4], (DEPTH, GLA_GATE_RANK, GLA_KEY_WIDTH), GLA_GATE_RANK ** -0.5)
    gla_gate_b = nrm(ks[5], (DEPTH, GLA_KEY_WIDTH), 0.1)
    gla_norm_w = 1.0 + nrm(ks[6], (DEPTH, GLA_HEAD_V), 0.02)
    conv_w = nrm(ks[7], (DEPTH, SSD_CONV, SSD_CONV_DIM), 0.5)
    conv_b = nrm(ks[8], (DEPTH, SSD_CONV_DIM), 0.02)
    dt0 = jnp.exp(jax.random.uniform(ks[9], (DEPTH, SSD_HEADS), f32) * (np.log(0.1) - np.log(0.001)) + np.log(0.001))
    dt_bias = dt0 + jnp.log(-jnp.expm1(-dt0))
    a_log = jnp.log(jax.random.uniform(ks[10], (DEPTH, SSD_HEADS), f32, 1.0, 16.0))
    d_skip = 1.0 + nrm(ks[11], (DEPTH, SSD_HEADS), 0.1)
    ssd_norm_w = 1.0 + nrm(ks[12], (DEPTH, SSD_WIDTH), 0.02)
    w_out = nrm(ks[13], (DEPTH, D_MIX, D_MODEL), D_MIX ** -0.5)
    w_pe = nrm(ks[14], (DEPTH, PLE_DIM, D_MODEL), PLE_DIM ** -0.5)
    w_pe_gate = nrm(ks[15], (DEPTH, D_MODEL, D_MODEL), D_MODEL ** -0.5)
    pe_norm_w = 1.0 + nrm(ks[16], (DEPTH, D_MODEL), 0.02)
    final_norm_w = 1.0 + nrm(ks[17], (D_MODEL,), 0.02)
    return {'x': x, 'p': p, 'norm_w': norm_w, 'w_in': w_in, 'gla_gate_up': gla_gate_up,
            'gla_gate_b': gla_gate_b, 'gla_norm_w': gla_norm_w, 'conv_w': conv_w, 'conv_b': conv_b,
            'dt_bias': dt_bias, 'a_log': a_log, 'd_skip': d_skip, 'ssd_norm_w': ssd_norm_w,
            'w_out': w_out, 'w_pe': w_pe, 'w_pe_gate': w_pe_gate, 'pe_norm_w': pe_norm_w,
            'final_norm_w': final_norm_w}


def reference(x, p, norm_w, w_in, gla_gate_up, gla_gate_b, gla_norm_w, conv_w, conv_b,
              dt_bias, a_log, d_skip, ssd_norm_w, w_out, w_pe, w_pe_gate, pe_norm_w, final_norm_w):
    h = x
    for i in range(DEPTH):
        h = hybrid_layer(h, p[i], norm_w[i], w_in[i], gla_gate_up[i], gla_gate_b[i], gla_norm_w[i],
                         conv_w[i], conv_b[i], dt_bias[i], a_log[i], d_skip[i], ssd_norm_w[i],
                         w_out[i], w_pe[i], w_pe_gate[i], pe_norm_w[i])
    return rmsnorm(h, final_norm_w)
```

```python
import threading
import numpy as np
import concourse.bass as bass
import concourse.mybir as mybir
from concourse.bass_utils import run_bass_kernel_spmd

F32 = mybir.dt.float32
BF16 = mybir.dt.bfloat16
AF = mybir.ActivationFunctionType
ALU = mybir.AluOpType

ENGS = ["tensor", "vector", "scalar", "gpsimd", "sync"]

D = 1024
DIN = 5664
C_Q, C_K, C_V, C_G, C_GLR, C_Z, C_XBC, C_DT = 0, 512, 1024, 2048, 3072, 3088, 4112, 5648
EPS = 1e-6
STOP = None
NT = 2
NTOK = NT * 128


class Sched:
    def __init__(self, nc):
        self.nc = nc
        self.ops = []
        self.last_writer = {}
        self.readers = {}
        self.eng_count = {e: 0 for e in ENGS}
        self.dma_streams = {}

    def _chk(self, names, is_write=False):
        out = []
        for b in names:
            if "#" in b:
                base, rest = b.split("#", 1)
                i = 0
                while i < len(rest) and rest[i].isdigit():
                    i += 1
                gen, suf = int(rest[:i]), rest[i:]
                assert GEN.get(base, 0) == gen, "stale buffer use: %s (current gen %d)" % (b, GEN.get(base, 0))
                out.append(base + suf)
            else:
                out.append(b)
        return out

    def op(self, eng, fn, r=(), w=(), dma=None, ndma=1):
        r = self._chk(r)
        w = self._chk(w)
        idx = len(self.ops)
        deps = set()
        for b in r:
            if b in self.last_writer:
                deps.add(self.last_writer[b])
        for b in w:
            if b in self.last_writer:
                deps.add(self.last_writer[b])
            for q in self.readers.get(b, ()):
                deps.add(q)
        deps.discard(idx)
        o = dict(idx=idx, eng=eng, fn=fn, deps=sorted(deps), dma=dma, ndma=ndma)
        if dma is None:
            self.eng_count[eng] += 1
            o["seq"] = self.eng_count[eng]
        else:
            c = self.dma_streams.get(dma, 0) + ndma
            self.dma_streams[dma] = c
            o["seq"] = c * 16
        self.ops.append(o)
        for b in r:
            self.readers.setdefault(b, []).append(idx)
        for b in w:
            self.last_writer[b] = idx
            self.readers[b] = []
        if CUR[0] is not None:
            CUR[0].pause()
        return idx

    def emit(self, final_waits=()):
        nc = self.nc
        sems = {e: nc.alloc_semaphore("sem_" + e) for e in ENGS}
        dsems = {s: nc.alloc_semaphore("dsem_" + s) for s in self.dma_streams}
        ops = self.ops
        per_eng = {e: [o for o in ops if o["eng"] == e] for e in ENGS}

        def sem_of(o):
            return dsems[o["dma"]] if o["dma"] is not None else sems[o["eng"]]

        def key_of(o):
            return ("d", o["dma"]) if o["dma"] is not None else ("e", o["eng"])

        def run(eng_name, eng):
            waited = {}
            if eng_name != "gpsimd" and per_eng["gpsimd"]:
                eng.wait_ge(sems["gpsimd"], 1)
                waited[("e", "gpsimd")] = 1
            for o in per_eng[eng_name]:
                need = {}
                for d in o["deps"]:
                    p = ops[d]
                    k = key_of(p)
                    if eng_name == "tensor" and k == ("e", "tensor"):
                        continue
                    if need.get(k, (0, None))[0] < p["seq"]:
                        need[k] = (p["seq"], sem_of(p))
                for k, (v, s) in need.items():
                    if waited.get(k, 0) >= v:
                        continue
                    eng.wait_ge(s, v)
                    waited[k] = v
                res = o["fn"](eng)
                if o["dma"] is not None:
                    rs = res if isinstance(res, (list, tuple)) else [res]
                    assert len(rs) == o["ndma"], (len(rs), o["ndma"])
                    for ins in rs:
                        ins.then_inc(dsems[o["dma"]], 16)
                else:
                    res.then_inc(sems[eng_name], 1)
            if eng_name == "sync":
                for st in final_waits:
                    eng.wait_ge(dsems[st], self.dma_streams[st] * 16)

        with nc.Block() as block:
            @block.tensor
            def _(e):
                run("tensor", e)

            @block.vector
            def _(e):
                run("vector", e)

            @block.scalar
            def _(e):
                run("scalar", e)

            @block.gpsimd
            def _(e):
                run("gpsimd", e)

            @block.sync
            def _(e):
                run("sync", e)


GEN = {}
CUR = [None]


class Stream:
    def __init__(self, fn, banks=None):
        self.fn = fn
        self.banks = banks
        self.go = threading.Semaphore(0)
        self.done = threading.Semaphore(0)
        self.finished = False
        self.exc = None
        self.th = threading.Thread(target=self._run, daemon=True)
        self.started = False

    def _run(self):
        self.go.acquire()
        try:
            self.fn()
        except BaseException as e:
            self.exc = e
        self.finished = True
        self.done.release()

    def step(self):
        if not self.started:
            self.started = True
            self.th.start()
        CUR[0] = self
        self.go.release()
        self.done.acquire()
        CUR[0] = None
        if self.exc is not None:
            raise self.exc

    def pause(self):
        self.done.release()
        self.go.acquire()


ILV = {"AB": True, "tile": True, "p2": True}


def interleave(fns, banks=None, kind="tile", weights=None):
    if not ILV[kind]:
        for f in fns:
            f()
        return
    sts = [Stream(f, banks[i] if banks else None) for i, f in enumerate(fns)]
    while any(not st.finished for st in sts):
        for i, st in enumerate(sts):
            for _ in range(weights[i] if weights else 1):
                if not st.finished:
                    st.step()


class Rot:
    def __init__(self, nc, name, n, shape, dt):
        self.bufs = [(nc.alloc_sbuf_tensor("%s%d" % (name, i), shape, dt), "%s%d" % (name, i)) for i in range(n)]
        self.i = 0

    def get(self):
        (t, n) = self.bufs[self.i % len(self.bufs)]
        self.i += 1
        GEN[n] = GEN.get(n, 0) + 1
        return (t, "%s#%d" % (n, GEN[n]))


def build_program(TP, TM):
    GEN.clear()
    nc = bass.Bass("TRN2", target_bir_lowering=False)
    S = Sched(nc)
    op = S.op

    def din(name, shape):
        return nc.dram_tensor(name, shape, F32, kind="ExternalInput").ap()

    xp_d = din("xp", [TP, D])
    xm_d = din("xm", [TM, D])
    pm_d = din("pm", [TM, 256])
    flag_d = din("flag", [128, 1])
    norm_w_d = din("norm_w", [D])
    w_in_d = din("w_in", [D, DIN])
    up_d = din("gla_gate_up", [16, 512])
    gate_b_d = din("gla_gate_b", [512])
    gla_nw_d = din("gla_norm_w", [256])
    conv_w_d = din("conv_w", [4, 1536])
    conv_b_d = din("conv_b", [1536])
    dt_bias_d = din("dt_bias", [16])
    a_log_d = din("a_log", [16])
    d_skip_d = din("d_skip", [16])
    ssd_nw_d = din("ssd_norm_w", [1024])
    w_out_d = din("w_out", [2048, D])
    w_pe_d = din("w_pe", [256, D])
    w_g_d = din("w_pe_gate", [D, D])
    pe_nw_d = din("pe_norm_w", [D])
    fin_w_d = din("final_norm_w", [D])
    out_d = nc.dram_tensor("out", [TM, D], F32, kind="ExternalOutput").ap()

    def sb(name, shape, dt=F32):
        return nc.alloc_sbuf_tensor(name, shape, dt)

    big = sb("big", [128, 8 * DIN], BF16)
    op("gpsimd", lambda e: e.memset(big[:, 0:8192], 0.0), w=["startgate"])

    ps = nc.alloc_psum_tensor("ps", [128, 8 * 512], F32)
    ps_i = [0]

    ps_l = [0]
    ps_ctr = {}

    def psb(long=False):
        cur = CUR[0]
        if long and not (cur is not None and cur.banks is not None and cur.banks[-1] >= 6):
            k = 6 + ps_l[0] % 2
            ps_l[0] += 1
        else:
            bs = tuple(cur.banks) if (cur is not None and cur.banks is not None) else (0, 1, 2, 3, 4, 5)
            c = ps_ctr.get(bs, 0)
            ps_ctr[bs] = c + 1
            k = bs[c % len(bs)]
        n = "ps%d" % k
        GEN[n] = GEN.get(n, 0) + 1
        return ps[:, k * 512:(k + 1) * 512], "%s#%d" % (n, GEN[n])

    ones_f = sb("ones_f", [128, 128])
    identf = sb("identf", [128, 128])
    ident = sb("ident", [128, 128], BF16)
    L_f = sb("L_f", [128, 128])
    SU_f = sb("SU_f", [128, 128])
    SU_b = sb("SU_b", [128, 128], BF16)
    nhalf = sb("nhalf", [128, 4])
    epsc = sb("epsc", [128, 1])
    rmask = sb("rmask", [128, NTOK])
    NV = 90
    vrows = sb("vrows", [NV, 128])
    vecs = sb("vecs", [128, NV])
    nw = vecs[:, 0:8]
    pnw = vecs[:, 8:16]
    snw = vecs[:, 16:24]
    gnw = vecs[:, 24:26]
    nb = vecs[:, 26:30]
    cb = vecs[:, 30:42]
    cw = vecs[:, 42:90].rearrange("p (j c) -> p j c", j=4)
    dtb_bc = sb("dtb_bc", [128, 16])
    aneg_bc = sb("aneg_bc", [128, 16])
    dsk_bc = sb("dsk_bc", [128, 16])
    flag = sb("flag_sb", [128, 1])
    up_b = sb("up_b", [16, 512], BF16)

    op("gpsimd", lambda e: e.memset(ones_f[:], 1.0), w=["ones_f"])
    op("gpsimd", lambda e: e.memset(nhalf[:], -0.5), w=["nhalf"])
    op("gpsimd", lambda e: e.memset(epsc[:], EPS), w=["epsc"])
    op("gpsimd", lambda e: e.affine_select(out=identf[:], in_=ones_f[:], pattern=[[-1, 128]], compare_op=ALU.is_equal,
                                           fill=0.0, base=0, channel_multiplier=1), r=["ones_f"], w=["identf"])
    op("gpsimd", lambda e: e.affine_select(out=L_f[:], in_=ones_f[:], pattern=[[1, 128]], compare_op=ALU.is_ge,
                                           fill=0.0, base=0, channel_multiplier=-1), r=["ones_f"], w=["L_f"])
    op("gpsimd", lambda e: e.affine_select(out=SU_f[:], in_=ones_f[:], pattern=[[-1, 128]], compare_op=ALU.is_gt,
                                           fill=0.0, base=0, channel_multiplier=1), r=["ones_f"], w=["SU_f"])
    op("vector", lambda e: e.tensor_copy(out=ident[:], in_=identf[:]), r=["identf"], w=["ident"])
    op("vector", lambda e: e.tensor_copy(out=SU_b[:], in_=SU_f[:]), r=["SU_f"], w=["SU_b"])
    op("gpsimd", lambda e: e.memset(rmask[:], 1.0), w=["rmask"])
    op("gpsimd", lambda e: e.memset(rmask[:].rearrange("p (t i) -> p t i", i=128)[:, :, 0], 0.0), w=["rmask"])

    def small_dma(dst, src, name, n=1):
        op("sync", lambda e: e.dma_start(out=dst, in_=src), w=[name], dma="c_" + name)

    def rows_dma(r0, src, nrow, tag):
        op("sync", lambda e: e.dma_start(out=vrows[r0:r0 + nrow, :], in_=src.rearrange("(k p) -> k p", p=128)), w=["vrows_" + tag], dma="c_" + tag)
    rows_dma(0, norm_w_d, 8, "nw")
    rows_dma(8, pe_nw_d, 8, "pnw")
    rows_dma(16, ssd_nw_d, 8, "snw")
    rows_dma(24, gla_nw_d, 2, "gnw")
    rows_dma(26, gate_b_d, 4, "nb")
    rows_dma(30, conv_b_d, 12, "cb")
    for j in range(4):
        rows_dma(42 + 12 * j, conv_w_d[j], 12, "cw%d" % j)
    VR = ["vrows_" + t for t in ("nw", "pnw", "snw", "gnw", "nb", "cb", "cw0", "cw1", "cw2", "cw3")]
    (pvv, pvvn) = psb()
    op("tensor", lambda e: e.matmul(out=pvv[:, 0:NV], lhsT=vrows[:, :], rhs=identf[0:NV, 0:NV], start=True, stop=True), r=VR + ["identf"], w=[pvvn])
    VN = ["nw", "pnw", "snw", "gnw", "nb", "cb", "cw"]
    op("vector", lambda e: e.tensor_copy(out=vecs[:, :], in_=pvv[:, 0:NV]), r=[pvvn], w=VN)
    small_dma(dtb_bc[:], dt_bias_d.partition_broadcast(128), "dtb_bc")
    small_dma(aneg_bc[:], a_log_d.partition_broadcast(128), "aneg_bc")
    small_dma(dsk_bc[:], d_skip_d.partition_broadcast(128), "dsk_bc")
    small_dma(flag[:], flag_d[:, :], "flag")
    op("gpsimd", lambda e: e.tensor_scalar(out=nb, in0=nb, scalar1=-1.0, scalar2=None, op0=ALU.mult), r=["nb"], w=["nb"])
    op("scalar", lambda e: e.activation(out=aneg_bc[:], in_=aneg_bc[:], func=AF.Exp), r=["aneg_bc"], w=["aneg_bc"])
    op("gpsimd", lambda e: e.tensor_scalar(out=aneg_bc[:], in0=aneg_bc[:], scalar1=-1.0, scalar2=None, op0=ALU.mult),
       r=["aneg_bc"], w=["aneg_bc"])


    Win = big[:, :].rearrange("p (k c) -> p k c", k=8)
    Wout = big[:, 0:16384].rearrange("p (k c) -> p k c", k=16)
    Wg = big[:, 16384:24576].rearrange("p (k c) -> p k c", k=8)
    Wpe = big[:, 24576:26624].rearrange("p (k c) -> p k c", k=2)

    Sg = sb("Sg", [128, 4, 256])
    Sgb = sb("Sgb", [128, 4, 256], BF16)
    Ss = sb("Ss", [128, 2, 512])
    Ssb = sb("Ssb", [128, 2, 512], BF16)
    halo = sb("halo", [128, 12, 3])
    for t_, n_ in ((Sg, ["Sg"]), (Sgb, ["Sgb"]), (Ss, ["Ss0", "Ss1"]), (Ssb, ["Ssb"]), (halo, ["halo"])):
        op("gpsimd", (lambda t_: (lambda e: e.memset(t_[:], 0.0)))(t_), w=n_)

    xt_pool = Rot(nc, "xt", 2, [128, D], F32)
    xn_pool = Rot(nc, "xn", NT, [128, D], BF16)
    uT = sb("uT", [128, 8, NTOK], BF16)
    glr = sb("glr", [16, NTOK], BF16)
    qdec2 = [sb("qdec%d" % i, [128, 4, NTOK], BF16) for i in range(2)]
    kinv = sb("kinv", [128, 4, NTOK], BF16)
    kdT = sb("kdT", [128, 4, NTOK], BF16)
    decg2 = [sb("decg%d" % i, [128, 4, NT]) for i in range(2)]
    xc = sb("xc", [128, 10, NTOK], BF16)
    xcC2 = [sb("xcC%d" % i, [128, 2, NTOK], BF16) for i in range(2)]
    gtmp = Rot(nc, "gtmp", 6, [128, NTOK], F32)
    ncl_pool = Rot(nc, "ncl", 2, [128, NT], F32)
    xpre_pool = Rot(nc, "xpre", 4, [128, NTOK + 3], F32)
    small = Rot(nc, "sm", 20, [128, 16], F32)
    V_pool = Rot(nc, "V", 2, [128, 1024], BF16)
    sg_pool = Rot(nc, "sg", 2, [128, 1024], BF16)
    sz_pool = Rot(nc, "sz", 2, [128, 1024], BF16)
    scm_pool = Rot(nc, "scm", 2, [128, 4, 128], BF16)
    kd_pool = Rot(nc, "kd", 2, [128, 512], BF16)
    xdt_pool = Rot(nc, "xdt", 2, [128, 1024], BF16)
    xdd_pool = Rot(nc, "xdd", 2, [128, 1024], BF16)
    xsD_pool = Rot(nc, "xsD", 2, [128, 1024], F32)
    Btm_pool = Rot(nc, "Btm", 2, [128, 256], BF16)
    ex_pool = Rot(nc, "ex", 2, [128, 48], F32)
    rhs2_pool = Rot(nc, "rhs2", 1, [128, 16, 128], BF16)
    dec_pool = Rot(nc, "dec", 2, [128, 8, 128], BF16)
    cbm_pool = Rot(nc, "cbm", 2, [128, 2, 128], BF16)
    MT_pool = Rot(nc, "MT", 2, [128, 8, 128], BF16)
    yt_pool = Rot(nc, "ytmp", 2, [128, 512], F32)
    mixed_pool = Rot(nc, "mixed", 1, [128, 2048], BF16)
    scr_d = nc.dram_tensor("mix_scratch", [max(TM, 128), 2048], BF16, kind="Internal").ap()

    stage_bufs = [(xdt_pool.bufs[0][0][:, :].bitcast(F32), "xdt0"), (xdt_pool.bufs[1][0][:, :].bitcast(F32), "xdt1"),
                  (xdd_pool.bufs[0][0][:, :].bitcast(F32), "xdd0"), (xdd_pool.bufs[1][0][:, :].bitcast(F32), "xdd1"),
                  (V_pool.bufs[0][0][:, :].bitcast(F32), "V0"), (V_pool.bufs[1][0][:, :].bitcast(F32), "V1")]
    stg_i = [0]
    cast_engs = ["vector", "gpsimd", "scalar", "vector", "scalar"]
    ci = [0]

    def load_w(dst3, kc, c0, c1, src_rows, scale_ap, scale_name, wname, extra_r=()):
        (st, stn) = stage_bufs[stg_i[0] % len(stage_bufs)]
        stg_i[0] += 1
        n = c1 - c0
        op("sync", lambda e: e.dma_start(out=st[:, 0:n], in_=src_rows[:, c0:c1]), w=[stn], dma="w_" + stn)
        eng = cast_engs[ci[0] % len(cast_engs)]
        ci[0] += 1
        rr = [stn] + list(extra_r) + ([scale_name] if scale_name else [])
        if eng == "scalar":
            if scale_ap is None:
                f = lambda e: e.activation(out=dst3[:, kc, c0:c1], in_=st[:, 0:n], func=AF.Copy)
            else:
                f = lambda e: e.activation(out=dst3[:, kc, c0:c1], in_=st[:, 0:n], func=AF.Copy, scale=scale_ap)
        else:
            if scale_ap is None:
                f = lambda e: e.tensor_copy(out=dst3[:, kc, c0:c1], in_=st[:, 0:n])
            elif eng == "gpsimd":
                f = lambda e: e.tensor_scalar(out=dst3[:, kc, c0:c1], in0=st[:, 0:n], scalar1=scale_ap, scalar2=0.0, op0=ALU.mult, op1=ALU.add)
            else:
                f = lambda e: e.tensor_scalar(out=dst3[:, kc, c0:c1], in0=st[:, 0:n], scalar1=scale_ap, scalar2=None, op0=ALU.mult)
        op(eng, f, r=rr, w=[wname])

    upst = xsD_pool.bufs[0][0]
    op("sync", lambda e: e.dma_start(out=upst[0:16, 0:512], in_=up_d[:, :]), w=["xsD0"], dma="c_up")
    op("vector", lambda e: e.tensor_copy(out=up_b[:], in_=upst[0:16, 0:512]), r=["xsD0"], w=["up_b"])

    def wn(c0, n):
        return ["W%d" % p for p in range(c0 // 512, (c0 + n - 1) // 512 + 1)]
    ALLW = ["W%d" % p for p in range((DIN + 511) // 512)]
    def load_pieces(pieces):
        for piece in pieces:
            c0 = piece * 512
            for kc in range(8):
                load_w(Win, kc, c0, min(DIN, c0 + 512), w_in_d[kc * 128:(kc + 1) * 128, :], nw[:, kc:kc + 1], "nw", "W%d" % piece)
    load_pieces((6, 0, 1, 8, 9, 10, 11))
    late_pieces = [(2, 3, 4, 5, 7)]

    def rstd_from(ssq_ap, ssq_name, n, inv_count):
        (t1, t1n) = small.get()
        (t2, t2n) = small.get()
        op("gpsimd", lambda e: e.tensor_scalar(out=t1[:, 0:n], in0=ssq_ap, scalar1=inv_count, scalar2=EPS, op0=ALU.mult, op1=ALU.add),
           r=[ssq_name], w=[t1n])
        op("gpsimd", lambda e: e.tensor_tensor(out=t2[:, 0:n], in0=t1[:, 0:n], in1=nhalf[:, 0:n], op=ALU.pow), r=[t1n, "nhalf"], w=[t2n])
        return t2, t2n

    def transposes(src_aps, src_names, dst_ap, dst_name, copy_eng, extra_r=()):
        n = len(src_aps)
        (pb_, pbn) = psb()
        pv = pb_.bitcast(BF16)

        def f(e):
            last = None
            for i, a in enumerate(src_aps):
                last = e.transpose(out=pv[:, i * 128:(i + 1) * 128], in_=a, identity=ident[:])
            return last
        op("tensor", f, r=list(src_names) + ["ident"], w=[pbn])
        src = pv[:, 0:n * 128]
        if dst_ap.ndim == 3:
            src = src.rearrange("p (k t) -> p k t", k=n)
        if copy_eng == "scalar":
            op("scalar", lambda e: e.activation(out=dst_ap, in_=src, func=AF.Copy), r=[pbn] + list(extra_r), w=[dst_name])
        else:
            op("vector", lambda e: e.tensor_copy(out=dst_ap, in_=src), r=[pbn] + list(extra_r), w=[dst_name])

    def inproj_fm(c0, m):
        (pb_, pbn) = psb()

        def f(e):
            last = None
            for kc in range(8):
                last = e.matmul(out=pb_[0:m, 0:NTOK], lhsT=Win[:, kc, c0:c0 + m], rhs=uT[:, kc, :], start=(kc == 0), stop=(kc == 7))
            return last
        op("tensor", f, r=wn(c0, m) + ["uT"], w=[pbn])
        return pb_, pbn

    def inproj_tm(t, c0, n):
        (pb_, pbn) = psb()

        def f(e):
            last = None
            for kc in range(8):
                last = e.matmul(out=pb_[:, 0:n], lhsT=uT[:, kc, t * 128:(t + 1) * 128], rhs=Win[:, kc, c0:c0 + n], start=(kc == 0), stop=(kc == 7))
            return last
        op("tensor", f, r=wn(c0, n) + ["uT"], w=[pbn])
        return pb_, pbn

    def x_loads(src_d, tok0):
        xts = []
        for t in range(NT):
            (xt, xtn) = xt_pool.get()
            r0 = tok0 + t * 128
            op("sync", (lambda xt, r0: lambda e: e.dma_start(out=xt[:], in_=src_d[r0:r0 + 128, :]))(xt, r0), w=[xtn], dma="x_" + xtn.split("#")[0])
            xts.append((xt, xtn))
        return xts

    def phaseA_pre(xts):
        tl = []
        for t in range(NT):
            (xt, xtn) = xts[t]
            (ssq, ssqn) = small.get()
            (xn, xnn) = xn_pool.get()
            op("scalar", (lambda xt, ssq, xn: lambda e: e.activation(out=xn[:], in_=xt[:], func=AF.Square, accum_out=ssq[:, 0:1]))(xt, ssq, xn),
               r=[xtn], w=[ssqn, xnn])
            tl.append((xt, xtn, ssq, ssqn, xn, xnn))
        rl = [rstd_from(ssq[:, 0:1], ssqn, 1, 1.0 / D) for (xt, xtn, ssq, ssqn, xn, xnn) in tl]
        xns = []
        for (xt, xtn, ssq, ssqn, xn, xnn), (rs, rsn) in zip(tl, rl):
            op("vector", (lambda xt, xn, rs: lambda e: e.tensor_scalar(out=xn[:], in0=xt[:], scalar1=rs[:, 0:1], scalar2=None, op0=ALU.mult))(xt, xn, rs),
               r=[xtn, rsn], w=[xnn])
            xns.append((xn, xnn))
        return xns

    def phaseAB(xns, main, par, all_chunks):
        qdec = qdec2[par]; qn = "qdec%d" % par
        decg = decg2[par]; dgn = "decg%d" % par
        xcC = xcC2[par]; xcn = "xcC%d" % par
        for t in range(NT):
            (xn, xnn) = xns[t]
            transposes([xn[:, k * 128:(k + 1) * 128] for k in range(8)], [xnn], uT[:, :, t * 128:(t + 1) * 128], "uT", "vector")

        pb_, pbn = inproj_fm(C_GLR, 16)
        op("scalar", lambda e: e.activation(out=glr[:, :], in_=pb_[0:16, 0:NTOK], func=AF.Copy), r=[pbn], w=["glr"])
        for hp in range(2):
            hs = (2 * hp, 2 * hp + 1)
            H = {}
            for h in hs:
                (pz, pzn) = psb()
                op("tensor", (lambda pz, h: lambda e: e.matmul(out=pz[:, 0:NTOK], lhsT=up_b[:, h * 128:(h + 1) * 128], rhs=glr[:, :], start=True, stop=True))(pz, h),
                   r=["up_b", "glr"], w=[pzn])
                H[h] = dict(pz=pz, pzn=pzn)
            for h in hs:
                d = H[h]
                (d["l"], d["ln"]) = gtmp.get()
                op("scalar", (lambda pz, l, h: lambda e: e.activation(out=l[:], in_=pz[:, 0:NTOK], func=AF.Exp, scale=-1.0, bias=nb[:, h:h + 1]))(d["pz"], d["l"], h),
                   r=[d["pzn"], "nb"], w=[d["ln"]])
            for h in hs:
                d = H[h]
                op("scalar", (lambda l: lambda e: e.activation(out=l[:], in_=l[:], func=AF.Ln, bias=1.0))(d["l"]), r=[d["ln"]], w=[d["ln"]])
            for h in hs:
                d = H[h]
                (d["cl"], d["cln"]) = gtmp.get()
                op("vector", (lambda l, cl: lambda e: e.tensor_tensor_scan(out=cl[:], data0=rmask[:], data1=l[:], initial=0.0, op0=ALU.mult, op1=ALU.add))(d["l"], d["cl"]),
                   r=[d["ln"], "rmask"], w=[d["cln"]])
            for h in hs:
                d = H[h]
                (d["e1"], d["e1n"]) = gtmp.get()
                op("scalar", (lambda cl, e1: lambda e: e.activation(out=e1[:], in_=cl[:], func=AF.Exp, scale=-1.0 / 16))(d["cl"], d["e1"]), r=[d["cln"]], w=[d["e1n"]])
            for h in hs:
                d = H[h]
                op("gpsimd", (lambda e1, h: lambda e: e.tensor_copy(out=decg[:, h, :], in_=e1[:].rearrange("p (t i) -> p t i", i=128)[:, :, 127]))(d["e1"], h),
                   r=[d["e1n"]], w=[dgn])
                (d["ncl"], d["ncln"]) = ncl_pool.get()
                op("gpsimd", (lambda cl, ncl: lambda e: e.tensor_scalar(out=ncl[:], in0=cl[:].rearrange("p (t i) -> p t i", i=128)[:, :, 127],
                                                                         scalar1=-1.0 / 16, scalar2=0.0, op0=ALU.mult, op1=ALU.add))(d["cl"], d["ncl"]), r=[d["cln"]], w=[d["ncln"]])
            for h in hs:
                d = H[h]
                (xb, xbn) = xpre_pool.get()
                d["ed"], d["edn"] = xb[:, 0:NTOK], [xbn, xbn + "h"]

                def fed(e, cl=d["cl"], ed=d["ed"], ncl=d["ncl"]):
                    last = None
                    for t in range(NT):
                        last = e.activation(out=ed[:, t * 128:(t + 1) * 128], in_=cl[:, t * 128:(t + 1) * 128], func=AF.Exp, scale=1.0 / 16, bias=ncl[:, t:t + 1])
                    return last
                op("scalar", fed, r=[d["cln"], d["ncln"]], w=d["edn"])
            if main:
                for h in hs:
                    d = H[h]
                    (xb, xbn) = xpre_pool.get()
                    d["e2"], d["e2n"] = xb[:, 0:NTOK], [xbn, xbn + "h"]
                    op("scalar", (lambda cl, e2: lambda e: e.activation(out=e2, in_=cl[:], func=AF.Exp, scale=1.0 / 16))(d["cl"], d["e2"]), r=[d["cln"]], w=d["e2n"])
                for h in hs:
                    d = H[h]
                    pq, pqn = inproj_fm(C_Q + h * 128, 128)
                    op("vector", (lambda pq, e1, h: lambda e: e.scalar_tensor_tensor(out=qdec[:, h, :], in0=pq[:, 0:NTOK], scalar=128.0 ** -0.5, in1=e1[:],
                                                                                      op0=ALU.mult, op1=ALU.mult))(pq, d["e1"], h), r=[pqn, d["e1n"]], w=[qn])
            for h in hs:
                d = H[h]
                pk, pkn = inproj_fm(C_K + h * 128, 128)
                if main:
                    op("vector", (lambda pk, e2, h: lambda e: e.tensor_tensor(out=kinv[:, h, :], in0=pk[:, 0:NTOK], in1=e2, op=ALU.mult))(pk, d["e2"], h),
                       r=[pkn] + d["e2n"], w=["kinv"])
                op("vector", (lambda pk, ed, h: lambda e: e.tensor_tensor(out=kdT[:, h, :], in0=pk[:, 0:NTOK], in1=ed, op=ALU.mult))(pk, d["ed"], h),
                   r=[pkn] + d["edn"], w=["kdT"])

        nchunks = 12 if all_chunks else 10
        pend = []
        for c0 in range(0, nchunks, 2):
            pair = []
            for c in (c0, c0 + 1):
                pc, pcn = inproj_fm(C_XBC + c * 128, 128)
                (xpre, xpren) = xpre_pool.get()
                op("gpsimd", (lambda xpre, c: lambda e: e.tensor_copy(out=xpre[:, 0:3], in_=halo[:, c, :]))(xpre, c), r=["halo"], w=[xpren + "h"])
                op("scalar", (lambda xpre, pc: lambda e: e.activation(out=xpre[:, 3:3 + NTOK], in_=pc[:, 0:NTOK], func=AF.Copy))(xpre, pc), r=[pcn], w=[xpren])
                op("gpsimd", (lambda xpre, c: lambda e: e.tensor_copy(out=halo[:, c, :], in_=xpre[:, NTOK:NTOK + 3]))(xpre, c), r=[xpren, xpren + "h"], w=["halo"])
                (acc, accn) = gtmp.get()
                op("scalar", (lambda pc, acc, c: lambda e: e.activation(out=acc[:], in_=pc[:, 0:NTOK], func=AF.Identity,
                                                                        scale=cw[:, 3, c:c + 1], bias=cb[:, c:c + 1]))(pc, acc, c),
                   r=[pcn, "cw", "cb"], w=[accn])
                pair.append((c, xpre, xpren, acc, accn))
            for j in range(3):
                for (c, xpre, xpren, acc, accn) in pair:
                    op("vector", (lambda xpre, acc, c, j: lambda e: e.scalar_tensor_tensor(out=acc[:], in0=xpre[:, j:j + NTOK], scalar=cw[:, j, c:c + 1], in1=acc[:],
                                                                                            op0=ALU.mult, op1=ALU.add))(xpre, acc, c, j),
                       r=[xpren, xpren + "h", accn, "cw"], w=[accn])
            for f in pend:
                f()
            pend = []
            for (c, xpre, xpren, acc, accn) in pair:
                if c < 10:
                    pend.append((lambda acc, accn, c: lambda: op("scalar", lambda e: e.activation(out=xc[:, c, :], in_=acc[:], func=AF.Silu), r=[accn], w=["xc"]))(acc, accn, c))
                else:
                    pend.append((lambda acc, accn, c: lambda: op("scalar", lambda e: e.activation(out=xcC[:, c - 10, :], in_=acc[:], func=AF.Silu), r=[accn], w=[xcn]))(acc, accn, c))
        for f in pend:
            f()

    def tile_gen(tok0, t, main, par):
        qdec = qdec2[par]; qn = "qdec%d" % par
        decg = decg2[par]; dgn = "decg%d" % par
        xcC = xcC2[par]; xcn = "xcC%d" % par
        tsl = slice(t * 128, (t + 1) * 128)
        if main:
            (sg, sgn) = sg_pool.get()
            (sz, szn) = sz_pool.get()
            for (dst, dstn, c0) in ((sg, sgn, C_G), (sz, szn, C_Z)):
                for half in range(2):
                    pg, pgn = inproj_tm(t, c0 + half * 512, 512)
                    op("scalar", (lambda pg, dst, half: lambda e: e.activation(out=dst[:, half * 512:(half + 1) * 512], in_=pg[:, :], func=AF.Silu))(pg, dst, half),
                       r=[pgn], w=[dstn])
        pd, pdn = inproj_tm(t, C_DT, 16)
        (dtr, dtrn) = small.get()
        op("vector", (lambda pd, dtr: lambda e: e.tensor_tensor(out=dtr[:], in0=pd[:, 0:16], in1=dtb_bc[:], op=ALU.add))(pd, dtr), r=[pdn, "dtb_bc"], w=[dtrn])
        op("scalar", (lambda dtr: lambda e: e.activation(out=dtr[:], in_=dtr[:], func=AF.Exp))(dtr), r=[dtrn], w=[dtrn])
        (dt, dtn) = small.get()
        op("scalar", (lambda dtr, dt: lambda e: e.activation(out=dt[:], in_=dtr[:], func=AF.Ln, bias=1.0))(dtr, dt), r=[dtrn], w=[dtn])
        (dtA, dtAn) = small.get()
        op("gpsimd", (lambda dt, dtA: lambda e: e.tensor_tensor(out=dtA[:], in0=dt[:], in1=aneg_bc[:], op=ALU.mult))(dt, dtA), r=[dtn, "aneg_bc"], w=[dtAn])
        (V, Vn) = V_pool.get()
        for half in range(2):
            pv, pvn = inproj_tm(t, C_V + half * 512, 512)
            if half == 0:
                op("scalar", (lambda pv, V: lambda e: e.activation(out=V[:, 0:512], in_=pv[:, :], func=AF.Copy))(pv, V), r=[pvn], w=[Vn])
            else:
                op("vector", (lambda pv, V: lambda e: e.tensor_copy(out=V[:, 512:1024], in_=pv[:, :]))(pv, V), r=[pvn], w=[Vn])
        (pcm, pcmn) = psb()

        def fcm(e, pcm=pcm, dtA=dtA):
            e.matmul(out=pcm[:, 0:16], lhsT=L_f[:], rhs=dtA[:], start=True, stop=True)
            e.matmul(out=pcm[:, 16:32], lhsT=SU_f[:], rhs=dtA[:], start=True, stop=True)
            return e.matmul(out=pcm[:, 32:48], lhsT=ones_f[:], rhs=dtA[:], start=True, stop=True)
        op("tensor", fcm, r=["L_f", "SU_f", "ones_f", dtAn], w=[pcmn])
        (ex, exn) = ex_pool.get()
        op("scalar", (lambda pcm, ex: lambda e: e.activation(out=ex[:], in_=pcm[:, 0:48], func=AF.Exp))(pcm, ex), r=[pcmn], w=[exn])
        if main:
            (rhs2, rhs2n) = rhs2_pool.get()
            op("gpsimd", (lambda rhs2, dtA: lambda e: e.tensor_tensor(out=rhs2[:], in0=dtA[:].unsqueeze(2).broadcast_to([128, 16, 128]),
                                                                       in1=L_f[:].unsqueeze(1).broadcast_to([128, 16, 128]), op=ALU.mult))(rhs2, dtA),
               r=[dtAn, "L_f"], w=[rhs2n])
        (pxa, pxan) = psb()
        pxav = pxa.bitcast(BF16)

        def ftx(e, pxav=pxav):
            last = None
            for c in range(8):
                last = e.transpose(out=pxav[:, c * 128:(c + 1) * 128], in_=xc[:, c, tsl], identity=ident[:])
            return last
        op("tensor", ftx, r=["xc", "ident"], w=[pxan])
        (xdt, xdtn) = xdt_pool.get()
        op("vector", (lambda pxav, xdt, dt: lambda e: e.tensor_tensor(out=xdt[:].rearrange("p (h q) -> p h q", h=16),
                                                                        in0=pxav[:, :].rearrange("p (h q) -> p h q", h=16),
                                                                        in1=dt[:].unsqueeze(2).broadcast_to([128, 16, 64]), op=ALU.mult))(pxav, xdt, dt),
           r=[pxan, dtn], w=[xdtn])
        if main:
            (xsD, xsDn) = xsD_pool.get()
            op("vector", (lambda pxav, xsD: lambda e: e.tensor_tensor(out=xsD[:].rearrange("p (h q) -> p h q", h=16),
                                                                        in0=pxav[:, :].rearrange("p (h q) -> p h q", h=16),
                                                                        in1=dsk_bc[:].unsqueeze(2).broadcast_to([128, 16, 64]), op=ALU.mult))(pxav, xsD),
               r=[pxan, "dsk_bc"], w=[xsDn])
        (Btm, Btmn) = Btm_pool.get()
        transposes([xc[:, 8 + g, tsl] for g in range(2)], ["xc"], Btm[:, :], Btmn, "vector")
        (xdd, xddn) = xdd_pool.get()
        op("gpsimd", (lambda xdt, xdd, ex: lambda e: e.tensor_tensor(out=xdd[:].rearrange("p (h q) -> p h q", h=16),
                                                                      in0=xdt[:].rearrange("p (h q) -> p h q", h=16),
                                                                      in1=ex[:, 16:32].unsqueeze(2).broadcast_to([128, 16, 64]), op=ALU.mult))(xdt, xdd, ex),
           r=[xdtn, exn], w=[xddn])
        (kd, kdn) = kd_pool.get()
        transposes([kdT[:, h, tsl] for h in range(4)], ["kdT"], kd[:, :], kdn, "vector")
        MTs = []
        if main:
            (psc, pscn) = psb()

            def fsc(e, psc=psc):
                last = None
                for h in range(4):
                    last = e.matmul(out=psc[:, h * 128:(h + 1) * 128], lhsT=kinv[:, h, tsl], rhs=qdec[:, h, tsl], start=True, stop=True)
                return last
            op("tensor", fsc, r=["kinv", qn], w=[pscn])
            (scm, scmn) = scm_pool.get()
            op("vector", (lambda psc, scm: lambda e: e.tensor_tensor(out=scm[:], in0=psc[:, :].rearrange("p (h i) -> p h i", h=4),
                                                                       in1=L_f[:].unsqueeze(1).broadcast_to([128, 4, 128]), op=ALU.mult))(psc, scm),
               r=[pscn, "L_f"], w=[scmn])
            (pcb, pcbn) = psb()

            def fcb(e, pcb=pcb):
                e.matmul(out=pcb[:, 0:128], lhsT=xc[:, 8, tsl], rhs=xcC[:, 0, tsl], start=True, stop=True)
                return e.matmul(out=pcb[:, 128:256], lhsT=xc[:, 9, tsl], rhs=xcC[:, 1, tsl], start=True, stop=True)
            op("tensor", fcb, r=["xc", xcn], w=[pcbn])
            (cbm, cbmn) = cbm_pool.get()
            op("vector", (lambda pcb, cbm: lambda e: e.tensor_tensor(out=cbm[:], in0=pcb[:, 0:256].rearrange("p (g i) -> p g i", g=2),
                                                                       in1=L_f[:].unsqueeze(1).broadcast_to([128, 2, 128]), op=ALU.mult))(pcb, cbm),
               r=[pcbn, "L_f"], w=[cbmn])
            for g in range(2):
                (dec, decn) = dec_pool.get()
                for q in range(2):
                    (psg, psgn) = psb()
                    hb = g * 8 + q * 4
                    op("tensor", (lambda psg, rhs2, hb: lambda e: e.matmul(out=psg[:, :], lhsT=SU_b[:], rhs=rhs2[:, hb:hb + 4, :], start=True, stop=True))(psg, rhs2, hb),
                       r=["SU_b", rhs2n], w=[psgn])
                    op("scalar", (lambda psg, dec, q: lambda e: e.activation(out=dec[:, q * 4:(q + 1) * 4, :], in_=psg[:, :].rearrange("p (h i) -> p h i", h=4),
                                                                              func=AF.Exp))(psg, dec, q), r=[psgn], w=[decn])
                (MT, MTn) = MT_pool.get()
                op("vector", (lambda MT, dec, cbm, g: lambda e: e.tensor_tensor(out=MT[:], in0=dec[:], in1=cbm[:, g, :].unsqueeze(1).broadcast_to([128, 8, 128]),
                                                                                 op=ALU.mult))(MT, dec, cbm, g), r=[decn, cbmn], w=[MTn])
                MTs.append((MT, MTn))
        yield
        pos = []
        if main:
            for hp in range(2):
                (po, pon) = psb(long=True)

                def fo(e, po=po, hp=hp):
                    last = None
                    for hh in range(2):
                        h = hp * 2 + hh
                        e.matmul(out=po[:, hh * 256:(hh + 1) * 256], lhsT=scm[:, h, :], rhs=V[:, h * 256:(h + 1) * 256], start=True, stop=False)
                        last = e.matmul(out=po[:, hh * 256:(hh + 1) * 256], lhsT=qdec[:, h, tsl], rhs=Sgb[:, h, :], start=False, stop=True)
                    return last
                op("tensor", fo, r=[scmn, Vn, qn, "Sgb"], w=[pon])
                pos.append((po, pon))
        for hp in range(2):
            (pds, pdsn) = psb()

            def fds(e, pds=pds, hp=hp):
                last = None
                for hh in range(2):
                    h = hp * 2 + hh
                    last = e.matmul(out=pds[:, hh * 256:(hh + 1) * 256], lhsT=kd[:, h * 128:(h + 1) * 128], rhs=V[:, h * 256:(h + 1) * 256], start=True, stop=True)
                return last
            op("tensor", fds, r=[kdn, Vn], w=[pdsn])
            for hh in range(2):
                h = hp * 2 + hh
                op("vector", (lambda pds, h, hh: lambda e: e.scalar_tensor_tensor(out=Sg[:, h, :], in0=Sg[:, h, :], scalar=decg[:, h, t:t + 1],
                                                                                   in1=pds[:, hh * 256:(hh + 1) * 256], op0=ALU.mult, op1=ALU.add))(pds, h, hh),
                   r=["Sg", dgn, pdsn], w=["Sg"])
        op("scalar", lambda e: e.activation(out=Sgb[:], in_=Sg[:], func=AF.Copy), r=["Sg"], w=["Sgb"])
        if main:
            (mixed, mixedn) = mixed_pool.get()
            (ssqg, ssqgn) = small.get()
            for h in range(4):
                po, pon = pos[h // 2]
                hh = h % 2
                op("scalar", (lambda po, hh, h: lambda e: e.activation(out=mixed[:, h * 256:(h + 1) * 256], in_=po[:, hh * 256:(hh + 1) * 256], func=AF.Square,
                                                                        accum_out=ssqg[:, h:h + 1]))(po, hh, h), r=[pon], w=[ssqgn, mixedn + "g"])
            rg, rgn = rstd_from(ssqg[:, 0:4], ssqgn, 4, 1.0 / 256)
            for h in range(4):
                po, pon = pos[h // 2]
                hh = h % 2
                op("vector", (lambda po, hh, h: lambda e: e.scalar_tensor_tensor(
                    out=mixed[:, h * 256:(h + 1) * 256], in0=po[:, hh * 256:(hh + 1) * 256], scalar=rg[:, h:h + 1],
                    in1=sg[:, h * 256:(h + 1) * 256], op0=ALU.mult, op1=ALU.mult))(po, hh, h),
                   r=[pon, rgn, sgn], w=[mixedn + "g"])
        yield
        yzs = []
        saved_banks = None
        if CUR[0] is not None and CUR[0].banks == (4, 5):
            saved_banks = CUR[0]
            saved_banks.banks = (4, 5, 6, 7)
        if main:
            yl = []
            for g in range(2):
                (MT, MTn) = MTs[g]
                (py, pyn) = psb()

                def fy(e, py=py, MT=MT, g=g):
                    last = None
                    for hh in range(8):
                        h = g * 8 + hh
                        last = e.matmul(out=py[:, hh * 64:(hh + 1) * 64], lhsT=MT[:, hh, :], rhs=xdt[:, h * 64:(h + 1) * 64], start=True, stop=True)
                    return last
                op("tensor", fy, r=[MTn, xdtn], w=[pyn])
                (pyi, pyin) = psb()
                op("tensor", (lambda pyi, g: lambda e: e.matmul(out=pyi[:, :], lhsT=xcC[:, g, tsl], rhs=Ssb[:, g, :], start=True, stop=True))(pyi, g),
                   r=[xcn, "Ssb"], w=[pyin])
                (t1, t1n) = yt_pool.get()
                yl.append((py, pyn, pyi, pyin, t1, t1n))
            for g in range(2):
                (py, pyn, pyi, pyin, t1, t1n) = yl[g]
                op("vector", (lambda pyi, t1, g: lambda e: e.tensor_tensor(out=t1[:].rearrange("p (h q) -> p h q", h=8),
                                                                            in0=pyi[:, :].rearrange("p (h q) -> p h q", h=8),
                                                                            in1=ex[:, g * 8:(g + 1) * 8].unsqueeze(2).broadcast_to([128, 8, 64]), op=ALU.mult))(pyi, t1, g),
                   r=[pyin, exn], w=[t1n])
            for g in range(2):
                (py, pyn, pyi, pyin, t1, t1n) = yl[g]
                op("vector", (lambda py, t1: lambda e: e.tensor_tensor(out=t1[:], in0=py[:, :], in1=t1[:], op=ALU.add))(py, t1), r=[pyn, t1n], w=[t1n])
                yzs.append((t1, t1n))
        psl = []
        for g in range(2):
            (pss, pssn) = psb()
            op("tensor", (lambda pss, g: lambda e: e.matmul(out=pss[:, :], lhsT=Btm[:, g * 128:(g + 1) * 128], rhs=xdd[:, g * 512:(g + 1) * 512],
                                                             start=True, stop=True))(pss, g), r=[Btmn, xddn], w=[pssn])
            psl.append((pss, pssn))
        if saved_banks is not None:
            saved_banks.banks = (4, 5)
        for g in range(2):
            op("vector", (lambda g: lambda e: e.tensor_tensor(out=Ss[:, g, :].rearrange("p (h q) -> p h q", h=8),
                                                              in0=Ss[:, g, :].rearrange("p (h q) -> p h q", h=8),
                                                              in1=ex[:, 32 + g * 8:40 + g * 8].unsqueeze(2).broadcast_to([128, 8, 64]), op=ALU.mult))(g),
               r=["Ss%d" % g, exn, "Ssb"], w=["Ss%d" % g])
        for g in range(2):
            (pss, pssn) = psl[g]
            op("vector", (lambda pss, g: lambda e: e.tensor_tensor(out=Ss[:, g, :], in0=Ss[:, g, :], in1=pss[:, :], op=ALU.add))(pss, g),
               r=["Ss%d" % g, pssn], w=["Ss%d" % g])
        op("scalar", lambda e: e.activation(out=Ssb[:], in_=Ss[:], func=AF.Copy), r=["Ss0", "Ss1"], w=["Ssb"])
        if main:
            (ssqs, ssqsn) = small.get()
            for g in range(2):
                (t1, t1n) = yzs[g]
                op("vector", (lambda t1, g: lambda e: e.tensor_tensor(out=t1[:], in0=t1[:], in1=xsD[:, g * 512:(g + 1) * 512], op=ALU.add))(t1, g),
                   r=[t1n, xsDn], w=[t1n])
            for g in range(2):
                (t1, t1n) = yzs[g]
                op("vector", (lambda t1, g: lambda e: e.tensor_tensor(out=t1[:], in0=t1[:], in1=sz[:, g * 512:(g + 1) * 512], op=ALU.mult))(t1, g),
                   r=[t1n, szn], w=[t1n])
            for g in range(2):
                (t1, t1n) = yzs[g]
                op("scalar", (lambda t1, g: lambda e: e.activation(out=mixed[:, 1024 + g * 512:1024 + (g + 1) * 512], in_=t1[:], func=AF.Square, accum_out=ssqs[:, g:g + 1]))(t1, g),
                   r=[t1n], w=[ssqsn, mixedn + "s%d" % g])
            rss, rssn = rstd_from(ssqs[:, 0:2], ssqsn, 2, 1.0 / 512)
            for g in range(2):
                t1, t1n = yzs[g]
                op("scalar", (lambda t1, g: lambda e: e.activation(out=mixed[:, 1024 + g * 512:1024 + (g + 1) * 512], in_=t1[:], func=AF.Copy,
                                                                    scale=rss[:, g:g + 1]))(t1, g), r=[t1n, rssn], w=[mixedn + "s%d" % g])
            r0 = tok0 + t * 128
            op("sync", (lambda r0: lambda e: e.dma_start(out=scr_d[r0:r0 + 128, :], in_=mixed[:]))(r0),
               r=[mixedn + "g", mixedn + "s0", mixedn + "s1"], w=["scr%d" % (r0 // 128)], dma="m_" + mixedn.split("#")[0])
        yield

    pendq = []

    def partner(nhalf):
        todo = []
        for a in pendq:
            while a[1] > 0 and len(todo) < nhalf:
                todo.append(a[0])
                a[1] -= 1
        pendq[:] = [a for a in pendq if a[1] > 0]
        if not todo:
            return None

        def f():
            for g in todo:
                next(g)
        return f

    def slot(main_fn, nhalf, extra=None, wmain=1):
        p = partner(nhalf)
        fns, bks, ws = [], [], []
        if p is not None:
            fns.append(p); bks.append((4, 5)); ws.append(1)
        fns.append(main_fn); bks.append((0, 1, 2, 3) if p is not None else (0, 1, 2, 3, 4, 5)); ws.append(wmain if p is not None else (2 if extra is not None else 1))
        if extra is not None:
            fns.append(extra); bks.append(None); ws.append(1)
        interleave(fns, bks, "tile", ws)

    def run_phase(src_d, ntok, main):
        nsup = ntok // NTOK
        if nsup == 0:
            return
        st = {"xns": phaseA_pre(x_loads(src_d, 0)), "xts": x_loads(src_d, NTOK) if nsup > 1 else None}
        for s in range(nsup):
            par = s % 2
            xns = st["xns"]
            extra0 = None
            if late_pieces:
                extra0 = (lambda ps_: lambda: load_pieces(ps_))(late_pieces.pop())
            slot(lambda: phaseAB(xns, main, par, main or s == nsup - 1), 1, extra0, wmain=9)
            for t in range(NT):
                g = tile_gen(s * NTOK, t, main, par)
                extra = None
                if t == NT - 1 and s + 1 < nsup:
                    def extra(s=s):
                        st["xns"] = phaseA_pre(st["xts"])
                        st["xts"] = x_loads(src_d, (s + 2) * NTOK) if s + 2 < nsup else None
                slot((lambda g: lambda: next(g))(g), 1 if t < NT - 1 else 99, extra, wmain=(3 if t < NT - 1 else 1))
                pendq.append([g, 2])

    def mask_gen():
        op("gpsimd", lambda e: e.tensor_scalar(out=Sg[:], in0=Sg[:], scalar1=flag[:, 0:1], scalar2=0.0, op0=ALU.mult, op1=ALU.add), r=["Sg", "flag"], w=["Sg"])
        op("gpsimd", lambda e: e.tensor_scalar(out=Ss[:], in0=Ss[:], scalar1=flag[:, 0:1], scalar2=0.0, op0=ALU.mult, op1=ALU.add),
           r=["Ss0", "Ss1", "flag"], w=["Ss0", "Ss1"])
        op("scalar", lambda e: e.activation(out=Sgb[:], in_=Sg[:], func=AF.Copy), r=["Sg"], w=["Sgb"])
        op("scalar", lambda e: e.activation(out=Ssb[:], in_=Ss[:], func=AF.Copy), r=["Ss0", "Ss1"], w=["Ssb"])
        yield

    if STOP != "setup":
        run_phase(xp_d, TP, False)
    if TP > 0 and STOP not in ("setup", "pre"):
        pendq.append([mask_gen(), 1])
    if STOP not in ("setup", "pre"):
        run_phase(xm_d, TM, True)
    p_last = partner(99)
    if p_last is not None:
        p_last()

    if TM > 0 and STOP in (None, "p2w", "p2s0", "p2s1"):
        bar = sb("bar", [128, 1])
        op("gpsimd", lambda e: e.memset(bar[:], 0.0), w=ALLW + ["bar"])
        BAR = ["bar"]

        def f32view(a, b_):
            return big[:, a:b_].bitcast(F32)
        o_ = 26624
        mixin = [(big[:, o_:o_ + 2048], "mixin0"), (big[:, o_ + 2048:o_ + 4096], "mixin1")]; o_ += 4096
        mixTs = [(big[:, o_:o_ + 2048].rearrange("p (k t) -> p k t", k=16), "mixT0"),
                 (big[:, o_ + 2048:o_ + 4096].rearrange("p (k t) -> p k t", k=16), "mixT1")]; o_ += 4096
        hns = [(big[:, o_:o_ + 1024], "hn0"), (big[:, o_ + 1024:o_ + 2048], "hn1")]; o_ += 2048
        hnTs = [(big[:, o_:o_ + 1024].rearrange("p (k t) -> p k t", k=8), "hnT0"),
                (big[:, o_ + 1024:o_ + 2048].rearrange("p (k t) -> p k t", k=8), "hnT1")]; o_ += 2048
        hbufs = [(f32view(o_, o_ + 2048), "h0"), (f32view(o_ + 2048, o_ + 4096), "h1")]; o_ += 4096
        assert o_ <= 8 * DIN
        tgs = [(xsD_pool.bufs[0][0][:, :], "xsD0"), (xsD_pool.bufs[0][0][:, :], "xsD0")]
        obs = [(xt_pool.bufs[0][0][:, :], "xt0"), (xt_pool.bufs[1][0][:, :], "xt1")]
        xrs = [(mixed_pool.bufs[0][0][:, :].bitcast(F32), "mixed0"), (xsD_pool.bufs[1][0][:, :], "xsD1")]
        pts = [(yt_pool.bufs[0][0][:, 0:256], "ytmp0"), (yt_pool.bufs[1][0][:, 0:256], "ytmp1")]
        pbfs = [(Btm_pool.bufs[0][0][:, :], "Btm0"), (Btm_pool.bufs[1][0][:, :], "Btm1")]
        pTs = [(kd_pool.bufs[0][0][:, 0:256].rearrange("p (k t) -> p k t", k=2), "kd0"),
               (kd_pool.bufs[1][0][:, 0:256].rearrange("p (k t) -> p k t", k=2), "kd1"),
               (scm_pool.bufs[0][0][:, 0:2, :], "scm0")]
        fw_bc = sb("fw_bc", [128, D])
        op("sync", lambda e: e.dma_start(out=fw_bc[:], in_=fin_w_d.partition_broadcast(128)), w=["fw_bc"], dma="c_fw")
        def load_wout():
            for fc in range(16):
                rows = w_out_d[fc * 128:(fc + 1) * 128, :]
                if fc < 8:
                    sc, scn = gnw[:, (fc % 2):(fc % 2) + 1], "gnw"
                else:
                    sc, scn = snw[:, fc - 8:fc - 7], "snw"
                for c0 in (0, 512):
                    load_w(Wout, fc, c0, c0 + 512, rows, sc, scn, "Wout", BAR)

        def load_wg_wpe():
            for kc in range(8):
                for c0 in (0, 512):
                    load_w(Wg, kc, c0, c0 + 512, w_g_d[kc * 128:(kc + 1) * 128, :], pnw[:, kc:kc + 1], "pnw", "Wg", BAR)
            for c in range(2):
                for c0 in (0, 512):
                    load_w(Wpe, c, c0, c0 + 512, w_pe_d[c * 128:(c + 1) * 128, :], None, None, "Wpe", BAR)
        p2_loaders = [load_wg_wpe, load_wout]

        def p2_gen(ti):
            r0 = ti * 128
            (mi, min_) = mixin[ti % 2]
            (mixT, mixTn) = mixTs[ti % 2]
            (xr, xrn) = xrs[ti % 2]
            (pt, ptn) = pts[ti % 2]
            (pbf, pbfn) = pbfs[ti % 2]
            (pT, pTn) = pTs[ti % 3]
            (hb_, hn_) = hbufs[ti % 2]
            (hn, hnn) = hns[ti % 2]
            (hnT, hnTn) = hnTs[ti % 2]
            (tg, tgn0) = tgs[ti % 2]
            (ob, obn) = obs[ti % 2]
            op("sync", lambda e: e.dma_start(out=mi, in_=scr_d[r0:r0 + 128, :]), r=["scr%d" % ti] + BAR, w=[min_], dma="mi_" + min_)
            xr_w = [xrn] + (["mixed0g", "mixed0s0", "mixed0s1"] if xrn == "mixed0" else [])
            op("sync", lambda e: e.dma_start(out=xr, in_=xm_d[r0:r0 + 128, :]), w=xr_w, dma="x2_" + xrn)
            op("sync", lambda e: e.dma_start(out=pt, in_=pm_d[r0:r0 + 128, :]), w=[ptn], dma="p_" + ptn)
            transposes([mi[:, fc * 128:(fc + 1) * 128] for fc in range(8)], [min_], mixT[:, 0:8, :], mixTn + "a", "scalar")
            transposes([mi[:, fc * 128:(fc + 1) * 128] for fc in range(8, 16)], [min_], mixT[:, 8:16, :], mixTn + "b", "vector")
            op("gpsimd", lambda e: e.tensor_copy(out=pbf, in_=pt), r=[ptn], w=[pbfn])
            transposes([pbf[:, c * 128:(c + 1) * 128] for c in range(2)], [pbfn], pT, pTn, "vector")
            yield
            if STOP == "p2s0":
                return
            for half in range(2):
                (ph, phn) = psb()

                def fh(e, ph=ph, half=half):
                    last = None
                    for fc in range(16):
                        last = e.matmul(out=ph[:, :], lhsT=mixT[:, fc, :], rhs=Wout[:, fc, half * 512:(half + 1) * 512], start=(fc == 0), stop=(fc == 15))
                    return last
                op("tensor", fh, r=[mixTn + "a", mixTn + "b", "Wout"], w=[phn])
                op("vector", (lambda ph, half: lambda e: e.tensor_tensor(out=hb_[:, half * 512:(half + 1) * 512], in0=ph[:, :],
                                                                          in1=xr[:, half * 512:(half + 1) * 512], op=ALU.add))(ph, half),
                   r=[phn, xrn] + BAR, w=[hn_ + "_%d" % half])
            (ssqh, ssqhn) = small.get()
            op("scalar", lambda e: e.activation(out=hn, in_=hb_, func=AF.Square, accum_out=ssqh[:, 0:1]),
               r=[hn_ + "_0", hn_ + "_1"] + BAR, w=[ssqhn, hnn])
            rh, rhn = rstd_from(ssqh[:, 0:1], ssqhn, 1, 1.0 / D)
            op("scalar", lambda e: e.activation(out=hn, in_=hb_, func=AF.Copy, scale=rh[:, 0:1]),
               r=[hn_ + "_0", hn_ + "_1", rhn] + BAR, w=[hnn])
            transposes([hn[:, k * 128:(k + 1) * 128] for k in range(8)], [hnn], hnT, hnTn, "vector")
            yield
            if STOP == "p2s1":
                return
            (ssqf, ssqfn) = small.get()
            for half in range(2):
                (pgt, pgtn) = psb()

                def fg(e, pgt=pgt, half=half):
                    last = None
                    for kc in range(8):
                        last = e.matmul(out=pgt[:, :], lhsT=hnT[:, kc, :], rhs=Wg[:, kc, half * 512:(half + 1) * 512], start=(kc == 0), stop=(kc == 7))
                    return last
                op("tensor", fg, r=[hnTn, "Wg"], w=[pgtn])
                (ppe, ppen) = psb()

                def fpe(e, ppe=ppe, half=half):
                    last = None
                    for c in range(2):
                        last = e.matmul(out=ppe[:, :], lhsT=pT[:, c, :], rhs=Wpe[:, c, half * 512:(half + 1) * 512], start=(c == 0), stop=(c == 1))
                    return last
                op("tensor", fpe, r=[pTn, "Wpe"], w=[ppen])
                hs = slice(half * 512, (half + 1) * 512)
                tgn = tgn0 + "_%d" % half
                op("scalar", (lambda pgt, hs: lambda e: e.activation(out=tg[:, hs], in_=pgt[:, :], func=AF.Tanh, scale=0.5))(pgt, hs), r=[pgtn], w=[tgn0, tgn])
                op("vector", (lambda ppe, hs: lambda e: e.scalar_tensor_tensor(out=tg[:, hs], in0=tg[:, hs], scalar=1.0, in1=ppe[:, :], op0=ALU.add, op1=ALU.mult))(ppe, hs),
                   r=[ppen, tgn], w=[tgn])
                op("vector", (lambda hs, half: lambda e: e.scalar_tensor_tensor(out=hb_[:, hs], in0=tg[:, hs], scalar=0.5, in1=hb_[:, hs], op0=ALU.mult, op1=ALU.add))(hs, half),
                   r=[tgn, hn_ + "_%d" % half], w=[hn_ + "_%d" % half])
            op("scalar", lambda e: e.activation(out=ob, in_=hb_, func=AF.Square, accum_out=ssqf[:, 0:1]),
               r=[hn_ + "_0", hn_ + "_1"], w=[ssqfn, obn])
            rf, rfn = rstd_from(ssqf[:, 0:1], ssqfn, 1, 1.0 / D)
            op("vector", lambda e: e.scalar_tensor_tensor(out=ob, in0=hb_, scalar=rf[:, 0:1], in1=fw_bc[:], op0=ALU.mult, op1=ALU.mult),
               r=[hn_ + "_0", hn_ + "_1", rfn, "fw_bc"], w=[obn])
            op("sync", lambda e: e.dma_start(out=out_d[r0:r0 + 128, :], in_=ob), r=[obn], dma="o_" + obn)
            yield

        active = []
        ntile = TM // 128 if STOP != "p2w" else 0
        nxt = 0
        stage_banks = {0: (0, 1), 1: (2, 3, 4), 2: (5, 6, 7)}
        while nxt < ntile or active:
            fns = []
            bks = []
            for a in active:
                fns.append((lambda g: lambda: next(g, None))(a[0]))
                bks.append(stage_banks[a[1]])
            if nxt < ntile:
                a = [p2_gen(nxt), 0]
                nxt += 1
                active.append(a)
                fns.append((lambda g: lambda: next(g, None))(a[0]))
                bks.append(stage_banks[0])
            if p2_loaders:
                fns.append(p2_loaders.pop())
                bks.append(None)
            interleave(fns, bks, "p2")
            for a in active:
                a[1] += 1
            active = [a for a in active if a[1] < 3]

    S.emit(final_waits=[k for k in S.dma_streams if k.startswith("o_")])
    return nc


WKEYS = ["norm_w", "w_in", "gla_gate_up", "gla_gate_b", "gla_norm_w", "conv_w", "conv_b", "dt_bias", "a_log",
         "d_skip", "ssd_norm_w", "w_out", "w_pe", "w_pe_gate", "pe_norm_w"]


def run_layer(x, p, w, final_norm_w):
    B, T, _ = x.shape
    half = T // 2
    nc = build_program(half, half)
    wmap = {k: np.ascontiguousarray(np.asarray(w[k], dtype=np.float32)) for k in WKEYS}
    wmap["final_norm_w"] = np.ascontiguousarray(np.asarray(final_norm_w, dtype=np.float32))
    in_maps = []
    for b in range(B):
        for s in range(2):
            m = dict(wmap)
            m["xm"] = np.ascontiguousarray(x[b, s * half:(s + 1) * half])
            m["xp"] = np.ascontiguousarray(x[b, 0:half]) if s == 1 else np.zeros((half, D), np.float32)
            m["pm"] = np.ascontiguousarray(p[b, s * half:(s + 1) * half])
            m["flag"] = np.full((128, 1), float(s), np.float32)
            in_maps.append(m)
    ncores = 2 * B
    res = run_bass_kernel_spmd(nc, in_maps, core_ids=list(range(ncores)))
    out = np.empty((B, T, D), np.float32)
    for b in range(B):
        for s in range(2):
            out[b, s * half:(s + 1) * half] = res.results[2 * b + s]["out"]
    return out


def kernel(**inputs):
    x = np.asarray(inputs["x"], dtype=np.float32)
    p = np.asarray(inputs["p"], dtype=np.float32)
    w = {k: np.asarray(inputs[k])[0] for k in WKEYS}
    return run_layer(x, p[0], w, inputs["final_norm_w"])
```

```python
import threading
import numpy as np
import concourse.bass as bass
import concourse.mybir as mybir
from concourse.bass_utils import run_bass_kernel_spmd

F32 = mybir.dt.float32
BF16 = mybir.dt.bfloat16
AF = mybir.ActivationFunctionType
ALU = mybir.AluOpType

ENGS = ["tensor", "vector", "scalar", "gpsimd", "sync"]

D = 1024
DIN = 5664
C_Q, C_K, C_V, C_G, C_GLR, C_Z, C_XBC, C_DT = 0, 512, 1024, 2048, 3072, 3088, 4112, 5648
EPS = 1e-6
STOP = None
NT = 2
NTOK = NT * 128


class Sched:
    def __init__(self, nc):
        self.nc = nc
        self.ops = []
        self.last_writer = {}
        self.readers = {}
        self.eng_count = {e: 0 for e in ENGS}
        self.dma_streams = {}

    def _chk(self, names, is_write=False):
        out = []
        for b in names:
            if "#" in b:
                base, rest = b.split("#", 1)
                i = 0
                while i < len(rest) and rest[i].isdigit():
                    i += 1
                gen, suf = int(rest[:i]), rest[i:]
                assert GEN.get(base, 0) == gen, "stale buffer use: %s (current gen %d)" % (b, GEN.get(base, 0))
                out.append(base + suf)
            else:
                out.append(b)
        return out

    def op(self, eng, fn, r=(), w=(), dma=None, ndma=1):
        r = self._chk(r)
        w = self._chk(w)
        idx = len(self.ops)
        deps = set()
        for b in r:
            if b in self.last_writer:
                deps.add(self.last_writer[b])
        for b in w:
            if b in self.last_writer:
                deps.add(self.last_writer[b])
            for q in self.readers.get(b, ()):
                deps.add(q)
        deps.discard(idx)
        o = dict(idx=idx, eng=eng, fn=fn, deps=sorted(deps), dma=dma, ndma=ndma)
        if dma is None:
            self.eng_count[eng] += 1
            o["seq"] = self.eng_count[eng]
        else:
            c = self.dma_streams.get(dma, 0) + ndma
            self.dma_streams[dma] = c
            o["seq"] = c * 16
        self.ops.append(o)
        for b in r:
            self.readers.setdefault(b, []).append(idx)
        for b in w:
            self.last_writer[b] = idx
            self.readers[b] = []
        if CUR[0] is not None:
            CUR[0].pause()
        return idx

    def emit(self, final_waits=()):
        nc = self.nc
        sems = {e: nc.alloc_semaphore("sem_" + e) for e in ENGS}
        dsems = {s: nc.alloc_semaphore("dsem_" + s) for s in self.dma_streams}
        ops = self.ops
        per_eng = {e: [o for o in ops if o["eng"] == e] for e in ENGS}

        def sem_of(o):
            return dsems[o["dma"]] if o["dma"] is not None else sems[o["eng"]]

        def key_of(o):
            return ("d", o["dma"]) if o["dma"] is not None else ("e", o["eng"])

        def run(eng_name, eng):
            waited = {}
            if eng_name != "gpsimd" and per_eng["gpsimd"]:
                eng.wait_ge(sems["gpsimd"], 1)
                waited[("e", "gpsimd")] = 1
            for o in per_eng[eng_name]:
                need = {}
                for d in o["deps"]:
                    p = ops[d]
                    k = key_of(p)
                    if eng_name == "tensor" and k == ("e", "tensor"):
                        continue
                    if need.get(k, (0, None))[0] < p["seq"]:
                        need[k] = (p["seq"], sem_of(p))
                for k, (v, s) in need.items():
                    if waited.get(k, 0) >= v:
                        continue
                    eng.wait_ge(s, v)
                    waited[k] = v
                res = o["fn"](eng)
                if o["dma"] is not None:
                    rs = res if isinstance(res, (list, tuple)) else [res]
                    assert len(rs) == o["ndma"], (len(rs), o["ndma"])
                    for ins in rs:
                        ins.then_inc(dsems[o["dma"]], 16)
                else:
                    res.then_inc(sems[eng_name], 1)
            if eng_name == "sync":
                for st in final_waits:
                    eng.wait_ge(dsems[st], self.dma_streams[st] * 16)

        with nc.Block() as block:
            @block.tensor
            def _(e):
                run("tensor", e)

            @block.vector
            def _(e):
                run("vector", e)

            @block.scalar
            def _(e):
                run("scalar", e)

            @block.gpsimd
            def _(e):
                run("gpsimd", e)

            @block.sync
            def _(e):
                run("sync", e)


GEN = {}
CUR = [None]


class Stream:
    def __init__(self, fn, banks=None):
        self.fn = fn
        self.banks = banks
        self.go = threading.Semaphore(0)
        self.done = threading.Semaphore(0)
        self.finished = False
        self.exc = None
        self.th = threading.Thread(target=self._run, daemon=True)
        self.started = False

    def _run(self):
        self.go.acquire()
        try:
            self.fn()
        except BaseException as e:
            self.exc = e
        self.finished = True
        self.done.release()

    def step(self):
        if not self.started:
            self.started = True
            self.th.start()
        CUR[0] = self
        self.go.release()
        self.done.acquire()
        CUR[0] = None
        if self.exc is not None:
            raise self.exc

    def pause(self):
        self.done.release()
        self.go.acquire()


ILV = {"AB": True, "tile": True, "p2": True}


def interleave(fns, banks=None, kind="tile", weights=None):
    if not ILV[kind]:
        for f in fns:
            f()
        return
    sts = [Stream(f, banks[i] if banks else None) for i, f in enumerate(fns)]
    while any(not st.finished for st in sts):
        for i, st in enumerate(sts):
            for _ in range(weights[i] if weights else 1):
                if not st.finished:
                    st.step()


class Rot:
    def __init__(self, nc, name, n, shape, dt):
        self.bufs = [(nc.alloc_sbuf_tensor("%s%d" % (name, i), shape, dt), "%s%d" % (name, i)) for i in range(n)]
        self.i = 0

    def get(self):
        (t, n) = self.bufs[self.i % len(self.bufs)]
        self.i += 1
        GEN[n] = GEN.get(n, 0) + 1
        return (t, "%s#%d" % (n, GEN[n]))


def build_program(TP, TM):
    GEN.clear()
    nc = bass.Bass("TRN2", target_bir_lowering=False)
    S = Sched(nc)
    op = S.op

    def din(name, shape):
        return nc.dram_tensor(name, shape, F32, kind="ExternalInput").ap()

    xp_d = din("xp", [TP, D])
    xm_d = din("xm", [TM, D])
    pm_d = din("pm", [TM, 256])
    flag_d = din("flag", [128, 1])
    norm_w_d = din("norm_w", [D])
    w_in_d = din("w_in", [D, DIN])
    up_d = din("gla_gate_up", [16, 512])
    gate_b_d = din("gla_gate_b", [512])
    gla_nw_d = din("gla_norm_w", [256])
    conv_w_d = din("conv_w", [4, 1536])
    conv_b_d = din("conv_b", [1536])
    dt_bias_d = din("dt_bias", [16])
    a_log_d = din("a_log", [16])
    d_skip_d = din("d_skip", [16])
    ssd_nw_d = din("ssd_norm_w", [1024])
    w_out_d = din("w_out", [2048, D])
    w_pe_d = din("w_pe", [256, D])
    w_g_d = din("w_pe_gate", [D, D])
    pe_nw_d = din("pe_norm_w", [D])
    fin_w_d = din("final_norm_w", [D])
    out_d = nc.dram_tensor("out", [TM, D], F32, kind="ExternalOutput").ap()

    def sb(name, shape, dt=F32):
        return nc.alloc_sbuf_tensor(name, shape, dt)

    big = sb("big", [128, 8 * DIN], BF16)
    op("gpsimd", lambda e: e.memset(big[:, 0:8192], 0.0), w=["startgate"])

    ps = nc.alloc_psum_tensor("ps", [128, 8 * 512], F32)
    ps_i = [0]

    ps_l = [0]
    ps_ctr = {}

    def psb(long=False):
        cur = CUR[0]
        if long and not (cur is not None and cur.banks is not None and cur.banks[-1] >= 6):
            k = 6 + ps_l[0] % 2
            ps_l[0] += 1
        else:
            bs = tuple(cur.banks) if (cur is not None and cur.banks is not None) else (0, 1, 2, 3, 4, 5)
            c = ps_ctr.get(bs, 0)
            ps_ctr[bs] = c + 1
            k = bs[c % len(bs)]
        n = "ps%d" % k
        GEN[n] = GEN.get(n, 0) + 1
        return ps[:, k * 512:(k + 1) * 512], "%s#%d" % (n, GEN[n])

    ones_f = sb("ones_f", [128, 128])
    identf = sb("identf", [128, 128])
    ident = sb("ident", [128, 128], BF16)
    L_f = sb("L_f", [128, 128])
    SU_f = sb("SU_f", [128, 128])
    SU_b = sb("SU_b", [128, 128], BF16)
    nhalf = sb("nhalf", [128, 4])
    epsc = sb("epsc", [128, 1])
    rmask = sb("rmask", [128, NTOK])
    NV = 90
    vrows = sb("vrows", [NV, 128])
    vecs = sb("vecs", [128, NV])
    nw = vecs[:, 0:8]
    pnw = vecs[:, 8:16]
    snw = vecs[:, 16:24]
    gnw = vecs[:, 24:26]
    nb = vecs[:, 26:30]
    cb = vecs[:, 30:42]
    cw = vecs[:, 42:90].rearrange("p (j c) -> p j c", j=4)
    dtb_bc = sb("dtb_bc", [128, 16])
    aneg_bc = sb("aneg_bc", [128, 16])
    dsk_bc = sb("dsk_bc", [128, 16])
    flag = sb("flag_sb", [128, 1])
    up_b = sb("up_b", [16, 512], BF16)

    op("gpsimd", lambda e: e.memset(ones_f[:], 1.0), w=["ones_f"])
    op("gpsimd", lambda e: e.memset(nhalf[:], -0.5), w=["nhalf"])
    op("gpsimd", lambda e: e.memset(epsc[:], EPS), w=["epsc"])
    op("gpsimd", lambda e: e.affine_select(out=identf[:], in_=ones_f[:], pattern=[[-1, 128]], compare_op=ALU.is_equal,
                                           fill=0.0, base=0, channel_multiplier=1), r=["ones_f"], w=["identf"])
    op("gpsimd", lambda e: e.affine_select(out=L_f[:], in_=ones_f[:], pattern=[[1, 128]], compare_op=ALU.is_ge,
                                           fill=0.0, base=0, channel_multiplier=-1), r=["ones_f"], w=["L_f"])
    op("gpsimd", lambda e: e.affine_select(out=SU_f[:], in_=ones_f[:], pattern=[[-1, 128]], compare_op=ALU.is_gt,
                                           fill=0.0, base=0, channel_multiplier=1), r=["ones_f"], w=["SU_f"])
    op("vector", lambda e: e.tensor_copy(out=ident[:], in_=identf[:]), r=["identf"], w=["ident"])
    op("vector", lambda e: e.tensor_copy(out=SU_b[:], in_=SU_f[:]), r=["SU_f"], w=["SU_b"])
    op("gpsimd", lambda e: e.memset(rmask[:], 1.0), w=["rmask"])
    op("gpsimd", lambda e: e.memset(rmask[:].rearrange("p (t i) -> p t i", i=128)[:, :, 0], 0.0), w=["rmask"])

    def small_dma(dst, src, name, n=1):
        op("sync", lambda e: e.dma_start(out=dst, in_=src), w=[name], dma="c_" + name)

    def rows_dma(r0, src, nrow, tag):
        op("sync", lambda e: e.dma_start(out=vrows[r0:r0 + nrow, :], in_=src.rearrange("(k p) -> k p", p=128)), w=["vrows_" + tag], dma="c_" + tag)
    rows_dma(0, norm_w_d, 8, "nw")
    rows_dma(8, pe_nw_d, 8, "pnw")
    rows_dma(16, ssd_nw_d, 8, "snw")
    rows_dma(24, gla_nw_d, 2, "gnw")
    rows_dma(26, gate_b_d, 4, "nb")
    rows_dma(30, conv_b_d, 12, "cb")
    for j in range(4):
        rows_dma(42 + 12 * j, conv_w_d[j], 12, "cw%d" % j)
    VR = ["vrows_" + t for t in ("nw", "pnw", "snw", "gnw", "nb", "cb", "cw0", "cw1", "cw2", "cw3")]
    (pvv, pvvn) = psb()
    op("tensor", lambda e: e.matmul(out=pvv[:, 0:NV], lhsT=vrows[:, :], rhs=identf[0:NV, 0:NV], start=True, stop=True), r=VR + ["identf"], w=[pvvn])
    VN = ["nw", "pnw", "snw", "gnw", "nb", "cb", "cw"]
    op("vector", lambda e: e.tensor_copy(out=vecs[:, :], in_=pvv[:, 0:NV]), r=[pvvn], w=VN)
    small_dma(dtb_bc[:], dt_bias_d.partition_broadcast(128), "dtb_bc")
    small_dma(aneg_bc[:], a_log_d.partition_broadcast(128), "aneg_bc")
    small_dma(dsk_bc[:], d_skip_d.partition_broadcast(128), "dsk_bc")
    small_dma(flag[:], flag_d[:, :], "flag")
    op("gpsimd", lambda e: e.tensor_scalar(out=nb, in0=nb, scalar1=-1.0, scalar2=None, op0=ALU.mult), r=["nb"], w=["nb"])
    op("scalar", lambda e: e.activation(out=aneg_bc[:], in_=aneg_bc[:], func=AF.Exp), r=["aneg_bc"], w=["aneg_bc"])
    op("gpsimd", lambda e: e.tensor_scalar(out=aneg_bc[:], in0=aneg_bc[:], scalar1=-1.0, scalar2=None, op0=ALU.mult),
       r=["aneg_bc"], w=["aneg_bc"])


    Win = big[:, :].rearrange("p (k c) -> p k c", k=8)
    Wout = big[:, 0:16384].rearrange("p (k c) -> p k c", k=16)
    Wg = big[:, 16384:24576].rearrange("p (k c) -> p k c", k=8)
    Wpe = big[:, 24576:26624].rearrange("p (k c) -> p k c", k=2)

    Sg = sb("Sg", [128, 4, 256])
    Sgb = sb("Sgb", [128, 4, 256], BF16)
    Ss = sb("Ss", [128, 2, 512])
    Ssb = sb("Ssb", [128, 2, 512], BF16)
    halo = sb("halo", [128, 12, 3])
    for t_, n_ in ((Sg, ["Sg"]), (Sgb, ["Sgb"]), (Ss, ["Ss0", "Ss1"]), (Ssb, ["Ssb"]), (halo, ["halo"])):
        op("gpsimd", (lambda t_: (lambda e: e.memset(t_[:], 0.0)))(t_), w=n_)

    xt_pool = Rot(nc, "xt", 2, [128, D], F32)
    xn_pool = Rot(nc, "xn", NT, [128, D], BF16)
    uT = sb("uT", [128, 8, NTOK], BF16)
    glr = sb("glr", [16, NTOK], BF16)
    qdec2 = [sb("qdec%d" % i, [128, 4, NTOK], BF16) for i in range(2)]
    kinv = sb("kinv", [128, 4, NTOK], BF16)
    kdT = sb("kdT", [128, 4, NTOK], BF16)
    decg2 = [sb("decg%d" % i, [128, 4, NT]) for i in range(2)]
    xc = sb("xc", [128, 10, NTOK], BF16)
    xcC2 = [sb("xcC%d" % i, [128, 2, NTOK], BF16) for i in range(2)]
    gtmp = Rot(nc, "gtmp", 6, [128, NTOK], F32)
    ncl_pool = Rot(nc, "ncl", 2, [128, NT], F32)
    xpre_pool = Rot(nc, "xpre", 4, [128, NTOK + 3], F32)
    small = Rot(nc, "sm", 20, [128, 16], F32)
    V_pool = Rot(nc, "V", 2, [128, 1024], BF16)
    sg_pool = Rot(nc, "sg", 2, [128, 1024], BF16)
    sz_pool = Rot(nc, "sz", 2, [128, 1024], BF16)
    scm_pool = Rot(nc, "scm", 2, [128, 4, 128], BF16)
    kd_pool = Rot(nc, "kd", 2, [128, 512], BF16)
    xdt_pool = Rot(nc, "xdt", 2, [128, 1024], BF16)
    xdd_pool = Rot(nc, "xdd", 2, [128, 1024], BF16)
    xsD_pool = Rot(nc, "xsD", 2, [128, 1024], F32)
    Btm_pool = Rot(nc, "Btm", 2, [128, 256], BF16)
    ex_pool = Rot(nc, "ex", 2, [128, 48], F32)
    rhs2_pool = Rot(nc, "rhs2", 1, [128, 16, 128], BF16)
    dec_pool = Rot(nc, "dec", 2, [128, 8, 128], BF16)
    cbm_pool = Rot(nc, "cbm", 2, [128, 2, 128], BF16)
    MT_pool = Rot(nc, "MT", 2, [128, 8, 128], BF16)
    yt_pool = Rot(nc, "ytmp", 2, [128, 512], F32)
    mixed_pool = Rot(nc, "mixed", 1, [128, 2048], BF16)
    scr_d = nc.dram_tensor("mix_scratch", [max(TM, 128), 2048], BF16, kind="Internal").ap()

    stage_bufs = [(xdt_pool.bufs[0][0][:, :].bitcast(F32), "xdt0"), (xdt_pool.bufs[1][0][:, :].bitcast(F32), "xdt1"),
                  (xdd_pool.bufs[0][0][:, :].bitcast(F32), "xdd0"), (xdd_pool.bufs[1][0][:, :].bitcast(F32), "xdd1"),
                  (V_pool.bufs[0][0][:, :].bitcast(F32), "V0"), (V_pool.bufs[1][0][:, :].bitcast(F32), "V1")]
    stg_i = [0]
    cast_engs = ["vector", "gpsimd", "scalar", "vector", "scalar"]
    ci = [0]

    def load_w(dst3, kc, c0, c1, src_rows, scale_ap, scale_name, wname, extra_r=()):
        (st, stn) = stage_bufs[stg_i[0] % len(stage_bufs)]
        stg_i[0] += 1
        n = c1 - c0
        op("sync", lambda e: e.dma_start(out=st[:, 0:n], in_=src_rows[:, c0:c1]), w=[stn], dma="w_" + stn)
        eng = cast_engs[ci[0] % len(cast_engs)]
        ci[0] += 1
        rr = [stn] + list(extra_r) + ([scale_name] if scale_name else [])
        if eng == "scalar":
            if scale_ap is None:
                f = lambda e: e.activation(out=dst3[:, kc, c0:c1], in_=st[:, 0:n], func=AF.Copy)
            else:
                f = lambda e: e.activation(out=dst3[:, kc, c0:c1], in_=st[:, 0:n], func=AF.Copy, scale=scale_ap)
        else:
            if scale_ap is None:
                f = lambda e: e.tensor_copy(out=dst3[:, kc, c0:c1], in_=st[:, 0:n])
            elif eng == "gpsimd":
                f = lambda e: e.tensor_scalar(out=dst3[:, kc, c0:c1], in0=st[:, 0:n], scalar1=scale_ap, scalar2=0.0, op0=ALU.mult, op1=ALU.add)
            else:
                f = lambda e: e.tensor_scalar(out=dst3[:, kc, c0:c1], in0=st[:, 0:n], scalar1=scale_ap, scalar2=None, op0=ALU.mult)
        op(eng, f, r=rr, w=[wname])

    upst = xsD_pool.bufs[0][0]
    op("sync", lambda e: e.dma_start(out=upst[0:16, 0:512], in_=up_d[:, :]), w=["xsD0"], dma="c_up")
    op("vector", lambda e: e.tensor_copy(out=up_b[:], in_=upst[0:16, 0:512]), r=["xsD0"], w=["up_b"])

    def wn(c0, n):
        return ["W%d" % p for p in range(c0 // 512, (c0 + n - 1) // 512 + 1)]
    ALLW = ["W%d" % p for p in range((DIN + 511) // 512)]
    def load_pieces(pieces):
        for piece in pieces:
            c0 = piece * 512
            for kc in range(8):
                load_w(Win, kc, c0, min(DIN, c0 + 512), w_in_d[kc * 128:(kc + 1) * 128, :], nw[:, kc:kc + 1], "nw", "W%d" % piece)
    load_pieces((6, 0, 1, 8, 9, 10, 11))
    late_pieces = [(2, 3, 4, 5, 7)]

    def rstd_from(ssq_ap, ssq_name, n, inv_count):
        (t1, t1n) = small.get()
        (t2, t2n) = small.get()
        op("gpsimd", lambda e: e.tensor_scalar(out=t1[:, 0:n], in0=ssq_ap, scalar1=inv_count, scalar2=EPS, op0=ALU.mult, op1=ALU.add),
           r=[ssq_name], w=[t1n])
        op("gpsimd", lambda e: e.tensor_tensor(out=t2[:, 0:n], in0=t1[:, 0:n], in1=nhalf[:, 0:n], op=ALU.pow), r=[t1n, "nhalf"], w=[t2n])
        return t2, t2n

    def transposes(src_aps, src_names, dst_ap, dst_name, copy_eng, extra_r=()):
        n = len(src_aps)
        (pb_, pbn) = psb()
        pv = pb_.bitcast(BF16)

        def f(e):
            last = None
            for i, a in enumerate(src_aps):
                last = e.transpose(out=pv[:, i * 128:(i + 1) * 128], in_=a, identity=ident[:])
            return last
        op("tensor", f, r=list(src_names) + ["ident"], w=[pbn])
        src = pv[:, 0:n * 128]
        if dst_ap.ndim == 3:
            src = src.rearrange("p (k t) -> p k t", k=n)
        if copy_eng == "scalar":
            op("scalar", lambda e: e.activation(out=dst_ap, in_=src, func=AF.Copy), r=[pbn] + list(extra_r), w=[dst_name])
        else:
            op("vector", lambda e: e.tensor_copy(out=dst_ap, in_=src), r=[pbn] + list(extra_r), w=[dst_name])

    def inproj_fm(c0, m):
        (pb_, pbn) = psb()

        def f(e):
            last = None
            for kc in range(8):
                last = e.matmul(out=pb_[0:m, 0:NTOK], lhsT=Win[:, kc, c0:c0 + m], rhs=uT[:, kc, :], start=(kc == 0), stop=(kc == 7))
            return last
        op("tensor", f, r=wn(c0, m) + ["uT"], w=[pbn])
        return pb_, pbn

    def inproj_tm(t, c0, n):
        (pb_, pbn) = psb()

        def f(e):
            last = None
            for kc in range(8):
                last = e.matmul(out=pb_[:, 0:n], lhsT=uT[:, kc, t * 128:(t + 1) * 128], rhs=Win[:, kc, c0:c0 + n], start=(kc == 0), stop=(kc == 7))
            return last
        op("tensor", f, r=wn(c0, n) + ["uT"], w=[pbn])
        return pb_, pbn

    def x_loads(src_d, tok0):
        xts = []
        for t in range(NT):
            (xt, xtn) = xt_pool.get()
            r0 = tok0 + t * 128
            op("sync", (lambda xt, r0: lambda e: e.dma_start(out=xt[:], in_=src_d[r0:r0 + 128, :]))(xt, r0), w=[xtn], dma="x_" + xtn.split("#")[0])
            xts.append((xt, xtn))
        return xts

    def phaseA_pre(xts):
        tl = []
        for t in range(NT):
            (xt, xtn) = xts[t]
            (ssq, ssqn) = small.get()
            (xn, xnn) = xn_pool.get()
            op("scalar", (lambda xt, ssq, xn: lambda e: e.activation(out=xn[:], in_=xt[:], func=AF.Square, accum_out=ssq[:, 0:1]))(xt, ssq, xn),
               r=[xtn], w=[ssqn, xnn])
            tl.append((xt, xtn, ssq, ssqn, xn, xnn))
        rl = [rstd_from(ssq[:, 0:1], ssqn, 1, 1.0 / D) for (xt, xtn, ssq, ssqn, xn, xnn) in tl]
        xns = []
        for (xt, xtn, ssq, ssqn, xn, xnn), (rs, rsn) in zip(tl, rl):
            op("vector", (lambda xt, xn, rs: lambda e: e.tensor_scalar(out=xn[:], in0=xt[:], scalar1=rs[:, 0:1], scalar2=None, op0=ALU.mult))(xt, xn, rs),
               r=[xtn, rsn], w=[xnn])
            xns.append((xn, xnn))
        return xns

    def phaseAB(xns, main, par, all_chunks):
        qdec = qdec2[par]; qn = "qdec%d" % par
        decg = decg2[par]; dgn = "decg%d" % par
        xcC = xcC2[par]; xcn = "xcC%d" % par
        for t in range(NT):
            (xn, xnn) = xns[t]
            transposes([xn[:, k * 128:(k + 1) * 128] for k in range(8)], [xnn], uT[:, :, t * 128:(t + 1) * 128], "uT", "vector")

        pb_, pbn = inproj_fm(C_GLR, 16)
        op("scalar", lambda e: e.activation(out=glr[:, :], in_=pb_[0:16, 0:NTOK], func=AF.Copy), r=[pbn], w=["glr"])
        for hp in range(2):
            hs = (2 * hp, 2 * hp + 1)
            H = {}
            for h in hs:
                (pz, pzn) = psb()
                op("tensor", (lambda pz, h: lambda e: e.matmul(out=pz[:, 0:NTOK], lhsT=up_b[:, h * 128:(h + 1) * 128], rhs=glr[:, :], start=True, stop=True))(pz, h),
                   r=["up_b", "glr"], w=[pzn])
                H[h] = dict(pz=pz, pzn=pzn)
            for h in hs:
                d = H[h]
                (d["l"], d["ln"]) = gtmp.get()
                op("scalar", (lambda pz, l, h: lambda e: e.activation(out=l[:], in_=pz[:, 0:NTOK], func=AF.Exp, scale=-1.0, bias=nb[:, h:h + 1]))(d["pz"], d["l"], h),
                   r=[d["pzn"], "nb"], w=[d["ln"]])
            for h in hs:
                d = H[h]
                op("scalar", (lambda l: lambda e: e.activation(out=l[:], in_=l[:], func=AF.Ln, bias=1.0))(d["l"]), r=[d["ln"]], w=[d["ln"]])
            for h in hs:
                d = H[h]
                (d["cl"], d["cln"]) = gtmp.get()
                op("vector", (lambda l, cl: lambda e: e.tensor_tensor_scan(out=cl[:], data0=rmask[:], data1=l[:], initial=0.0, op0=ALU.mult, op1=ALU.add))(d["l"], d["cl"]),
                   r=[d["ln"], "rmask"], w=[d["cln"]])
            for h in hs:
                d = H[h]
                (d["e1"], d["e1n"]) = gtmp.get()
                op("scalar", (lambda cl, e1: lambda e: e.activation(out=e1[:], in_=cl[:], func=AF.Exp, scale=-1.0 / 16))(d["cl"], d["e1"]), r=[d["cln"]], w=[d["e1n"]])
            for h in hs:
                d = H[h]
                op("gpsimd", (lambda e1, h: lambda e: e.tensor_copy(out=decg[:, h, :], in_=e1[:].rearrange("p (t i) -> p t i", i=128)[:, :, 127]))(d["e1"], h),
                   r=[d["e1n"]], w=[dgn])
                (d["ncl"], d["ncln"]) = ncl_pool.get()
                op("gpsimd", (lambda cl, ncl: lambda e: e.tensor_scalar(out=ncl[:], in0=cl[:].rearrange("p (t i) -> p t i", i=128)[:, :, 127],
                                                                         scalar1=-1.0 / 16, scalar2=0.0, op0=ALU.mult, op1=ALU.add))(d["cl"], d["ncl"]), r=[d["cln"]], w=[d["ncln"]])
            for h in hs:
                d = H[h]
                (xb, xbn) = xpre_pool.get()
                d["ed"], d["edn"] = xb[:, 0:NTOK], [xbn, xbn + "h"]

                def fed(e, cl=d["cl"], ed=d["ed"], ncl=d["ncl"]):
                    last = None
                    for t in range(NT):
                        last = e.activation(out=ed[:, t * 128:(t + 1) * 128], in_=cl[:, t * 128:(t + 1) * 128], func=AF.Exp, scale=1.0 / 16, bias=ncl[:, t:t + 1])
                    return last
                op("scalar", fed, r=[d["cln"], d["ncln"]], w=d["edn"])
            if main:
                for h in hs:
                    d = H[h]
                    (xb, xbn) = xpre_pool.get()
                    d["e2"], d["e2n"] = xb[:, 0:NTOK], [xbn, xbn + "h"]
                    op("scalar", (lambda cl, e2: lambda e: e.activation(out=e2, in_=cl[:], func=AF.Exp, scale=1.0 / 16))(d["cl"], d["e2"]), r=[d["cln"]], w=d["e2n"])
                for h in hs:
                    d = H[h]
                    pq, pqn = inproj_fm(C_Q + h * 128, 128)
                    op("vector", (lambda pq, e1, h: lambda e: e.scalar_tensor_tensor(out=qdec[:, h, :], in0=pq[:, 0:NTOK], scalar=128.0 ** -0.5, in1=e1[:],
                                                                                      op0=ALU.mult, op1=ALU.mult))(pq, d["e1"], h), r=[pqn, d["e1n"]], w=[qn])
            for h in hs:
                d = H[h]
                pk, pkn = inproj_fm(C_K + h * 128, 128)
                if main:
                    op("vector", (lambda pk, e2, h: lambda e: e.tensor_tensor(out=kinv[:, h, :], in0=pk[:, 0:NTOK], in1=e2, op=ALU.mult))(pk, d["e2"], h),
                       r=[pkn] + d["e2n"], w=["kinv"])
                op("vector", (lambda pk, ed, h: lambda e: e.tensor_tensor(out=kdT[:, h, :], in0=pk[:, 0:NTOK], in1=ed, op=ALU.mult))(pk, d["ed"], h),
                   r=[pkn] + d["edn"], w=["kdT"])

        nchunks = 12 if all_chunks else 10
        pend = []
        for c0 in range(0, nchunks, 2):
            pair = []
            for c in (c0, c0 + 1):
                pc, pcn = inproj_fm(C_XBC + c * 128, 128)
                (xpre, xpren) = xpre_pool.get()
                op("gpsimd", (lambda xpre, c: lambda e: e.tensor_copy(out=xpre[:, 0:3], in_=halo[:, c, :]))(xpre, c), r=["halo"], w=[xpren + "h"])
                op("scalar", (lambda xpre, pc: lambda e: e.activation(out=xpre[:, 3:3 + NTOK], in_=pc[:, 0:NTOK], func=AF.Copy))(xpre, pc), r=[pcn], w=[xpren])
                op("gpsimd", (lambda xpre, c: lambda e: e.tensor_copy(out=halo[:, c, :], in_=xpre[:, NTOK:NTOK + 3]))(xpre, c), r=[xpren, xpren + "h"], w=["halo"])
                (acc, accn) = gtmp.get()
                op("scalar", (lambda pc, acc, c: lambda e: e.activation(out=acc[:], in_=pc[:, 0:NTOK], func=AF.Identity,
                                                                        scale=cw[:, 3, c:c + 1], bias=cb[:, c:c + 1]))(pc, acc, c),
                   r=[pcn, "cw", "cb"], w=[accn])
                pair.append((c, xpre, xpren, acc, accn))
            for j in range(3):
                for (c, xpre, xpren, acc, accn) in pair:
                    op("vector", (lambda xpre, acc, c, j: lambda e: e.scalar_tensor_tensor(out=acc[:], in0=xpre[:, j:j + NTOK], scalar=cw[:, j, c:c + 1], in1=acc[:],
                                                                                            op0=ALU.mult, op1=ALU.add))(xpre, acc, c, j),
                       r=[xpren, xpren + "h", accn, "cw"], w=[accn])
            for f in pend:
                f()
            pend = []
            for (c, xpre, xpren, acc, accn) in pair:
                if c < 10:
                    pend.append((lambda acc, accn, c: lambda: op("scalar", lambda e: e.activation(out=xc[:, c, :], in_=acc[:], func=AF.Silu), r=[accn], w=["xc"]))(acc, accn, c))
                else:
                    pend.append((lambda acc, accn, c: lambda: op("scalar", lambda e: e.activation(out=xcC[:, c - 10, :], in_=acc[:], func=AF.Silu), r=[accn], w=[xcn]))(acc, accn, c))
        for f in pend:
            f()

    def tile_gen(tok0, t, main, par):
        qdec = qdec2[par]; qn = "qdec%d" % par
        decg = decg2[par]; dgn = "decg%d" % par
        xcC = xcC2[par]; xcn = "xcC%d" % par
        tsl = slice(t * 128, (t + 1) * 128)
        if main:
            (sg, sgn) = sg_pool.get()
            (sz, szn) = sz_pool.get()
            for (dst, dstn, c0) in ((sg, sgn, C_G), (sz, szn, C_Z)):
                for half in range(2):
                    pg, pgn = inproj_tm(t, c0 + half * 512, 512)
                    op("scalar", (lambda pg, dst, half: lambda e: e.activation(out=dst[:, half * 512:(half + 1) * 512], in_=pg[:, :], func=AF.Silu))(pg, dst, half),
                       r=[pgn], w=[dstn])
        pd, pdn = inproj_tm(t, C_DT, 16)
        (dtr, dtrn) = small.get()
        op("vector", (lambda pd, dtr: lambda e: e.tensor_tensor(out=dtr[:], in0=pd[:, 0:16], in1=dtb_bc[:], op=ALU.add))(pd, dtr), r=[pdn, "dtb_bc"], w=[dtrn])
        op("scalar", (lambda dtr: lambda e: e.activation(out=dtr[:], in_=dtr[:], func=AF.Exp))(dtr), r=[dtrn], w=[dtrn])
        (dt, dtn) = small.get()
        op("scalar", (lambda dtr, dt: lambda e: e.activation(out=dt[:], in_=dtr[:], func=AF.Ln, bias=1.0))(dtr, dt), r=[dtrn], w=[dtn])
        (dtA, dtAn) = small.get()
        op("gpsimd", (lambda dt, dtA: lambda e: e.tensor_tensor(out=dtA[:], in0=dt[:], in1=aneg_bc[:], op=ALU.mult))(dt, dtA), r=[dtn, "aneg_bc"], w=[dtAn])
        (V, Vn) = V_pool.get()
        for half in range(2):
            pv, pvn = inproj_tm(t, C_V + half * 512, 512)
            if half == 0:
                op("scalar", (lambda pv, V: lambda e: e.activation(out=V[:, 0:512], in_=pv[:, :], func=AF.Copy))(pv, V), r=[pvn], w=[Vn])
            else:
                op("vector", (lambda pv, V: lambda e: e.tensor_copy(out=V[:, 512:1024], in_=pv[:, :]))(pv, V), r=[pvn], w=[Vn])
        (pcm, pcmn) = psb()

        def fcm(e, pcm=pcm, dtA=dtA):
            e.matmul(out=pcm[:, 0:16], lhsT=L_f[:], rhs=dtA[:], start=True, stop=True)
            e.matmul(out=pcm[:, 16:32], lhsT=SU_f[:], rhs=dtA[:], start=True, stop=True)
            return e.matmul(out=pcm[:, 32:48], lhsT=ones_f[:], rhs=dtA[:], start=True, stop=True)
        op("tensor", fcm, r=["L_f", "SU_f", "ones_f", dtAn], w=[pcmn])
        (ex, exn) = ex_pool.get()
        op("scalar", (lambda pcm, ex: lambda e: e.activation(out=ex[:], in_=pcm[:, 0:48], func=AF.Exp))(pcm, ex), r=[pcmn], w=[exn])
        if main:
            (rhs2, rhs2n) = rhs2_pool.get()
            op("gpsimd", (lambda rhs2, dtA: lambda e: e.tensor_tensor(out=rhs2[:], in0=dtA[:].unsqueeze(2).broadcast_to([128, 16, 128]),
                                                                       in1=L_f[:].unsqueeze(1).broadcast_to([128, 16, 128]), op=ALU.mult))(rhs2, dtA),
               r=[dtAn, "L_f"], w=[rhs2n])
        (pxa, pxan) = psb()
        pxav = pxa.bitcast(BF16)

        def ftx(e, pxav=pxav):
            last = None
            for c in range(8):
                last = e.transpose(out=pxav[:, c * 128:(c + 1) * 128], in_=xc[:, c, tsl], identity=ident[:])
            return last
        op("tensor", ftx, r=["xc", "ident"], w=[pxan])
        (xdt, xdtn) = xdt_pool.get()
        op("vector", (lambda pxav, xdt, dt: lambda e: e.tensor_tensor(out=xdt[:].rearrange("p (h q) -> p h q", h=16),
                                                                        in0=pxav[:, :].rearrange("p (h q) -> p h q", h=16),
                                                                        in1=dt[:].unsqueeze(2).broadcast_to([128, 16, 64]), op=ALU.mult))(pxav, xdt, dt),
           r=[pxan, dtn], w=[xdtn])
        if main:
            (xsD, xsDn) = xsD_pool.get()
            op("vector", (lambda pxav, xsD: lambda e: e.tensor_tensor(out=xsD[:].rearrange("p (h q) -> p h q", h=16),
                                                                        in0=pxav[:, :].rearrange("p (h q) -> p h q", h=16),
                                                                        in1=dsk_bc[:].unsqueeze(2).broadcast_to([128, 16, 64]), op=ALU.mult))(pxav, xsD),
               r=[pxan, "dsk_bc"], w=[xsDn])
        (Btm, Btmn) = Btm_pool.get()
        transposes([xc[:, 8 + g, tsl] for g in range(2)], ["xc"], Btm[:, :], Btmn, "vector")
        (xdd, xddn) = xdd_pool.get()
        op("gpsimd", (lambda xdt, xdd, ex: lambda e: e.tensor_tensor(out=xdd[:].rearrange("p (h q) -> p h q", h=16),
                                                                      in0=xdt[:].rearrange("p (h q) -> p h q", h=16),
                                                                      in1=ex[:, 16:32].unsqueeze(2).broadcast_to([128, 16, 64]), op=ALU.mult))(xdt, xdd, ex),
           r=[xdtn, exn], w=[xddn])
        (kd, kdn) = kd_pool.get()
        transposes([kdT[:, h, tsl] for h in range(4)], ["kdT"], kd[:, :], kdn, "vector")
        MTs = []
        if main:
            (psc, pscn) = psb()

            def fsc(e, psc=psc):
                last = None
                for h in range(4):
                    last = e.matmul(out=psc[:, h * 128:(h + 1) * 128], lhsT=kinv[:, h, tsl], rhs=qdec[:, h, tsl], start=True, stop=True)
                return last
            op("tensor", fsc, r=["kinv", qn], w=[pscn])
            (scm, scmn) = scm_pool.get()
            op("vector", (lambda psc, scm: lambda e: e.tensor_tensor(out=scm[:], in0=psc[:, :].rearrange("p (h i) -> p h i", h=4),
                                                                       in1=L_f[:].unsqueeze(1).broadcast_to([128, 4, 128]), op=ALU.mult))(psc, scm),
               r=[pscn, "L_f"], w=[scmn])
            (pcb, pcbn) = psb()

            def fcb(e, pcb=pcb):
                e.matmul(out=pcb[:, 0:128], lhsT=xc[:, 8, tsl], rhs=xcC[:, 0, tsl], start=True, stop=True)
                return e.matmul(out=pcb[:, 128:256], lhsT=xc[:, 9, tsl], rhs=xcC[:, 1, tsl], start=True, stop=True)
            op("tensor", fcb, r=["xc", xcn], w=[pcbn])
            (cbm, cbmn) = cbm_pool.get()
            op("vector", (lambda pcb, cbm: lambda e: e.tensor_tensor(out=cbm[:], in0=pcb[:, 0:256].rearrange("p (g i) -> p g i", g=2),
                                                                       in1=L_f[:].unsqueeze(1).broadcast_to([128, 2, 128]), op=ALU.mult))(pcb, cbm),
               r=[pcbn, "L_f"], w=[cbmn])
            decs = [dec_pool.get() for g in range(2)]
            psgs = []
            for g in range(2):
                for q in range(2):
                    (psg, psgn) = psb()
                    hb = g * 8 + q * 4
                    op("tensor", (lambda psg, rhs2, hb: lambda e: e.matmul(out=psg[:, :], lhsT=SU_b[:], rhs=rhs2[:, hb:hb + 4, :], start=True, stop=True))(psg, rhs2, hb),
                       r=["SU_b", rhs2n], w=[psgn])
                    psgs.append((g, q, psg, psgn))
            for (g, q, psg, psgn) in psgs:
                (dec, decn) = decs[g]
                op("scalar", (lambda psg, dec, q: lambda e: e.activation(out=dec[:, q * 4:(q + 1) * 4, :], in_=psg[:, :].rearrange("p (h i) -> p h i", h=4),
                                                                          func=AF.Exp))(psg, dec, q), r=[psgn], w=[decn])
            for g in range(2):
                (dec, decn) = decs[g]
                (MT, MTn) = MT_pool.get()
                op("vector", (lambda MT, dec, cbm, g: lambda e: e.tensor_tensor(out=MT[:], in0=dec[:], in1=cbm[:, g, :].unsqueeze(1).broadcast_to([128, 8, 128]),
                                                                                 op=ALU.mult))(MT, dec, cbm, g), r=[decn, cbmn], w=[MTn])
                MTs.append((MT, MTn))
        yield
        pos = []
        if main:
            for hp in range(2):
                (po, pon) = psb(long=True)

                def fo(e, po=po, hp=hp):
                    last = None
                    for hh in range(2):
                        h = hp * 2 + hh
                        e.matmul(out=po[:, hh * 256:(hh + 1) * 256], lhsT=scm[:, h, :], rhs=V[:, h * 256:(h + 1) * 256], start=True, stop=False)
                        last = e.matmul(out=po[:, hh * 256:(hh + 1) * 256], lhsT=qdec[:, h, tsl], rhs=Sgb[:, h, :], start=False, stop=True)
                    return last
                op("tensor", fo, r=[scmn, Vn, qn, "Sgb"], w=[pon])
                pos.append((po, pon))
        for hp in range(2):
            (pds, pdsn) = psb()

            def fds(e, pds=pds, hp=hp):
                last = None
                for hh in range(2):
                    h = hp * 2 + hh
                    last = e.matmul(out=pds[:, hh * 256:(hh + 1) * 256], lhsT=kd[:, h * 128:(h + 1) * 128], rhs=V[:, h * 256:(h + 1) * 256], start=True, stop=True)
                return last
            op("tensor", fds, r=[kdn, Vn], w=[pdsn])
            for hh in range(2):
                h = hp * 2 + hh
                op("vector", (lambda pds, h, hh: lambda e: e.scalar_tensor_tensor(out=Sg[:, h, :], in0=Sg[:, h, :], scalar=decg[:, h, t:t + 1],
                                                                                   in1=pds[:, hh * 256:(hh + 1) * 256], op0=ALU.mult, op1=ALU.add))(pds, h, hh),
                   r=["Sg", dgn, pdsn], w=["Sg"])
        op("scalar", lambda e: e.activation(out=Sgb[:], in_=Sg[:], func=AF.Copy), r=["Sg"], w=["Sgb"])
        if main:
            (mixed, mixedn) = mixed_pool.get()
            (ssqg, ssqgn) = small.get()
            for h in range(4):
                po, pon = pos[h // 2]
                hh = h % 2
                op("scalar", (lambda po, hh, h: lambda e: e.activation(out=mixed[:, h * 256:(h + 1) * 256], in_=po[:, hh * 256:(hh + 1) * 256], func=AF.Square,
                                                                        accum_out=ssqg[:, h:h + 1]))(po, hh, h), r=[pon], w=[ssqgn, mixedn + "g"])
            rg, rgn = rstd_from(ssqg[:, 0:4], ssqgn, 4, 1.0 / 256)
            for h in range(4):
                po, pon = pos[h // 2]
                hh = h % 2
                op("vector", (lambda po, hh, h: lambda e: e.scalar_tensor_tensor(
                    out=mixed[:, h * 256:(h + 1) * 256], in0=po[:, hh * 256:(hh + 1) * 256], scalar=rg[:, h:h + 1],
                    in1=sg[:, h * 256:(h + 1) * 256], op0=ALU.mult, op1=ALU.mult))(po, hh, h),
                   r=[pon, rgn, sgn], w=[mixedn + "g"])
        yield
        yzs = []
        saved_banks = None
        if CUR[0] is not None and CUR[0].banks == (4, 5):
            saved_banks = CUR[0]
            saved_banks.banks = (4, 5, 6, 7)
        if main:
            yl = []
            for g in range(2):
                (MT, MTn) = MTs[g]
                (py, pyn) = psb()

                def fy(e, py=py, MT=MT, g=g):
                    last = None
                    for hh in range(8):
                        h = g * 8 + hh
                        last = e.matmul(out=py[:, hh * 64:(hh + 1) * 64], lhsT=MT[:, hh, :], rhs=xdt[:, h * 64:(h + 1) * 64], start=True, stop=True)
                    return last
                op("tensor", fy, r=[MTn, xdtn], w=[pyn])
                (pyi, pyin) = psb()
                op("tensor", (lambda pyi, g: lambda e: e.matmul(out=pyi[:, :], lhsT=xcC[:, g, tsl], rhs=Ssb[:, g, :], start=True, stop=True))(pyi, g),
                   r=[xcn, "Ssb"], w=[pyin])
                (t1, t1n) = yt_pool.get()
                yl.append((py, pyn, pyi, pyin, t1, t1n))
            for g in range(2):
                (py, pyn, pyi, pyin, t1, t1n) = yl[g]
                op("vector", (lambda pyi, t1, g: lambda e: e.tensor_tensor(out=t1[:].rearrange("p (h q) -> p h q", h=8),
                                                                            in0=pyi[:, :].rearrange("p (h q) -> p h q", h=8),
                                                                            in1=ex[:, g * 8:(g + 1) * 8].unsqueeze(2).broadcast_to([128, 8, 64]), op=ALU.mult))(pyi, t1, g),
                   r=[pyin, exn], w=[t1n])
            for g in range(2):
                (py, pyn, pyi, pyin, t1, t1n) = yl[g]
                op("vector", (lambda py, t1: lambda e: e.tensor_tensor(out=t1[:], in0=py[:, :], in1=t1[:], op=ALU.add))(py, t1), r=[pyn, t1n], w=[t1n])
                yzs.append((t1, t1n))
        psl = []
        for g in range(2):
            (pss, pssn) = psb()
            op("tensor", (lambda pss, g: lambda e: e.matmul(out=pss[:, :], lhsT=Btm[:, g * 128:(g + 1) * 128], rhs=xdd[:, g * 512:(g + 1) * 512],
                                                             start=True, stop=True))(pss, g), r=[Btmn, xddn], w=[pssn])
            psl.append((pss, pssn))
        if saved_banks is not None:
            saved_banks.banks = (4, 5)
        for g in range(2):
            op("vector", (lambda g: lambda e: e.tensor_tensor(out=Ss[:, g, :].rearrange("p (h q) -> p h q", h=8),
                                                              in0=Ss[:, g, :].rearrange("p (h q) -> p h q", h=8),
                                                              in1=ex[:, 32 + g * 8:40 + g * 8].unsqueeze(2).broadcast_to([128, 8, 64]), op=ALU.mult))(g),
               r=["Ss%d" % g, exn, "Ssb"], w=["Ss%d" % g])
        for g in range(2):
            (pss, pssn) = psl[g]
            op("vector", (lambda pss, g: lambda e: e.tensor_tensor(out=Ss[:, g, :], in0=Ss[:, g, :], in1=pss[:, :], op=ALU.add))(pss, g),
               r=["Ss%d" % g, pssn], w=["Ss%d" % g])
        op("scalar", lambda e: e.activation(out=Ssb[:], in_=Ss[:], func=AF.Copy), r=["Ss0", "Ss1"], w=["Ssb"])
        if main:
            (ssqs, ssqsn) = small.get()
            for g in range(2):
                (t1, t1n) = yzs[g]
                op("vector", (lambda t1, g: lambda e: e.tensor_tensor(out=t1[:], in0=t1[:], in1=xsD[:, g * 512:(g + 1) * 512], op=ALU.add))(t1, g),
                   r=[t1n, xsDn], w=[t1n])
            for g in range(2):
                (t1, t1n) = yzs[g]
                op("vector", (lambda t1, g: lambda e: e.tensor_tensor(out=t1[:], in0=t1[:], in1=sz[:, g * 512:(g + 1) * 512], op=ALU.mult))(t1, g),
                   r=[t1n, szn], w=[t1n])
            for g in range(2):
                (t1, t1n) = yzs[g]
                op("scalar", (lambda t1, g: lambda e: e.activation(out=mixed[:, 1024 + g * 512:1024 + (g + 1) * 512], in_=t1[:], func=AF.Square, accum_out=ssqs[:, g:g + 1]))(t1, g),
                   r=[t1n], w=[ssqsn, mixedn + "s%d" % g])
            rss, rssn = rstd_from(ssqs[:, 0:2], ssqsn, 2, 1.0 / 512)
            for g in range(2):
                t1, t1n = yzs[g]
                op("scalar", (lambda t1, g: lambda e: e.activation(out=mixed[:, 1024 + g * 512:1024 + (g + 1) * 512], in_=t1[:], func=AF.Copy,
                                                                    scale=rss[:, g:g + 1]))(t1, g), r=[t1n, rssn], w=[mixedn + "s%d" % g])
            r0 = tok0 + t * 128
            op("sync", (lambda r0: lambda e: e.dma_start(out=scr_d[r0:r0 + 128, :], in_=mixed[:]))(r0),
               r=[mixedn + "g", mixedn + "s0", mixedn + "s1"], w=["scr%d" % (r0 // 128)], dma="m_" + mixedn.split("#")[0])
        yield

    pendq = []

    def partner(nhalf):
        todo = []
        for a in pendq:
            while a[1] > 0 and len(todo) < nhalf:
                todo.append(a[0])
                a[1] -= 1
        pendq[:] = [a for a in pendq if a[1] > 0]
        if not todo:
            return None

        def f():
            for g in todo:
                next(g)
        return f

    def slot(main_fn, nhalf, extra=None, wmain=1):
        p = partner(nhalf)
        fns, bks, ws = [], [], []
        if p is not None:
            fns.append(p); bks.append((4, 5)); ws.append(1)
        fns.append(main_fn); bks.append((0, 1, 2, 3) if p is not None else (0, 1, 2, 3, 4, 5)); ws.append(wmain if p is not None else (2 if extra is not None else 1))
        if extra is not None:
            fns.append(extra); bks.append(None); ws.append(1)
        interleave(fns, bks, "tile", ws)

    def run_phase(src_d, ntok, main):
        nsup = ntok // NTOK
        if nsup == 0:
            return
        st = {"xns": phaseA_pre(x_loads(src_d, 0)), "xts": x_loads(src_d, NTOK) if nsup > 1 else None}
        for s in range(nsup):
            par = s % 2
            xns = st["xns"]
            extra0 = None
            if late_pieces:
                extra0 = (lambda ps_: lambda: load_pieces(ps_))(late_pieces.pop())
            slot(lambda: phaseAB(xns, main, par, main or s == nsup - 1), 1, extra0, wmain=6)
            for t in range(NT):
                g = tile_gen(s * NTOK, t, main, par)
                extra = None
                if t == NT - 1 and s + 1 < nsup:
                    def extra(s=s):
                        st["xns"] = phaseA_pre(st["xts"])
                        st["xts"] = x_loads(src_d, (s + 2) * NTOK) if s + 2 < nsup else None
                slot((lambda g: lambda: next(g))(g), 1 if t < NT - 1 else 99, extra, wmain=(2 if t < NT - 1 else 1))
                pendq.append([g, 2])

    def mask_gen():
        op("gpsimd", lambda e: e.tensor_scalar(out=Sg[:], in0=Sg[:], scalar1=flag[:, 0:1], scalar2=0.0, op0=ALU.mult, op1=ALU.add), r=["Sg", "flag"], w=["Sg"])
        op("gpsimd", lambda e: e.tensor_scalar(out=Ss[:], in0=Ss[:], scalar1=flag[:, 0:1], scalar2=0.0, op0=ALU.mult, op1=ALU.add),
           r=["Ss0", "Ss1", "flag"], w=["Ss0", "Ss1"])
        op("scalar", lambda e: e.activation(out=Sgb[:], in_=Sg[:], func=AF.Copy), r=["Sg"], w=["Sgb"])
        op("scalar", lambda e: e.activation(out=Ssb[:], in_=Ss[:], func=AF.Copy), r=["Ss0", "Ss1"], w=["Ssb"])
        yield

    if STOP != "setup":
        run_phase(xp_d, TP, False)
    if TP > 0 and STOP not in ("setup", "pre"):
        pendq.append([mask_gen(), 1])
    if STOP not in ("setup", "pre"):
        run_phase(xm_d, TM, True)
    p_last = partner(99)
    if p_last is not None:
        p_last()

    if TM > 0 and STOP in (None, "p2w", "p2s0", "p2s1"):
        bar = sb("bar", [128, 1])
        op("gpsimd", lambda e: e.memset(bar[:], 0.0), w=ALLW + ["bar"])
        BAR = ["bar"]

        def f32view(a, b_):
            return big[:, a:b_].bitcast(F32)
        o_ = 26624
        mixin = [(big[:, o_:o_ + 2048], "mixin0"), (big[:, o_ + 2048:o_ + 4096], "mixin1")]; o_ += 4096
        mixTs = [(big[:, o_:o_ + 2048].rearrange("p (k t) -> p k t", k=16), "mixT0"),
                 (big[:, o_ + 2048:o_ + 4096].rearrange("p (k t) -> p k t", k=16), "mixT1")]; o_ += 4096
        hns = [(big[:, o_:o_ + 1024], "hn0"), (big[:, o_ + 1024:o_ + 2048], "hn1")]; o_ += 2048
        hnTs = [(big[:, o_:o_ + 1024].rearrange("p (k t) -> p k t", k=8), "hnT0"),
                (big[:, o_ + 1024:o_ + 2048].rearrange("p (k t) -> p k t", k=8), "hnT1")]; o_ += 2048
        hbufs = [(f32view(o_, o_ + 2048), "h0"), (f32view(o_ + 2048, o_ + 4096), "h1")]; o_ += 4096
        assert o_ <= 8 * DIN
        tgs = [(xsD_pool.bufs[0][0][:, :], "xsD0"), (xsD_pool.bufs[0][0][:, :], "xsD0")]
        obs = [(xt_pool.bufs[0][0][:, :], "xt0"), (xt_pool.bufs[1][0][:, :], "xt1")]
        xrs = [(mixed_pool.bufs[0][0][:, :].bitcast(F32), "mixed0"), (xsD_pool.bufs[1][0][:, :], "xsD1")]
        pts = [(yt_pool.bufs[0][0][:, 0:256], "ytmp0"), (yt_pool.bufs[1][0][:, 0:256], "ytmp1")]
        pbfs = [(Btm_pool.bufs[0][0][:, :], "Btm0"), (Btm_pool.bufs[1][0][:, :], "Btm1")]
        pTs = [(kd_pool.bufs[0][0][:, 0:256].rearrange("p (k t) -> p k t", k=2), "kd0"),
               (kd_pool.bufs[1][0][:, 0:256].rearrange("p (k t) -> p k t", k=2), "kd1"),
               (scm_pool.bufs[0][0][:, 0:2, :], "scm0")]
        fw_bc = sb("fw_bc", [128, D])
        op("sync", lambda e: e.dma_start(out=fw_bc[:], in_=fin_w_d.partition_broadcast(128)), w=["fw_bc"], dma="c_fw")
        def load_wout():
            for fc in range(16):
                rows = w_out_d[fc * 128:(fc + 1) * 128, :]
                if fc < 8:
                    sc, scn = gnw[:, (fc % 2):(fc % 2) + 1], "gnw"
                else:
                    sc, scn = snw[:, fc - 8:fc - 7], "snw"
                for c0 in (0, 512):
                    load_w(Wout, fc, c0, c0 + 512, rows, sc, scn, "Wout", BAR)

        def load_wg_wpe():
            for kc in range(8):
                for c0 in (0, 512):
                    load_w(Wg, kc, c0, c0 + 512, w_g_d[kc * 128:(kc + 1) * 128, :], pnw[:, kc:kc + 1], "pnw", "Wg", BAR)
            for c in range(2):
                for c0 in (0, 512):
                    load_w(Wpe, c, c0, c0 + 512, w_pe_d[c * 128:(c + 1) * 128, :], None, None, "Wpe", BAR)
        p2_loaders = [load_wg_wpe, load_wout]

        def p2_gen(ti):
            r0 = ti * 128
            (mi, min_) = mixin[ti % 2]
            (mixT, mixTn) = mixTs[ti % 2]
            (xr, xrn) = xrs[ti % 2]
            (pt, ptn) = pts[ti % 2]
            (pbf, pbfn) = pbfs[ti % 2]
            (pT, pTn) = pTs[ti % 3]
            (hb_, hn_) = hbufs[ti % 2]
            (hn, hnn) = hns[ti % 2]
            (hnT, hnTn) = hnTs[ti % 2]
            (tg, tgn0) = tgs[ti % 2]
            (ob, obn) = obs[ti % 2]
            op("sync", lambda e: e.dma_start(out=mi, in_=scr_d[r0:r0 + 128, :]), r=["scr%d" % ti] + BAR, w=[min_], dma="mi_" + min_)
            xr_w = [xrn] + (["mixed0g", "mixed0s0", "mixed0s1"] if xrn == "mixed0" else [])
            op("sync", lambda e: e.dma_start(out=xr, in_=xm_d[r0:r0 + 128, :]), w=xr_w, dma="x2_" + xrn)
            op("sync", lambda e: e.dma_start(out=pt, in_=pm_d[r0:r0 + 128, :]), w=[ptn], dma="p_" + ptn)
            transposes([mi[:, fc * 128:(fc + 1) * 128] for fc in range(8)], [min_], mixT[:, 0:8, :], mixTn + "a", "scalar")
            transposes([mi[:, fc * 128:(fc + 1) * 128] for fc in range(8, 16)], [min_], mixT[:, 8:16, :], mixTn + "b", "vector")
            op("gpsimd", lambda e: e.tensor_copy(out=pbf, in_=pt), r=[ptn], w=[pbfn])
            transposes([pbf[:, c * 128:(c + 1) * 128] for c in range(2)], [pbfn], pT, pTn, "vector")
            yield
            if STOP == "p2s0":
                return
            for half in range(2):
                (ph, phn) = psb()

                def fh(e, ph=ph, half=half):
                    last = None
                    for fc in range(16):
                        last = e.matmul(out=ph[:, :], lhsT=mixT[:, fc, :], rhs=Wout[:, fc, half * 512:(half + 1) * 512], start=(fc == 0), stop=(fc == 15))
                    return last
                op("tensor", fh, r=[mixTn + "a", mixTn + "b", "Wout"], w=[phn])
                op("vector", (lambda ph, half: lambda e: e.tensor_tensor(out=hb_[:, half * 512:(half + 1) * 512], in0=ph[:, :],
                                                                          in1=xr[:, half * 512:(half + 1) * 512], op=ALU.add))(ph, half),
                   r=[phn, xrn] + BAR, w=[hn_ + "_%d" % half])
            (ssqh, ssqhn) = small.get()
            op("scalar", lambda e: e.activation(out=hn, in_=hb_, func=AF.Square, accum_out=ssqh[:, 0:1]),
               r=[hn_ + "_0", hn_ + "_1"] + BAR, w=[ssqhn, hnn])
            rh, rhn = rstd_from(ssqh[:, 0:1], ssqhn, 1, 1.0 / D)
            op("scalar", lambda e: e.activation(out=hn, in_=hb_, func=AF.Copy, scale=rh[:, 0:1]),
               r=[hn_ + "_0", hn_ + "_1", rhn] + BAR, w=[hnn])
            transposes([hn[:, k * 128:(k + 1) * 128] for k in range(8)], [hnn], hnT, hnTn, "vector")
            yield
            if STOP == "p2s1":
                return
            (ssqf, ssqfn) = small.get()
            for half in range(2):
                (pgt, pgtn) = psb()

                def fg(e, pgt=pgt, half=half):
                    last = None
                    for kc in range(8):
                        last = e.matmul(out=pgt[:, :], lhsT=hnT[:, kc, :], rhs=Wg[:, kc, half * 512:(half + 1) * 512], start=(kc == 0), stop=(kc == 7))
                    return last
                op("tensor", fg, r=[hnTn, "Wg"], w=[pgtn])
                (ppe, ppen) = psb()

                def fpe(e, ppe=ppe, half=half):
                    last = None
                    for c in range(2):
                        last = e.matmul(out=ppe[:, :], lhsT=pT[:, c, :], rhs=Wpe[:, c, half * 512:(half + 1) * 512], start=(c == 0), stop=(c == 1))
                    return last
                op("tensor", fpe, r=[pTn, "Wpe"], w=[ppen])
                hs = slice(half * 512, (half + 1) * 512)
                tgn = tgn0 + "_%d" % half
                op("scalar", (lambda pgt, hs: lambda e: e.activation(out=tg[:, hs], in_=pgt[:, :], func=AF.Tanh, scale=0.5))(pgt, hs), r=[pgtn], w=[tgn0, tgn])
                op("vector", (lambda ppe, hs: lambda e: e.scalar_tensor_tensor(out=tg[:, hs], in0=tg[:, hs], scalar=1.0, in1=ppe[:, :], op0=ALU.add, op1=ALU.mult))(ppe, hs),
                   r=[ppen, tgn], w=[tgn])
                op("vector", (lambda hs, half: lambda e: e.scalar_tensor_tensor(out=hb_[:, hs], in0=tg[:, hs], scalar=0.5, in1=hb_[:, hs], op0=ALU.mult, op1=ALU.add))(hs, half),
                   r=[tgn, hn_ + "_%d" % half], w=[hn_ + "_%d" % half])
            op("scalar", lambda e: e.activation(out=ob, in_=hb_, func=AF.Square, accum_out=ssqf[:, 0:1]),
               r=[hn_ + "_0", hn_ + "_1"], w=[ssqfn, obn])
            rf, rfn = rstd_from(ssqf[:, 0:1], ssqfn, 1, 1.0 / D)
            op("vector", lambda e: e.scalar_tensor_tensor(out=ob, in0=hb_, scalar=rf[:, 0:1], in1=fw_bc[:], op0=ALU.mult, op1=ALU.mult),
               r=[hn_ + "_0", hn_ + "_1", rfn, "fw_bc"], w=[obn])
            op("sync", lambda e: e.dma_start(out=out_d[r0:r0 + 128, :], in_=ob), r=[obn], dma="o_" + obn)
            yield

        active = []
        ntile = TM // 128 if STOP != "p2w" else 0
        nxt = 0
        stage_banks = {0: (0, 1), 1: (2, 3, 4), 2: (5, 6, 7)}
        while nxt < ntile or active:
            fns = []
            bks = []
            for a in active:
                fns.append((lambda g: lambda: next(g, None))(a[0]))
                bks.append(stage_banks[a[1]])
            if nxt < ntile:
                a = [p2_gen(nxt), 0]
                nxt += 1
                active.append(a)
                fns.append((lambda g: lambda: next(g, None))(a[0]))
                bks.append(stage_banks[0])
            if p2_loaders:
                fns.append(p2_loaders.pop())
                bks.append(None)
            interleave(fns, bks, "p2")
            for a in active:
                a[1] += 1
            active = [a for a in active if a[1] < 3]

    S.emit(final_waits=[k for k in S.dma_streams if k.startswith("o_")])
    return nc


WKEYS = ["norm_w", "w_in", "gla_gate_up", "gla_gate_b", "gla_norm_w", "conv_w", "conv_b", "dt_bias", "a_log",
         "d_skip", "ssd_norm_w", "w_out", "w_pe", "w_pe_gate", "pe_norm_w"]


def run_layer(x, p, w, final_norm_w):
    B, T, _ = x.shape
    half = T // 2
    nc = build_program(half, half)
    wmap = {k: np.ascontiguousarray(np.asarray(w[k], dtype=np.float32)) for k in WKEYS}
    wmap["final_norm_w"] = np.ascontiguousarray(np.asarray(final_norm_w, dtype=np.float32))
    in_maps = []
    for b in range(B):
        for s in range(2):
            m = dict(wmap)
            m["xm"] = np.ascontiguousarray(x[b, s * half:(s + 1) * half])
            m["xp"] = np.ascontiguousarray(x[b, 0:half]) if s == 1 else np.zeros((half, D), np.float32)
            m["pm"] = np.ascontiguousarray(p[b, s * half:(s + 1) * half])
            m["flag"] = np.full((128, 1), float(s), np.float32)
            in_maps.append(m)
    ncores = 2 * B
    res = run_bass_kernel_spmd(nc, in_maps, core_ids=list(range(ncores)))
    out = np.empty((B, T, D), np.float32)
    for b in range(B):
        for s in range(2):
            out[b, s * half:(s + 1) * half] = res.results[2 * b + s]["out"]
    return out


def kernel(**inputs):
    x = np.asarray(inputs["x"], dtype=np.float32)
    p = np.asarray(inputs["p"], dtype=np.float32)
    w = {k: np.asarray(inputs[k])[0] for k in WKEYS}
    return run_layer(x, p[0], w, inputs["final_norm_w"])
```

```python
import threading
import numpy as np
import concourse.bass as bass
import concourse.mybir as mybir
from concourse.bass_utils import run_bass_kernel_spmd

F32 = mybir.dt.float32
BF16 = mybir.dt.bfloat16
AF = mybir.ActivationFunctionType
ALU = mybir.AluOpType

ENGS = ["tensor", "vector", "scalar", "gpsimd", "sync"]

D = 1024
DIN = 5664
C_Q, C_K, C_V, C_G, C_GLR, C_Z, C_XBC, C_DT = 0, 512, 1024, 2048, 3072, 3088, 4112, 5648
EPS = 1e-6
STOP = None
NT = 2
NTOK = NT * 128


class Sched:
    def __init__(self, nc):
        self.nc = nc
        self.ops = []
        self.last_writer = {}
        self.readers = {}
        self.eng_count = {e: 0 for e in ENGS}
        self.dma_streams = {}

    def _chk(self, names, is_write=False):
        out = []
        for b in names:
            if "#" in b:
                base, rest = b.split("#", 1)
                i = 0
                while i < len(rest) and rest[i].isdigit():
                    i += 1
                gen, suf = int(rest[:i]), rest[i:]
                assert GEN.get(base, 0) == gen, "stale buffer use: %s (current gen %d)" % (b, GEN.get(base, 0))
                out.append(base + suf)
            else:
                out.append(b)
        return out

    def op(self, eng, fn, r=(), w=(), dma=None, ndma=1):
        r = self._chk(r)
        w = self._chk(w)
        idx = len(self.ops)
        deps = set()
        for b in r:
            if b in self.last_writer:
                deps.add(self.last_writer[b])
        for b in w:
            if b in self.last_writer:
                deps.add(self.last_writer[b])
            for q in self.readers.get(b, ()):
                deps.add(q)
        deps.discard(idx)
        o = dict(idx=idx, eng=eng, fn=fn, deps=sorted(deps), dma=dma, ndma=ndma)
        if dma is None:
            self.eng_count[eng] += 1
            o["seq"] = self.eng_count[eng]
        else:
            c = self.dma_streams.get(dma, 0) + ndma
            self.dma_streams[dma] = c
            o["seq"] = c * 16
        self.ops.append(o)
        for b in r:
            self.readers.setdefault(b, []).append(idx)
        for b in w:
            self.last_writer[b] = idx
            self.readers[b] = []
        if CUR[0] is not None:
            CUR[0].pause()
        return idx

    def emit(self, final_waits=()):
        nc = self.nc
        sems = {e: nc.alloc_semaphore("sem_" + e) for e in ENGS}
        dsems = {s: nc.alloc_semaphore("dsem_" + s) for s in self.dma_streams}
        ops = self.ops
        per_eng = {e: [o for o in ops if o["eng"] == e] for e in ENGS}

        def sem_of(o):
            return dsems[o["dma"]] if o["dma"] is not None else sems[o["eng"]]

        def key_of(o):
            return ("d", o["dma"]) if o["dma"] is not None else ("e", o["eng"])

        def run(eng_name, eng):
            waited = {}
            if eng_name != "gpsimd" and per_eng["gpsimd"]:
                eng.wait_ge(sems["gpsimd"], 1)
                waited[("e", "gpsimd")] = 1
            for o in per_eng[eng_name]:
                need = {}
                for d in o["deps"]:
                    p = ops[d]
                    k = key_of(p)
                    if eng_name == "tensor" and k == ("e", "tensor"):
                        continue
                    if need.get(k, (0, None))[0] < p["seq"]:
                        need[k] = (p["seq"], sem_of(p))
                for k, (v, s) in need.items():
                    if waited.get(k, 0) >= v:
                        continue
                    eng.wait_ge(s, v)
                    waited[k] = v
                res = o["fn"](eng)
                if o["dma"] is not None:
                    rs = res if isinstance(res, (list, tuple)) else [res]
                    assert len(rs) == o["ndma"], (len(rs), o["ndma"])
                    for ins in rs:
                        ins.then_inc(dsems[o["dma"]], 16)
                else:
                    res.then_inc(sems[eng_name], 1)
            if eng_name == "sync":
                for st in final_waits:
                    eng.wait_ge(dsems[st], self.dma_streams[st] * 16)

        with nc.Block() as block:
            @block.tensor
            def _(e):
                run("tensor", e)

            @block.vector
            def _(e):
                run("vector", e)

            @block.scalar
            def _(e):
                run("scalar", e)

            @block.gpsimd
            def _(e):
                run("gpsimd", e)

            @block.sync
            def _(e):
                run("sync", e)


GEN = {}
CUR = [None]


class Stream:
    def __init__(self, fn, banks=None):
        self.fn = fn
        self.banks = banks
        self.go = threading.Semaphore(0)
        self.done = threading.Semaphore(0)
        self.finished = False
        self.exc = None
        self.th = threading.Thread(target=self._run, daemon=True)
        self.started = False

    def _run(self):
        self.go.acquire()
        try:
            self.fn()
        except BaseException as e:
            self.exc = e
        self.finished = True
        self.done.release()

    def step(self):
        if not self.started:
            self.started = True
            self.th.start()
        CUR[0] = self
        self.go.release()
        self.done.acquire()
        CUR[0] = None
        if self.exc is not None:
            raise self.exc

    def pause(self):
        self.done.release()
        self.go.acquire()


ILV = {"AB": True, "tile": True, "p2": True}


def interleave(fns, banks=None, kind="tile", weights=None):
    if not ILV[kind]:
        for f in fns:
            f()
        return
    sts = [Stream(f, banks[i] if banks else None) for i, f in enumerate(fns)]
    while any(not st.finished for st in sts):
        for i, st in enumerate(sts):
            for _ in range(weights[i] if weights else 1):
                if not st.finished:
                    st.step()


class Rot:
    def __init__(self, nc, name, n, shape, dt):
        self.bufs = [(nc.alloc_sbuf_tensor("%s%d" % (name, i), shape, dt), "%s%d" % (name, i)) for i in range(n)]
        self.i = 0

    def get(self):
        (t, n) = self.bufs[self.i % len(self.bufs)]
        self.i += 1
        GEN[n] = GEN.get(n, 0) + 1
        return (t, "%s#%d" % (n, GEN[n]))


def build_program(TP, TM):
    GEN.clear()
    nc = bass.Bass("TRN2", target_bir_lowering=False)
    S = Sched(nc)
    op = S.op

    def din(name, shape):
        return nc.dram_tensor(name, shape, F32, kind="ExternalInput").ap()

    xp_d = din("xp", [TP, D])
    xm_d = din("xm", [TM, D])
    pm_d = din("pm", [TM, 256])
    flag_d = din("flag", [128, 1])
    norm_w_d = din("norm_w", [D])
    w_in_d = din("w_in", [D, DIN])
    up_d = din("gla_gate_up", [16, 512])
    gate_b_d = din("gla_gate_b", [512])
    gla_nw_d = din("gla_norm_w", [256])
    conv_w_d = din("conv_w", [4, 1536])
    conv_b_d = din("conv_b", [1536])
    dt_bias_d = din("dt_bias", [16])
    a_log_d = din("a_log", [16])
    d_skip_d = din("d_skip", [16])
    ssd_nw_d = din("ssd_norm_w", [1024])
    w_out_d = din("w_out", [2048, D])
    w_pe_d = din("w_pe", [256, D])
    w_g_d = din("w_pe_gate", [D, D])
    pe_nw_d = din("pe_norm_w", [D])
    fin_w_d = din("final_norm_w", [D])
    out_d = nc.dram_tensor("out", [TM, D], F32, kind="ExternalOutput").ap()

    def sb(name, shape, dt=F32):
        return nc.alloc_sbuf_tensor(name, shape, dt)

    big = sb("big", [128, 8 * DIN], BF16)
    op("gpsimd", lambda e: e.memset(big[:, 0:8192], 0.0), w=["startgate"])

    ps = nc.alloc_psum_tensor("ps", [128, 8 * 512], F32)
    ps_i = [0]

    ps_l = [0]
    ps_ctr = {}

    def psb(long=False):
        cur = CUR[0]
        if long and not (cur is not None and cur.banks is not None and cur.banks[-1] >= 6):
            k = 6 + ps_l[0] % 2
            ps_l[0] += 1
        else:
            bs = tuple(cur.banks) if (cur is not None and cur.banks is not None) else (0, 1, 2, 3, 4, 5)
            c = ps_ctr.get(bs, 0)
            ps_ctr[bs] = c + 1
            k = bs[c % len(bs)]
        n = "ps%d" % k
        GEN[n] = GEN.get(n, 0) + 1
        return ps[:, k * 512:(k + 1) * 512], "%s#%d" % (n, GEN[n])

    ones_f = sb("ones_f", [128, 128])
    identf = sb("identf", [128, 128])
    ident = sb("ident", [128, 128], BF16)
    L_f = sb("L_f", [128, 128])
    SU_f = sb("SU_f", [128, 128])
    SU_b = sb("SU_b", [128, 128], BF16)
    nhalf = sb("nhalf", [128, 4])
    epsc = sb("epsc", [128, 1])
    rmask = sb("rmask", [128, NTOK])
    NV = 90
    vrows = sb("vrows", [NV, 128])
    vecs = sb("vecs", [128, NV])
    nw = vecs[:, 0:8]
    pnw = vecs[:, 8:16]
    snw = vecs[:, 16:24]
    gnw = vecs[:, 24:26]
    nb = vecs[:, 26:30]
    cb = vecs[:, 30:42]
    cw = vecs[:, 42:90].rearrange("p (j c) -> p j c", j=4)
    dtb_bc = sb("dtb_bc", [128, 16])
    aneg_bc = sb("aneg_bc", [128, 16])
    dsk_bc = sb("dsk_bc", [128, 16])
    flag = sb("flag_sb", [128, 1])
    up_b = sb("up_b", [16, 512], BF16)

    op("gpsimd", lambda e: e.memset(ones_f[:], 1.0), w=["ones_f"])
    op("gpsimd", lambda e: e.memset(nhalf[:], -0.5), w=["nhalf"])
    op("gpsimd", lambda e: e.memset(epsc[:], EPS), w=["epsc"])
    op("gpsimd", lambda e: e.affine_select(out=identf[:], in_=ones_f[:], pattern=[[-1, 128]], compare_op=ALU.is_equal,
                                           fill=0.0, base=0, channel_multiplier=1), r=["ones_f"], w=["identf"])
    op("gpsimd", lambda e: e.affine_select(out=L_f[:], in_=ones_f[:], pattern=[[1, 128]], compare_op=ALU.is_ge,
                                           fill=0.0, base=0, channel_multiplier=-1), r=["ones_f"], w=["L_f"])
    op("gpsimd", lambda e: e.affine_select(out=SU_f[:], in_=ones_f[:], pattern=[[-1, 128]], compare_op=ALU.is_gt,
                                           fill=0.0, base=0, channel_multiplier=1), r=["ones_f"], w=["SU_f"])
    op("vector", lambda e: e.tensor_copy(out=ident[:], in_=identf[:]), r=["identf"], w=["ident"])
    op("vector", lambda e: e.tensor_copy(out=SU_b[:], in_=SU_f[:]), r=["SU_f"], w=["SU_b"])
    op("gpsimd", lambda e: e.memset(rmask[:], 1.0), w=["rmask"])
    op("gpsimd", lambda e: e.memset(rmask[:].rearrange("p (t i) -> p t i", i=128)[:, :, 0], 0.0), w=["rmask"])

    def small_dma(dst, src, name, n=1):
        op("sync", lambda e: e.dma_start(out=dst, in_=src), w=[name], dma="c_" + name)

    def rows_dma(r0, src, nrow, tag):
        op("sync", lambda e: e.dma_start(out=vrows[r0:r0 + nrow, :], in_=src.rearrange("(k p) -> k p", p=128)), w=["vrows_" + tag], dma="c_" + tag)
    rows_dma(0, norm_w_d, 8, "nw")
    rows_dma(8, pe_nw_d, 8, "pnw")
    rows_dma(16, ssd_nw_d, 8, "snw")
    rows_dma(24, gla_nw_d, 2, "gnw")
    rows_dma(26, gate_b_d, 4, "nb")
    rows_dma(30, conv_b_d, 12, "cb")
    for j in range(4):
        rows_dma(42 + 12 * j, conv_w_d[j], 12, "cw%d" % j)
    VR = ["vrows_" + t for t in ("nw", "pnw", "snw", "gnw", "nb", "cb", "cw0", "cw1", "cw2", "cw3")]
    (pvv, pvvn) = psb()
    op("tensor", lambda e: e.matmul(out=pvv[:, 0:NV], lhsT=vrows[:, :], rhs=identf[0:NV, 0:NV], start=True, stop=True), r=VR + ["identf"], w=[pvvn])
    VN = ["nw", "pnw", "snw", "gnw", "nb", "cb", "cw"]
    op("vector", lambda e: e.tensor_copy(out=vecs[:, :], in_=pvv[:, 0:NV]), r=[pvvn], w=VN)
    small_dma(dtb_bc[:], dt_bias_d.partition_broadcast(128), "dtb_bc")
    small_dma(aneg_bc[:], a_log_d.partition_broadcast(128), "aneg_bc")
    small_dma(dsk_bc[:], d_skip_d.partition_broadcast(128), "dsk_bc")
    small_dma(flag[:], flag_d[:, :], "flag")
    op("gpsimd", lambda e: e.tensor_scalar(out=nb, in0=nb, scalar1=-1.0, scalar2=None, op0=ALU.mult), r=["nb"], w=["nb"])
    op("scalar", lambda e: e.activation(out=aneg_bc[:], in_=aneg_bc[:], func=AF.Exp), r=["aneg_bc"], w=["aneg_bc"])
    op("gpsimd", lambda e: e.tensor_scalar(out=aneg_bc[:], in0=aneg_bc[:], scalar1=-1.0, scalar2=None, op0=ALU.mult),
       r=["aneg_bc"], w=["aneg_bc"])


    Win = big[:, :].rearrange("p (k c) -> p k c", k=8)
    Wout = big[:, 0:16384].rearrange("p (k c) -> p k c", k=16)
    Wg = big[:, 16384:24576].rearrange("p (k c) -> p k c", k=8)
    Wpe = big[:, 24576:26624].rearrange("p (k c) -> p k c", k=2)

    Sg = sb("Sg", [128, 4, 256])
    Sgb = sb("Sgb", [128, 4, 256], BF16)
    Ss = sb("Ss", [128, 2, 512])
    Ssb = sb("Ssb", [128, 2, 512], BF16)
    halo = sb("halo", [128, 12, 3])
    for t_, n_ in ((Sg, ["Sg"]), (Sgb, ["Sgb"]), (Ss, ["Ss0", "Ss1"]), (Ssb, ["Ssb"]), (halo, ["halo"])):
        op("gpsimd", (lambda t_: (lambda e: e.memset(t_[:], 0.0)))(t_), w=n_)

    xt_pool = Rot(nc, "xt", 2, [128, D], F32)
    xn_pool = Rot(nc, "xn", NT, [128, D], BF16)
    uT = sb("uT", [128, 8, NTOK], BF16)
    glr = sb("glr", [16, NTOK], BF16)
    qdec2 = [sb("qdec%d" % i, [128, 4, NTOK], BF16) for i in range(2)]
    kinv = sb("kinv", [128, 4, NTOK], BF16)
    kdT = sb("kdT", [128, 4, NTOK], BF16)
    decg2 = [sb("decg%d" % i, [128, 4, NT]) for i in range(2)]
    xc = sb("xc", [128, 10, NTOK], BF16)
    xcC2 = [sb("xcC%d" % i, [128, 2, NTOK], BF16) for i in range(2)]
    gtmp = Rot(nc, "gtmp", 6, [128, NTOK], F32)
    ncl_pool = Rot(nc, "ncl", 2, [128, NT], F32)
    xpre_pool = Rot(nc, "xpre", 4, [128, NTOK + 3], F32)
    small = Rot(nc, "sm", 20, [128, 16], F32)
    V_pool = Rot(nc, "V", 2, [128, 1024], BF16)
    sg_pool = Rot(nc, "sg", 2, [128, 1024], BF16)
    sz_pool = Rot(nc, "sz", 2, [128, 1024], BF16)
    scm_pool = Rot(nc, "scm", 2, [128, 4, 128], BF16)
    kd_pool = Rot(nc, "kd", 2, [128, 512], BF16)
    xdt_pool = Rot(nc, "xdt", 2, [128, 1024], BF16)
    xdd_pool = Rot(nc, "xdd", 2, [128, 1024], BF16)
    xsD_pool = Rot(nc, "xsD", 2, [128, 1024], F32)
    Btm_pool = Rot(nc, "Btm", 2, [128, 256], BF16)
    ex_pool = Rot(nc, "ex", 2, [128, 48], F32)
    rhs2_pool = Rot(nc, "rhs2", 1, [128, 16, 128], BF16)
    dec_pool = Rot(nc, "dec", 2, [128, 8, 128], BF16)
    cbm_pool = Rot(nc, "cbm", 2, [128, 2, 128], BF16)
    MT_pool = Rot(nc, "MT", 2, [128, 8, 128], BF16)
    yt_pool = Rot(nc, "ytmp", 2, [128, 512], F32)
    mixed_pool = Rot(nc, "mixed", 1, [128, 2048], BF16)
    scr_d = nc.dram_tensor("mix_scratch", [max(TM, 128), 2048], BF16, kind="Internal").ap()

    stage_bufs = [(xdt_pool.bufs[0][0][:, :].bitcast(F32), "xdt0"), (xdt_pool.bufs[1][0][:, :].bitcast(F32), "xdt1"),
                  (xdd_pool.bufs[0][0][:, :].bitcast(F32), "xdd0"), (xdd_pool.bufs[1][0][:, :].bitcast(F32), "xdd1"),
                  (V_pool.bufs[0][0][:, :].bitcast(F32), "V0"), (V_pool.bufs[1][0][:, :].bitcast(F32), "V1")]
    stg_i = [0]
    cast_engs = ["vector", "gpsimd", "scalar", "vector", "scalar"]
    ci = [0]

    def load_w(dst3, kc, c0, c1, src_rows, scale_ap, scale_name, wname, extra_r=()):
        (st, stn) = stage_bufs[stg_i[0] % len(stage_bufs)]
        stg_i[0] += 1
        n = c1 - c0
        op("sync", lambda e: e.dma_start(out=st[:, 0:n], in_=src_rows[:, c0:c1]), w=[stn], dma="w_" + stn)
        eng = cast_engs[ci[0] % len(cast_engs)]
        ci[0] += 1
        rr = [stn] + list(extra_r) + ([scale_name] if scale_name else [])
        if eng == "scalar":
            if scale_ap is None:
                f = lambda e: e.activation(out=dst3[:, kc, c0:c1], in_=st[:, 0:n], func=AF.Copy)
            else:
                f = lambda e: e.activation(out=dst3[:, kc, c0:c1], in_=st[:, 0:n], func=AF.Copy, scale=scale_ap)
        else:
            if scale_ap is None:
                f = lambda e: e.tensor_copy(out=dst3[:, kc, c0:c1], in_=st[:, 0:n])
            elif eng == "gpsimd":
                f = lambda e: e.tensor_scalar(out=dst3[:, kc, c0:c1], in0=st[:, 0:n], scalar1=scale_ap, scalar2=0.0, op0=ALU.mult, op1=ALU.add)
            else:
                f = lambda e: e.tensor_scalar(out=dst3[:, kc, c0:c1], in0=st[:, 0:n], scalar1=scale_ap, scalar2=None, op0=ALU.mult)
        op(eng, f, r=rr, w=[wname])

    upst = xsD_pool.bufs[0][0]
    op("sync", lambda e: e.dma_start(out=upst[0:16, 0:512], in_=up_d[:, :]), w=["xsD0"], dma="c_up")
    op("vector", lambda e: e.tensor_copy(out=up_b[:], in_=upst[0:16, 0:512]), r=["xsD0"], w=["up_b"])

    def wn(c0, n):
        return ["W%d" % p for p in range(c0 // 512, (c0 + n - 1) // 512 + 1)]
    ALLW = ["W%d" % p for p in range((DIN + 511) // 512)]
    def load_pieces(pieces):
        for piece in pieces:
            c0 = piece * 512
            for kc in range(8):
                load_w(Win, kc, c0, min(DIN, c0 + 512), w_in_d[kc * 128:(kc + 1) * 128, :], nw[:, kc:kc + 1], "nw", "W%d" % piece)
    load_pieces((6, 0, 1, 8, 9, 10, 11))
    late_pieces = [(2, 3, 4, 5, 7)]

    def rstd_from(ssq_ap, ssq_name, n, inv_count):
        (t1, t1n) = small.get()
        (t2, t2n) = small.get()
        op("gpsimd", lambda e: e.tensor_scalar(out=t1[:, 0:n], in0=ssq_ap, scalar1=inv_count, scalar2=EPS, op0=ALU.mult, op1=ALU.add),
           r=[ssq_name], w=[t1n])
        op("gpsimd", lambda e: e.tensor_tensor(out=t2[:, 0:n], in0=t1[:, 0:n], in1=nhalf[:, 0:n], op=ALU.pow), r=[t1n, "nhalf"], w=[t2n])
        return t2, t2n

    def transposes(src_aps, src_names, dst_ap, dst_name, copy_eng, extra_r=()):
        n = len(src_aps)
        (pb_, pbn) = psb()
        pv = pb_.bitcast(BF16)

        def f(e):
            last = None
            for i, a in enumerate(src_aps):
                last = e.transpose(out=pv[:, i * 128:(i + 1) * 128], in_=a, identity=ident[:])
            return last
        op("tensor", f, r=list(src_names) + ["ident"], w=[pbn])
        src = pv[:, 0:n * 128]
        if dst_ap.ndim == 3:
            src = src.rearrange("p (k t) -> p k t", k=n)
        if copy_eng == "scalar":
            op("scalar", lambda e: e.activation(out=dst_ap, in_=src, func=AF.Copy), r=[pbn] + list(extra_r), w=[dst_name])
        else:
            op("vector", lambda e: e.tensor_copy(out=dst_ap, in_=src), r=[pbn] + list(extra_r), w=[dst_name])

    def inproj_fm(c0, m):
        (pb_, pbn) = psb()

        def f(e):
            last = None
            for kc in range(8):
                last = e.matmul(out=pb_[0:m, 0:NTOK], lhsT=Win[:, kc, c0:c0 + m], rhs=uT[:, kc, :], start=(kc == 0), stop=(kc == 7))
            return last
        op("tensor", f, r=wn(c0, m) + ["uT"], w=[pbn])
        return pb_, pbn

    def inproj_tm(t, c0, n):
        (pb_, pbn) = psb()

        def f(e):
            last = None
            for kc in range(8):
                last = e.matmul(out=pb_[:, 0:n], lhsT=uT[:, kc, t * 128:(t + 1) * 128], rhs=Win[:, kc, c0:c0 + n], start=(kc == 0), stop=(kc == 7))
            return last
        op("tensor", f, r=wn(c0, n) + ["uT"], w=[pbn])
        return pb_, pbn

    def x_loads(src_d, tok0):
        xts = []
        for t in range(NT):
            (xt, xtn) = xt_pool.get()
            r0 = tok0 + t * 128
            op("sync", (lambda xt, r0: lambda e: e.dma_start(out=xt[:], in_=src_d[r0:r0 + 128, :]))(xt, r0), w=[xtn], dma="x_" + xtn.split("#")[0])
            xts.append((xt, xtn))
        return xts

    def phaseA_pre(xts):
        tl = []
        for t in range(NT):
            (xt, xtn) = xts[t]
            (ssq, ssqn) = small.get()
            (xn, xnn) = xn_pool.get()
            op("scalar", (lambda xt, ssq, xn: lambda e: e.activation(out=xn[:], in_=xt[:], func=AF.Square, accum_out=ssq[:, 0:1]))(xt, ssq, xn),
               r=[xtn], w=[ssqn, xnn])
            tl.append((xt, xtn, ssq, ssqn, xn, xnn))
        rl = [rstd_from(ssq[:, 0:1], ssqn, 1, 1.0 / D) for (xt, xtn, ssq, ssqn, xn, xnn) in tl]
        xns = []
        for (xt, xtn, ssq, ssqn, xn, xnn), (rs, rsn) in zip(tl, rl):
            op("vector", (lambda xt, xn, rs: lambda e: e.tensor_scalar(out=xn[:], in0=xt[:], scalar1=rs[:, 0:1], scalar2=None, op0=ALU.mult))(xt, xn, rs),
               r=[xtn, rsn], w=[xnn])
            xns.append((xn, xnn))
        return xns

    def phaseAB(xns, main, par, all_chunks):
        qdec = qdec2[par]; qn = "qdec%d" % par
        decg = decg2[par]; dgn = "decg%d" % par
        xcC = xcC2[par]; xcn = "xcC%d" % par
        for t in range(NT):
            (xn, xnn) = xns[t]
            transposes([xn[:, k * 128:(k + 1) * 128] for k in range(8)], [xnn], uT[:, :, t * 128:(t + 1) * 128], "uT", "vector")

        pb_, pbn = inproj_fm(C_GLR, 16)
        op("scalar", lambda e: e.activation(out=glr[:, :], in_=pb_[0:16, 0:NTOK], func=AF.Copy), r=[pbn], w=["glr"])
        for hp in range(2):
            hs = (2 * hp, 2 * hp + 1)
            H = {}
            for h in hs:
                (pz, pzn) = psb()
                op("tensor", (lambda pz, h: lambda e: e.matmul(out=pz[:, 0:NTOK], lhsT=up_b[:, h * 128:(h + 1) * 128], rhs=glr[:, :], start=True, stop=True))(pz, h),
                   r=["up_b", "glr"], w=[pzn])
                H[h] = dict(pz=pz, pzn=pzn)
            for h in hs:
                d = H[h]
                (d["l"], d["ln"]) = gtmp.get()
                op("scalar", (lambda pz, l, h: lambda e: e.activation(out=l[:], in_=pz[:, 0:NTOK], func=AF.Exp, scale=-1.0, bias=nb[:, h:h + 1]))(d["pz"], d["l"], h),
                   r=[d["pzn"], "nb"], w=[d["ln"]])
            for h in hs:
                d = H[h]
                op("scalar", (lambda l: lambda e: e.activation(out=l[:], in_=l[:], func=AF.Ln, bias=1.0))(d["l"]), r=[d["ln"]], w=[d["ln"]])
            for h in hs:
                d = H[h]
                (d["cl"], d["cln"]) = gtmp.get()
                op("vector", (lambda l, cl: lambda e: e.tensor_tensor_scan(out=cl[:], data0=rmask[:], data1=l[:], initial=0.0, op0=ALU.mult, op1=ALU.add))(d["l"], d["cl"]),
                   r=[d["ln"], "rmask"], w=[d["cln"]])
            for h in hs:
                d = H[h]
                (d["e1"], d["e1n"]) = gtmp.get()
                op("scalar", (lambda cl, e1: lambda e: e.activation(out=e1[:], in_=cl[:], func=AF.Exp, scale=-1.0 / 16))(d["cl"], d["e1"]), r=[d["cln"]], w=[d["e1n"]])
            for h in hs:
                d = H[h]
                op("gpsimd", (lambda e1, h: lambda e: e.tensor_copy(out=decg[:, h, :], in_=e1[:].rearrange("p (t i) -> p t i", i=128)[:, :, 127]))(d["e1"], h),
                   r=[d["e1n"]], w=[dgn])
                (d["ncl"], d["ncln"]) = ncl_pool.get()
                op("gpsimd", (lambda cl, ncl: lambda e: e.tensor_scalar(out=ncl[:], in0=cl[:].rearrange("p (t i) -> p t i", i=128)[:, :, 127],
                                                                         scalar1=-1.0 / 16, scalar2=0.0, op0=ALU.mult, op1=ALU.add))(d["cl"], d["ncl"]), r=[d["cln"]], w=[d["ncln"]])
            for h in hs:
                d = H[h]
                (xb, xbn) = xpre_pool.get()
                d["ed"], d["edn"] = xb[:, 0:NTOK], [xbn, xbn + "h"]

                def fed(e, cl=d["cl"], ed=d["ed"], ncl=d["ncl"]):
                    last = None
                    for t in range(NT):
                        last = e.activation(out=ed[:, t * 128:(t + 1) * 128], in_=cl[:, t * 128:(t + 1) * 128], func=AF.Exp, scale=1.0 / 16, bias=ncl[:, t:t + 1])
                    return last
                op("scalar", fed, r=[d["cln"], d["ncln"]], w=d["edn"])
            if main:
                for h in hs:
                    d = H[h]
                    (xb, xbn) = xpre_pool.get()
                    d["e2"], d["e2n"] = xb[:, 0:NTOK], [xbn, xbn + "h"]
                    op("scalar", (lambda cl, e2: lambda e: e.activation(out=e2, in_=cl[:], func=AF.Exp, scale=1.0 / 16))(d["cl"], d["e2"]), r=[d["cln"]], w=d["e2n"])
                for h in hs:
                    d = H[h]
                    pq, pqn = inproj_fm(C_Q + h * 128, 128)
                    op("vector", (lambda pq, e1, h: lambda e: e.scalar_tensor_tensor(out=qdec[:, h, :], in0=pq[:, 0:NTOK], scalar=128.0 ** -0.5, in1=e1[:],
                                                                                      op0=ALU.mult, op1=ALU.mult))(pq, d["e1"], h), r=[pqn, d["e1n"]], w=[qn])
            for h in hs:
                d = H[h]
                pk, pkn = inproj_fm(C_K + h * 128, 128)
                if main:
                    op("vector", (lambda pk, e2, h: lambda e: e.tensor_tensor(out=kinv[:, h, :], in0=pk[:, 0:NTOK], in1=e2, op=ALU.mult))(pk, d["e2"], h),
                       r=[pkn] + d["e2n"], w=["kinv"])
                op("vector", (lambda pk, ed, h: lambda e: e.tensor_tensor(out=kdT[:, h, :], in0=pk[:, 0:NTOK], in1=ed, op=ALU.mult))(pk, d["ed"], h),
                   r=[pkn] + d["edn"], w=["kdT"])

        nchunks = 12 if all_chunks else 10
        pend = []
        for c0 in range(0, nchunks, 2):
            pair = []
            for c in (c0, c0 + 1):
                pc, pcn = inproj_fm(C_XBC + c * 128, 128)
                (xpre, xpren) = xpre_pool.get()
                op("gpsimd", (lambda xpre, c: lambda e: e.tensor_copy(out=xpre[:, 0:3], in_=halo[:, c, :]))(xpre, c), r=["halo"], w=[xpren + "h"])
                op("scalar", (lambda xpre, pc: lambda e: e.activation(out=xpre[:, 3:3 + NTOK], in_=pc[:, 0:NTOK], func=AF.Copy))(xpre, pc), r=[pcn], w=[xpren])
                op("gpsimd", (lambda xpre, c: lambda e: e.tensor_copy(out=halo[:, c, :], in_=xpre[:, NTOK:NTOK + 3]))(xpre, c), r=[xpren, xpren + "h"], w=["halo"])
                (acc, accn) = gtmp.get()
                op("scalar", (lambda pc, acc, c: lambda e: e.activation(out=acc[:], in_=pc[:, 0:NTOK], func=AF.Identity,
                                                                        scale=cw[:, 3, c:c + 1], bias=cb[:, c:c + 1]))(pc, acc, c),
                   r=[pcn, "cw", "cb"], w=[accn])
                pair.append((c, xpre, xpren, acc, accn))
            for j in range(3):
                for (c, xpre, xpren, acc, accn) in pair:
                    op("vector", (lambda xpre, acc, c, j: lambda e: e.scalar_tensor_tensor(out=acc[:], in0=xpre[:, j:j + NTOK], scalar=cw[:, j, c:c + 1], in1=acc[:],
                                                                                            op0=ALU.mult, op1=ALU.add))(xpre, acc, c, j),
                       r=[xpren, xpren + "h", accn, "cw"], w=[accn])
            for f in pend:
                f()
            pend = []
            for (c, xpre, xpren, acc, accn) in pair:
                if c < 10:
                    pend.append((lambda acc, accn, c: lambda: op("scalar", lambda e: e.activation(out=xc[:, c, :], in_=acc[:], func=AF.Silu), r=[accn], w=["xc"]))(acc, accn, c))
                else:
                    pend.append((lambda acc, accn, c: lambda: op("scalar", lambda e: e.activation(out=xcC[:, c - 10, :], in_=acc[:], func=AF.Silu), r=[accn], w=[xcn]))(acc, accn, c))
        for f in pend:
            f()

    def tile_gen(tok0, t, main, par):
        qdec = qdec2[par]; qn = "qdec%d" % par
        decg = decg2[par]; dgn = "decg%d" % par
        xcC = xcC2[par]; xcn = "xcC%d" % par
        tsl = slice(t * 128, (t + 1) * 128)
        if main:
            (sg, sgn) = sg_pool.get()
            (sz, szn) = sz_pool.get()
            for (dst, dstn, c0) in ((sg, sgn, C_G), (sz, szn, C_Z)):
                for half in range(2):
                    pg, pgn = inproj_tm(t, c0 + half * 512, 512)
                    op("scalar", (lambda pg, dst, half: lambda e: e.activation(out=dst[:, half * 512:(half + 1) * 512], in_=pg[:, :], func=AF.Silu))(pg, dst, half),
                       r=[pgn], w=[dstn])
        pd, pdn = inproj_tm(t, C_DT, 16)
        (dtr, dtrn) = small.get()
        op("vector", (lambda pd, dtr: lambda e: e.tensor_tensor(out=dtr[:], in0=pd[:, 0:16], in1=dtb_bc[:], op=ALU.add))(pd, dtr), r=[pdn, "dtb_bc"], w=[dtrn])
        op("scalar", (lambda dtr: lambda e: e.activation(out=dtr[:], in_=dtr[:], func=AF.Exp))(dtr), r=[dtrn], w=[dtrn])
        (dt, dtn) = small.get()
        op("scalar", (lambda dtr, dt: lambda e: e.activation(out=dt[:], in_=dtr[:], func=AF.Ln, bias=1.0))(dtr, dt), r=[dtrn], w=[dtn])
        (dtA, dtAn) = small.get()
        op("gpsimd", (lambda dt, dtA: lambda e: e.tensor_tensor(out=dtA[:], in0=dt[:], in1=aneg_bc[:], op=ALU.mult))(dt, dtA), r=[dtn, "aneg_bc"], w=[dtAn])
        (V, Vn) = V_pool.get()
        for half in range(2):
            pv, pvn = inproj_tm(t, C_V + half * 512, 512)
            if half == 0:
                op("scalar", (lambda pv, V: lambda e: e.activation(out=V[:, 0:512], in_=pv[:, :], func=AF.Copy))(pv, V), r=[pvn], w=[Vn])
            else:
                op("vector", (lambda pv, V: lambda e: e.tensor_copy(out=V[:, 512:1024], in_=pv[:, :]))(pv, V), r=[pvn], w=[Vn])
        (pcm, pcmn) = psb()

        def fcm(e, pcm=pcm, dtA=dtA):
            e.matmul(out=pcm[:, 0:16], lhsT=L_f[:], rhs=dtA[:], start=True, stop=True)
            e.matmul(out=pcm[:, 16:32], lhsT=SU_f[:], rhs=dtA[:], start=True, stop=True)
            return e.matmul(out=pcm[:, 32:48], lhsT=ones_f[:], rhs=dtA[:], start=True, stop=True)
        op("tensor", fcm, r=["L_f", "SU_f", "ones_f", dtAn], w=[pcmn])
        (ex, exn) = ex_pool.get()
        op("scalar", (lambda pcm, ex: lambda e: e.activation(out=ex[:], in_=pcm[:, 0:48], func=AF.Exp))(pcm, ex), r=[pcmn], w=[exn])
        if main:
            (rhs2, rhs2n) = rhs2_pool.get()
            op("gpsimd", (lambda rhs2, dtA: lambda e: e.tensor_tensor(out=rhs2[:], in0=dtA[:].unsqueeze(2).broadcast_to([128, 16, 128]),
                                                                       in1=L_f[:].unsqueeze(1).broadcast_to([128, 16, 128]), op=ALU.mult))(rhs2, dtA),
               r=[dtAn, "L_f"], w=[rhs2n])
        (pxa, pxan) = psb()
        pxav = pxa.bitcast(BF16)

        def ftx(e, pxav=pxav):
            last = None
            for c in range(8):
                last = e.transpose(out=pxav[:, c * 128:(c + 1) * 128], in_=xc[:, c, tsl], identity=ident[:])
            return last
        op("tensor", ftx, r=["xc", "ident"], w=[pxan])
        (xdt, xdtn) = xdt_pool.get()
        op("vector", (lambda pxav, xdt, dt: lambda e: e.tensor_tensor(out=xdt[:].rearrange("p (h q) -> p h q", h=16),
                                                                        in0=pxav[:, :].rearrange("p (h q) -> p h q", h=16),
                                                                        in1=dt[:].unsqueeze(2).broadcast_to([128, 16, 64]), op=ALU.mult))(pxav, xdt, dt),
           r=[pxan, dtn], w=[xdtn])
        if main:
            (xsD, xsDn) = xsD_pool.get()
            op("vector", (lambda pxav, xsD: lambda e: e.tensor_tensor(out=xsD[:].rearrange("p (h q) -> p h q", h=16),
                                                                        in0=pxav[:, :].rearrange("p (h q) -> p h q", h=16),
                                                                        in1=dsk_bc[:].unsqueeze(2).broadcast_to([128, 16, 64]), op=ALU.mult))(pxav, xsD),
               r=[pxan, "dsk_bc"], w=[xsDn])
        (Btm, Btmn) = Btm_pool.get()
        transposes([xc[:, 8 + g, tsl] for g in range(2)], ["xc"], Btm[:, :], Btmn, "vector")
        (xdd, xddn) = xdd_pool.get()
        op("gpsimd", (lambda xdt, xdd, ex: lambda e: e.tensor_tensor(out=xdd[:].rearrange("p (h q) -> p h q", h=16),
                                                                      in0=xdt[:].rearrange("p (h q) -> p h q", h=16),
                                                                      in1=ex[:, 16:32].unsqueeze(2).broadcast_to([128, 16, 64]), op=ALU.mult))(xdt, xdd, ex),
           r=[xdtn, exn], w=[xddn])
        (kd, kdn) = kd_pool.get()
        transposes([kdT[:, h, tsl] for h in range(4)], ["kdT"], kd[:, :], kdn, "vector")
        MTs = []
        if main:
            (psc, pscn) = psb()

            def fsc(e, psc=psc):
                last = None
                for h in range(4):
                    last = e.matmul(out=psc[:, h * 128:(h + 1) * 128], lhsT=kinv[:, h, tsl], rhs=qdec[:, h, tsl], start=True, stop=True)
                return last
            op("tensor", fsc, r=["kinv", qn], w=[pscn])
            (scm, scmn) = scm_pool.get()
            op("vector", (lambda psc, scm: lambda e: e.tensor_tensor(out=scm[:], in0=psc[:, :].rearrange("p (h i) -> p h i", h=4),
                                                                       in1=L_f[:].unsqueeze(1).broadcast_to([128, 4, 128]), op=ALU.mult))(psc, scm),
               r=[pscn, "L_f"], w=[scmn])
            (pcb, pcbn) = psb()

            def fcb(e, pcb=pcb):
                e.matmul(out=pcb[:, 0:128], lhsT=xc[:, 8, tsl], rhs=xcC[:, 0, tsl], start=True, stop=True)
                return e.matmul(out=pcb[:, 128:256], lhsT=xc[:, 9, tsl], rhs=xcC[:, 1, tsl], start=True, stop=True)
            op("tensor", fcb, r=["xc", xcn], w=[pcbn])
            (cbm, cbmn) = cbm_pool.get()
            op("vector", (lambda pcb, cbm: lambda e: e.tensor_tensor(out=cbm[:], in0=pcb[:, 0:256].rearrange("p (g i) -> p g i", g=2),
                                                                       in1=L_f[:].unsqueeze(1).broadcast_to([128, 2, 128]), op=ALU.mult))(pcb, cbm),
               r=[pcbn, "L_f"], w=[cbmn])
            decs = [dec_pool.get() for g in range(2)]
            psgs = []
            for g in range(2):
                for q in range(2):
                    (psg, psgn) = psb()
                    hb = g * 8 + q * 4
                    op("tensor", (lambda psg, rhs2, hb: lambda e: e.matmul(out=psg[:, :], lhsT=SU_b[:], rhs=rhs2[:, hb:hb + 4, :], start=True, stop=True))(psg, rhs2, hb),
                       r=["SU_b", rhs2n], w=[psgn])
                    psgs.append((g, q, psg, psgn))
            for (g, q, psg, psgn) in psgs:
                (dec, decn) = decs[g]
                op("scalar", (lambda psg, dec, q: lambda e: e.activation(out=dec[:, q * 4:(q + 1) * 4, :], in_=psg[:, :].rearrange("p (h i) -> p h i", h=4),
                                                                          func=AF.Exp))(psg, dec, q), r=[psgn], w=[decn])
            for g in range(2):
                (dec, decn) = decs[g]
                (MT, MTn) = MT_pool.get()
                op("vector", (lambda MT, dec, cbm, g: lambda e: e.tensor_tensor(out=MT[:], in0=dec[:], in1=cbm[:, g, :].unsqueeze(1).broadcast_to([128, 8, 128]),
                                                                                 op=ALU.mult))(MT, dec, cbm, g), r=[decn, cbmn], w=[MTn])
                MTs.append((MT, MTn))
        yield
        pos = []
        if main:
            for hp in range(2):
                (po, pon) = psb(long=True)

                def fo(e, po=po, hp=hp):
                    last = None
                    for hh in range(2):
                        h = hp * 2 + hh
                        e.matmul(out=po[:, hh * 256:(hh + 1) * 256], lhsT=scm[:, h, :], rhs=V[:, h * 256:(h + 1) * 256], start=True, stop=False)
                        last = e.matmul(out=po[:, hh * 256:(hh + 1) * 256], lhsT=qdec[:, h, tsl], rhs=Sgb[:, h, :], start=False, stop=True)
                    return last
                op("tensor", fo, r=[scmn, Vn, qn, "Sgb"], w=[pon])
                pos.append((po, pon))
        for hp in range(2):
            (pds, pdsn) = psb()

            def fds(e, pds=pds, hp=hp):
                last = None
                for hh in range(2):
                    h = hp * 2 + hh
                    last = e.matmul(out=pds[:, hh * 256:(hh + 1) * 256], lhsT=kd[:, h * 128:(h + 1) * 128], rhs=V[:, h * 256:(h + 1) * 256], start=True, stop=True)
                return last
            op("tensor", fds, r=[kdn, Vn], w=[pdsn])
            for hh in range(2):
                h = hp * 2 + hh
                op("vector", (lambda pds, h, hh: lambda e: e.scalar_tensor_tensor(out=Sg[:, h, :], in0=Sg[:, h, :], scalar=decg[:, h, t:t + 1],
                                                                                   in1=pds[:, hh * 256:(hh + 1) * 256], op0=ALU.mult, op1=ALU.add))(pds, h, hh),
                   r=["Sg", dgn, pdsn], w=["Sg"])
        if not main:
            op("scalar", lambda e: e.activation(out=Sgb[:], in_=Sg[:], func=AF.Copy), r=["Sg"], w=["Sgb"])
        if main:
            (mixed, mixedn) = mixed_pool.get()
            (ssqg, ssqgn) = small.get()
            for h in range(4):
                po, pon = pos[h // 2]
                hh = h % 2
                op("scalar", (lambda po, hh, h: lambda e: e.activation(out=mixed[:, h * 256:(h + 1) * 256], in_=po[:, hh * 256:(hh + 1) * 256], func=AF.Square,
                                                                        accum_out=ssqg[:, h:h + 1]))(po, hh, h), r=[pon], w=[ssqgn, mixedn + "g"])
            op("scalar", lambda e: e.activation(out=Sgb[:], in_=Sg[:], func=AF.Copy), r=["Sg"], w=["Sgb"])
            rg, rgn = rstd_from(ssqg[:, 0:4], ssqgn, 4, 1.0 / 256)
            for h in range(4):
                po, pon = pos[h // 2]
                hh = h % 2
                op("vector", (lambda po, hh, h: lambda e: e.scalar_tensor_tensor(
                    out=mixed[:, h * 256:(h + 1) * 256], in0=po[:, hh * 256:(hh + 1) * 256], scalar=rg[:, h:h + 1],
                    in1=sg[:, h * 256:(h + 1) * 256], op0=ALU.mult, op1=ALU.mult))(po, hh, h),
                   r=[pon, rgn, sgn], w=[mixedn + "g"])
        yield
        yzs = []
        saved_banks = None
        if CUR[0] is not None and CUR[0].banks == (4, 5):
            saved_banks = CUR[0]
            saved_banks.banks = (4, 5, 6, 7)
        if main:
            yl = []
            for g in range(2):
                (MT, MTn) = MTs[g]
                (py, pyn) = psb()

                def fy(e, py=py, MT=MT, g=g):
                    last = None
                    for hh in range(8):
                        h = g * 8 + hh
                        last = e.matmul(out=py[:, hh * 64:(hh + 1) * 64], lhsT=MT[:, hh, :], rhs=xdt[:, h * 64:(h + 1) * 64], start=True, stop=True)
                    return last
                op("tensor", fy, r=[MTn, xdtn], w=[pyn])
                (pyi, pyin) = psb()
                op("tensor", (lambda pyi, g: lambda e: e.matmul(out=pyi[:, :], lhsT=xcC[:, g, tsl], rhs=Ssb[:, g, :], start=True, stop=True))(pyi, g),
                   r=[xcn, "Ssb"], w=[pyin])
                (t1, t1n) = yt_pool.get()
                yl.append((py, pyn, pyi, pyin, t1, t1n))
            for g in range(2):
                (py, pyn, pyi, pyin, t1, t1n) = yl[g]
                op("vector", (lambda pyi, t1, g: lambda e: e.tensor_tensor(out=t1[:].rearrange("p (h q) -> p h q", h=8),
                                                                            in0=pyi[:, :].rearrange("p (h q) -> p h q", h=8),
                                                                            in1=ex[:, g * 8:(g + 1) * 8].unsqueeze(2).broadcast_to([128, 8, 64]), op=ALU.mult))(pyi, t1, g),
                   r=[pyin, exn], w=[t1n])
            for g in range(2):
                (py, pyn, pyi, pyin, t1, t1n) = yl[g]
                op("vector", (lambda py, t1: lambda e: e.tensor_tensor(out=t1[:], in0=py[:, :], in1=t1[:], op=ALU.add))(py, t1), r=[pyn, t1n], w=[t1n])
                yzs.append((t1, t1n))
        psl = []
        for g in range(2):
            (pss, pssn) = psb()
            op("tensor", (lambda pss, g: lambda e: e.matmul(out=pss[:, :], lhsT=Btm[:, g * 128:(g + 1) * 128], rhs=xdd[:, g * 512:(g + 1) * 512],
                                                             start=True, stop=True))(pss, g), r=[Btmn, xddn], w=[pssn])
            psl.append((pss, pssn))
        if saved_banks is not None:
            saved_banks.banks = (4, 5)
        for g in range(2):
            op("vector", (lambda g: lambda e: e.tensor_tensor(out=Ss[:, g, :].rearrange("p (h q) -> p h q", h=8),
                                                              in0=Ss[:, g, :].rearrange("p (h q) -> p h q", h=8),
                                                              in1=ex[:, 32 + g * 8:40 + g * 8].unsqueeze(2).broadcast_to([128, 8, 64]), op=ALU.mult))(g),
               r=["Ss%d" % g, exn, "Ssb"], w=["Ss%d" % g])
        for g in range(2):
            (pss, pssn) = psl[g]
            op("vector", (lambda pss, g: lambda e: e.tensor_tensor(out=Ss[:, g, :], in0=Ss[:, g, :], in1=pss[:, :], op=ALU.add))(pss, g),
               r=["Ss%d" % g, pssn], w=["Ss%d" % g])
        op("scalar", lambda e: e.activation(out=Ssb[:], in_=Ss[:], func=AF.Copy), r=["Ss0", "Ss1"], w=["Ssb"])
        if main:
            (ssqs, ssqsn) = small.get()
            for g in range(2):
                (t1, t1n) = yzs[g]
                op("vector", (lambda t1, g: lambda e: e.tensor_tensor(out=t1[:], in0=t1[:], in1=xsD[:, g * 512:(g + 1) * 512], op=ALU.add))(t1, g),
                   r=[t1n, xsDn], w=[t1n])
            for g in range(2):
                (t1, t1n) = yzs[g]
                op("vector", (lambda t1, g: lambda e: e.tensor_tensor(out=t1[:], in0=t1[:], in1=sz[:, g * 512:(g + 1) * 512], op=ALU.mult))(t1, g),
                   r=[t1n, szn], w=[t1n])
            for g in range(2):
                (t1, t1n) = yzs[g]
                op("scalar", (lambda t1, g: lambda e: e.activation(out=mixed[:, 1024 + g * 512:1024 + (g + 1) * 512], in_=t1[:], func=AF.Square, accum_out=ssqs[:, g:g + 1]))(t1, g),
                   r=[t1n], w=[ssqsn, mixedn + "s%d" % g])
            rss, rssn = rstd_from(ssqs[:, 0:2], ssqsn, 2, 1.0 / 512)
            for g in range(2):
                t1, t1n = yzs[g]
                op("scalar", (lambda t1, g: lambda e: e.activation(out=mixed[:, 1024 + g * 512:1024 + (g + 1) * 512], in_=t1[:], func=AF.Copy,
                                                                    scale=rss[:, g:g + 1]))(t1, g), r=[t1n, rssn], w=[mixedn + "s%d" % g])
            r0 = tok0 + t * 128
            op("sync", (lambda r0: lambda e: e.dma_start(out=scr_d[r0:r0 + 128, :], in_=mixed[:]))(r0),
               r=[mixedn + "g", mixedn + "s0", mixedn + "s1"], w=["scr%d" % (r0 // 128)], dma="m_" + mixedn.split("#")[0])
        yield

    pendq = []

    def partner(nhalf):
        todo = []
        for a in pendq:
            while a[1] > 0 and len(todo) < nhalf:
                todo.append(a[0])
                a[1] -= 1
        pendq[:] = [a for a in pendq if a[1] > 0]
        if not todo:
            return None

        def f():
            for g in todo:
                next(g)
        return f

    def slot(main_fn, nhalf, extra=None, wmain=1):
        p = partner(nhalf)
        fns, bks, ws = [], [], []
        if p is not None:
            fns.append(p); bks.append((4, 5)); ws.append(1)
        fns.append(main_fn); bks.append((0, 1, 2, 3) if p is not None else (0, 1, 2, 3, 4, 5)); ws.append(wmain if p is not None else (2 if extra is not None else 1))
        if extra is not None:
            fns.append(extra); bks.append(None); ws.append(1)
        interleave(fns, bks, "tile", ws)

    def run_phase(src_d, ntok, main):
        nsup = ntok // NTOK
        if nsup == 0:
            return
        st = {"xns": phaseA_pre(x_loads(src_d, 0)), "xts": x_loads(src_d, NTOK) if nsup > 1 else None}
        for s in range(nsup):
            par = s % 2
            xns = st["xns"]
            extra0 = None
            if late_pieces:
                extra0 = (lambda ps_: lambda: load_pieces(ps_))(late_pieces.pop())
            slot(lambda: phaseAB(xns, main, par, main or s == nsup - 1), 1, extra0, wmain=6)
            for t in range(NT):
                g = tile_gen(s * NTOK, t, main, par)
                extra = None
                if t == NT - 1 and s + 1 < nsup:
                    def extra(s=s):
                        st["xns"] = phaseA_pre(st["xts"])
                        st["xts"] = x_loads(src_d, (s + 2) * NTOK) if s + 2 < nsup else None
                slot((lambda g: lambda: next(g))(g), 1 if t < NT - 1 else 99, extra, wmain=(2 if t < NT - 1 else 1))
                pendq.append([g, 2])

    def mask_gen():
        op("gpsimd", lambda e: e.tensor_scalar(out=Sg[:], in0=Sg[:], scalar1=flag[:, 0:1], scalar2=0.0, op0=ALU.mult, op1=ALU.add), r=["Sg", "flag"], w=["Sg"])
        op("gpsimd", lambda e: e.tensor_scalar(out=Ss[:], in0=Ss[:], scalar1=flag[:, 0:1], scalar2=0.0, op0=ALU.mult, op1=ALU.add),
           r=["Ss0", "Ss1", "flag"], w=["Ss0", "Ss1"])
        op("scalar", lambda e: e.activation(out=Sgb[:], in_=Sg[:], func=AF.Copy), r=["Sg"], w=["Sgb"])
        op("scalar", lambda e: e.activation(out=Ssb[:], in_=Ss[:], func=AF.Copy), r=["Ss0", "Ss1"], w=["Ssb"])
        yield

    if STOP != "setup":
        run_phase(xp_d, TP, False)
    if TP > 0 and STOP not in ("setup", "pre"):
        pendq.append([mask_gen(), 1])
    if STOP not in ("setup", "pre"):
        run_phase(xm_d, TM, True)
    p_last = partner(99)
    if p_last is not None:
        p_last()

    if TM > 0 and STOP in (None, "p2w", "p2s0", "p2s1"):
        bar = sb("bar", [128, 1])
        op("gpsimd", lambda e: e.memset(bar[:], 0.0), w=ALLW + ["bar"])
        BAR = ["bar"]

        def f32view(a, b_):
            return big[:, a:b_].bitcast(F32)
        o_ = 26624
        mixin = [(big[:, o_:o_ + 2048], "mixin0"), (big[:, o_ + 2048:o_ + 4096], "mixin1")]; o_ += 4096
        mixTs = [(big[:, o_:o_ + 2048].rearrange("p (k t) -> p k t", k=16), "mixT0"),
                 (big[:, o_ + 2048:o_ + 4096].rearrange("p (k t) -> p k t", k=16), "mixT1")]; o_ += 4096
        hns = [(big[:, o_:o_ + 1024], "hn0"), (big[:, o_ + 1024:o_ + 2048], "hn1")]; o_ += 2048
        hnTs = [(big[:, o_:o_ + 1024].rearrange("p (k t) -> p k t", k=8), "hnT0"),
                (big[:, o_ + 1024:o_ + 2048].rearrange("p (k t) -> p k t", k=8), "hnT1")]; o_ += 2048
        hbufs = [(f32view(o_, o_ + 2048), "h0"), (f32view(o_ + 2048, o_ + 4096), "h1")]; o_ += 4096
        assert o_ <= 8 * DIN
        tgs = [(xsD_pool.bufs[0][0][:, :], "xsD0"), (xsD_pool.bufs[0][0][:, :], "xsD0")]
        obs = [(xt_pool.bufs[0][0][:, :], "xt0"), (xt_pool.bufs[1][0][:, :], "xt1")]
        xrs = [(mixed_pool.bufs[0][0][:, :].bitcast(F32), "mixed0"), (xsD_pool.bufs[1][0][:, :], "xsD1")]
        pts = [(yt_pool.bufs[0][0][:, 0:256], "ytmp0"), (yt_pool.bufs[1][0][:, 0:256], "ytmp1")]
        pbfs = [(Btm_pool.bufs[0][0][:, :], "Btm0"), (Btm_pool.bufs[1][0][:, :], "Btm1")]
        pTs = [(kd_pool.bufs[0][0][:, 0:256].rearrange("p (k t) -> p k t", k=2), "kd0"),
               (kd_pool.bufs[1][0][:, 0:256].rearrange("p (k t) -> p k t", k=2), "kd1"),
               (scm_pool.bufs[0][0][:, 0:2, :], "scm0")]
        fw_bc = sb("fw_bc", [128, D])
        op("sync", lambda e: e.dma_start(out=fw_bc[:], in_=fin_w_d.partition_broadcast(128)), w=["fw_bc"], dma="c_fw")
        def load_wout():
            for fc in range(16):
                rows = w_out_d[fc * 128:(fc + 1) * 128, :]
                if fc < 8:
                    sc, scn = gnw[:, (fc % 2):(fc % 2) + 1], "gnw"
                else:
                    sc, scn = snw[:, fc - 8:fc - 7], "snw"
                for c0 in (0, 512):
                    load_w(Wout, fc, c0, c0 + 512, rows, sc, scn, "Wout", BAR)

        def load_wg_wpe():
            for kc in range(8):
                for c0 in (0, 512):
                    load_w(Wg, kc, c0, c0 + 512, w_g_d[kc * 128:(kc + 1) * 128, :], pnw[:, kc:kc + 1], "pnw", "Wg", BAR)
            for c in range(2):
                for c0 in (0, 512):
                    load_w(Wpe, c, c0, c0 + 512, w_pe_d[c * 128:(c + 1) * 128, :], None, None, "Wpe", BAR)
        p2_loaders = [load_wg_wpe, load_wout]

        def p2_gen(ti):
            r0 = ti * 128
            (mi, min_) = mixin[ti % 2]
            (mixT, mixTn) = mixTs[ti % 2]
            (xr, xrn) = xrs[ti % 2]
            (pt, ptn) = pts[ti % 2]
            (pbf, pbfn) = pbfs[ti % 2]
            (pT, pTn) = pTs[ti % 3]
            (hb_, hn_) = hbufs[ti % 2]
            (hn, hnn) = hns[ti % 2]
            (hnT, hnTn) = hnTs[ti % 2]
            (tg, tgn0) = tgs[ti % 2]
            (ob, obn) = obs[ti % 2]
            op("sync", lambda e: e.dma_start(out=mi, in_=scr_d[r0:r0 + 128, :]), r=["scr%d" % ti] + BAR, w=[min_], dma="mi_" + min_)
            xr_w = [xrn] + (["mixed0g", "mixed0s0", "mixed0s1"] if xrn == "mixed0" else [])
            op("sync", lambda e: e.dma_start(out=xr, in_=xm_d[r0:r0 + 128, :]), w=xr_w, dma="x2_" + xrn)
            op("sync", lambda e: e.dma_start(out=pt, in_=pm_d[r0:r0 + 128, :]), w=[ptn], dma="p_" + ptn)
            transposes([mi[:, fc * 128:(fc + 1) * 128] for fc in range(8)], [min_], mixT[:, 0:8, :], mixTn + "a", "scalar")
            transposes([mi[:, fc * 128:(fc + 1) * 128] for fc in range(8, 16)], [min_], mixT[:, 8:16, :], mixTn + "b", "vector")
            op("gpsimd", lambda e: e.tensor_copy(out=pbf, in_=pt), r=[ptn], w=[pbfn])
            transposes([pbf[:, c * 128:(c + 1) * 128] for c in range(2)], [pbfn], pT, pTn, "vector")
            yield
            if STOP == "p2s0":
                return
            for half in range(2):
                (ph, phn) = psb()

                def fh(e, ph=ph, half=half):
                    last = None
                    for fc in range(16):
                        last = e.matmul(out=ph[:, :], lhsT=mixT[:, fc, :], rhs=Wout[:, fc, half * 512:(half + 1) * 512], start=(fc == 0), stop=(fc == 15))
                    return last
                op("tensor", fh, r=[mixTn + "a", mixTn + "b", "Wout"], w=[phn])
                op("vector", (lambda ph, half: lambda e: e.tensor_tensor(out=hb_[:, half * 512:(half + 1) * 512], in0=ph[:, :],
                                                                          in1=xr[:, half * 512:(half + 1) * 512], op=ALU.add))(ph, half),
                   r=[phn, xrn] + BAR, w=[hn_ + "_%d" % half])
            (ssqh, ssqhn) = small.get()
            op("scalar", lambda e: e.activation(out=hn, in_=hb_, func=AF.Square, accum_out=ssqh[:, 0:1]),
               r=[hn_ + "_0", hn_ + "_1"] + BAR, w=[ssqhn, hnn])
            rh, rhn = rstd_from(ssqh[:, 0:1], ssqhn, 1, 1.0 / D)
            op("scalar", lambda e: e.activation(out=hn, in_=hb_, func=AF.Copy, scale=rh[:, 0:1]),
               r=[hn_ + "_0", hn_ + "_1", rhn] + BAR, w=[hnn])
            transposes([hn[:, k * 128:(k + 1) * 128] for k in range(8)], [hnn], hnT, hnTn, "vector")
            yield
            if STOP == "p2s1":
                return
            (ssqf, ssqfn) = small.get()
            for half in range(2):
                (pgt, pgtn) = psb()

                def fg(e, pgt=pgt, half=half):
                    last = None
                    for kc in range(8):
                        last = e.matmul(out=pgt[:, :], lhsT=hnT[:, kc, :], rhs=Wg[:, kc, half * 512:(half + 1) * 512], start=(kc == 0), stop=(kc == 7))
                    return last
                op("tensor", fg, r=[hnTn, "Wg"], w=[pgtn])
                (ppe, ppen) = psb()

                def fpe(e, ppe=ppe, half=half):
                    last = None
                    for c in range(2):
                        last = e.matmul(out=ppe[:, :], lhsT=pT[:, c, :], rhs=Wpe[:, c, half * 512:(half + 1) * 512], start=(c == 0), stop=(c == 1))
                    return last
                op("tensor", fpe, r=[pTn, "Wpe"], w=[ppen])
                hs = slice(half * 512, (half + 1) * 512)
                tgn = tgn0 + "_%d" % half
                op("scalar", (lambda pgt, hs: lambda e: e.activation(out=tg[:, hs], in_=pgt[:, :], func=AF.Tanh, scale=0.5))(pgt, hs), r=[pgtn], w=[tgn0, tgn])
                op("vector", (lambda ppe, hs: lambda e: e.scalar_tensor_tensor(out=tg[:, hs], in0=tg[:, hs], scalar=1.0, in1=ppe[:, :], op0=ALU.add, op1=ALU.mult))(ppe, hs),
                   r=[ppen, tgn], w=[tgn])
                op("vector", (lambda hs, half: lambda e: e.scalar_tensor_tensor(out=hb_[:, hs], in0=tg[:, hs], scalar=0.5, in1=hb_[:, hs], op0=ALU.mult, op1=ALU.add))(hs, half),
                   r=[tgn, hn_ + "_%d" % half], w=[hn_ + "_%d" % half])
            op("scalar", lambda e: e.activation(out=ob, in_=hb_, func=AF.Square, accum_out=ssqf[:, 0:1]),
               r=[hn_ + "_0", hn_ + "_1"], w=[ssqfn, obn])
            rf, rfn = rstd_from(ssqf[:, 0:1], ssqfn, 1, 1.0 / D)
            op("vector", lambda e: e.scalar_tensor_tensor(out=ob, in0=hb_, scalar=rf[:, 0:1], in1=fw_bc[:], op0=ALU.mult, op1=ALU.mult),
               r=[hn_ + "_0", hn_ + "_1", rfn, "fw_bc"], w=[obn])
            op("sync", lambda e: e.dma_start(out=out_d[r0:r0 + 128, :], in_=ob), r=[obn], dma="o_" + obn)
            yield

        active = []
        ntile = TM // 128 if STOP != "p2w" else 0
        nxt = 0
        stage_banks = {0: (0, 1), 1: (2, 3, 4), 2: (5, 6, 7)}
        while nxt < ntile or active:
            fns = []
            bks = []
            for a in active:
                fns.append((lambda g: lambda: next(g, None))(a[0]))
                bks.append(stage_banks[a[1]])
            if nxt < ntile:
                a = [p2_gen(nxt), 0]
                nxt += 1
                active.append(a)
                fns.append((lambda g: lambda: next(g, None))(a[0]))
                bks.append(stage_banks[0])
            if p2_loaders:
                fns.append(p2_loaders.pop())
                bks.append(None)
            interleave(fns, bks, "p2")
            for a in active:
                a[1] += 1
            active = [a for a in active if a[1] < 3]

    S.emit(final_waits=[k for k in S.dma_streams if k.startswith("o_")])
    return nc


WKEYS = ["norm_w", "w_in", "gla_gate_up", "gla_gate_b", "gla_norm_w", "conv_w", "conv_b", "dt_bias", "a_log",
         "d_skip", "ssd_norm_w", "w_out", "w_pe", "w_pe_gate", "pe_norm_w"]


def run_layer(x, p, w, final_norm_w):
    B, T, _ = x.shape
    half = T // 2
    nc = build_program(half, half)
    wmap = {k: np.ascontiguousarray(np.asarray(w[k], dtype=np.float32)) for k in WKEYS}
    wmap["final_norm_w"] = np.ascontiguousarray(np.asarray(final_norm_w, dtype=np.float32))
    in_maps = []
    for b in range(B):
        for s in range(2):
            m = dict(wmap)
            m["xm"] = np.ascontiguousarray(x[b, s * half:(s + 1) * half])
            m["xp"] = np.ascontiguousarray(x[b, 0:half]) if s == 1 else np.zeros((half, D), np.float32)
            m["pm"] = np.ascontiguousarray(p[b, s * half:(s + 1) * half])
            m["flag"] = np.full((128, 1), float(s), np.float32)
            in_maps.append(m)
    ncores = 2 * B
    res = run_bass_kernel_spmd(nc, in_maps, core_ids=list(range(ncores)))
    out = np.empty((B, T, D), np.float32)
    for b in range(B):
        for s in range(2):
            out[b, s * half:(s + 1) * half] = res.results[2 * b + s]["out"]
    return out


def kernel(**inputs):
    x = np.asarray(inputs["x"], dtype=np.float32)
    p = np.asarray(inputs["p"], dtype=np.float32)
    w = {k: np.asarray(inputs[k])[0] for k in WKEYS}
    return run_layer(x, p[0], w, inputs["final_norm_w"])
```

```python
import threading
import numpy as np
import concourse.bass as bass
import concourse.mybir as mybir
from concourse.bass_utils import run_bass_kernel_spmd

F32 = mybir.dt.float32
BF16 = mybir.dt.bfloat16
AF = mybir.ActivationFunctionType
ALU = mybir.AluOpType

ENGS = ["tensor", "vector", "scalar", "gpsimd", "sync"]

D = 1024
DIN = 5664
C_Q, C_K, C_V, C_G, C_GLR, C_Z, C_XBC, C_DT = 0, 512, 1024, 2048, 3072, 3088, 4112, 5648
EPS = 1e-6
STOP = None
NT = 2
NTOK = NT * 128


class Sched:
    def __init__(self, nc):
        self.nc = nc
        self.ops = []
        self.last_writer = {}
        self.readers = {}
        self.eng_count = {e: 0 for e in ENGS}
        self.dma_streams = {}

    def _chk(self, names, is_write=False):
        out = []
        for b in names:
            if "#" in b:
                base, rest = b.split("#", 1)
                i = 0
                while i < len(rest) and rest[i].isdigit():
                    i += 1
                gen, suf = int(rest[:i]), rest[i:]
                assert GEN.get(base, 0) == gen, "stale buffer use: %s (current gen %d)" % (b, GEN.get(base, 0))
                out.append(base + suf)
            else:
                out.append(b)
        return out

    def op(self, eng, fn, r=(), w=(), dma=None, ndma=1):
        r = self._chk(r)
        w = self._chk(w)
        idx = len(self.ops)
        deps = set()
        for b in r:
            if b in self.last_writer:
                deps.add(self.last_writer[b])
        for b in w:
            if b in self.last_writer:
                deps.add(self.last_writer[b])
            for q in self.readers.get(b, ()):
                deps.add(q)
        deps.discard(idx)
        o = dict(idx=idx, eng=eng, fn=fn, deps=sorted(deps), dma=dma, ndma=ndma)
        if dma is None:
            self.eng_count[eng] += 1
            o["seq"] = self.eng_count[eng]
        else:
            c = self.dma_streams.get(dma, 0) + ndma
            self.dma_streams[dma] = c
            o["seq"] = c * 16
        self.ops.append(o)
        for b in r:
            self.readers.setdefault(b, []).append(idx)
        for b in w:
            self.last_writer[b] = idx
            self.readers[b] = []
        if CUR[0] is not None:
            CUR[0].pause()
        return idx

    def emit(self, final_waits=()):
        nc = self.nc
        sems = {e: nc.alloc_semaphore("sem_" + e) for e in ENGS}
        dsems = {s: nc.alloc_semaphore("dsem_" + s) for s in self.dma_streams}
        ops = self.ops
        per_eng = {e: [o for o in ops if o["eng"] == e] for e in ENGS}

        def sem_of(o):
            return dsems[o["dma"]] if o["dma"] is not None else sems[o["eng"]]

        def key_of(o):
            return ("d", o["dma"]) if o["dma"] is not None else ("e", o["eng"])

        def run(eng_name, eng):
            waited = {}
            if eng_name != "gpsimd" and per_eng["gpsimd"]:
                eng.wait_ge(sems["gpsimd"], 1)
                waited[("e", "gpsimd")] = 1
            for o in per_eng[eng_name]:
                need = {}
                for d in o["deps"]:
                    p = ops[d]
                    k = key_of(p)
                    if eng_name == "tensor" and k == ("e", "tensor"):
                        continue
                    if need.get(k, (0, None))[0] < p["seq"]:
                        need[k] = (p["seq"], sem_of(p))
                for k, (v, s) in need.items():
                    if waited.get(k, 0) >= v:
                        continue
                    eng.wait_ge(s, v)
                    waited[k] = v
                res = o["fn"](eng)
                if o["dma"] is not None:
                    rs = res if isinstance(res, (list, tuple)) else [res]
                    assert len(rs) == o["ndma"], (len(rs), o["ndma"])
                    for ins in rs:
                        ins.then_inc(dsems[o["dma"]], 16)
                else:
                    res.then_inc(sems[eng_name], 1)
            if eng_name == "sync":
                for st in final_waits:
                    eng.wait_ge(dsems[st], self.dma_streams[st] * 16)

        with nc.Block() as block:
            @block.tensor
            def _(e):
                run("tensor", e)

            @block.vector
            def _(e):
                run("vector", e)

            @block.scalar
            def _(e):
                run("scalar", e)

            @block.gpsimd
            def _(e):
                run("gpsimd", e)

            @block.sync
            def _(e):
                run("sync", e)


GEN = {}
CUR = [None]


class Stream:
    def __init__(self, fn, banks=None):
        self.fn = fn
        self.banks = banks
        self.go = threading.Semaphore(0)
        self.done = threading.Semaphore(0)
        self.finished = False
        self.exc = None
        self.th = threading.Thread(target=self._run, daemon=True)
        self.started = False

    def _run(self):
        self.go.acquire()
        try:
            self.fn()
        except BaseException as e:
            self.exc = e
        self.finished = True
        self.done.release()

    def step(self):
        if not self.started:
            self.started = True
            self.th.start()
        CUR[0] = self
        self.go.release()
        self.done.acquire()
        CUR[0] = None
        if self.exc is not None:
            raise self.exc

    def pause(self):
        self.done.release()
        self.go.acquire()


ILV = {"AB": True, "tile": True, "p2": True}


def interleave(fns, banks=None, kind="tile", weights=None):
    if not ILV[kind]:
        for f in fns:
            f()
        return
    sts = [Stream(f, banks[i] if banks else None) for i, f in enumerate(fns)]
    while any(not st.finished for st in sts):
        for i, st in enumerate(sts):
            for _ in range(weights[i] if weights else 1):
                if not st.finished:
                    st.step()


class Rot:
    def __init__(self, nc, name, n, shape, dt):
        self.bufs = [(nc.alloc_sbuf_tensor("%s%d" % (name, i), shape, dt), "%s%d" % (name, i)) for i in range(n)]
        self.i = 0

    def get(self):
        (t, n) = self.bufs[self.i % len(self.bufs)]
        self.i += 1
        GEN[n] = GEN.get(n, 0) + 1
        return (t, "%s#%d" % (n, GEN[n]))


def build_program(TP, TM):
    GEN.clear()
    nc = bass.Bass("TRN2", target_bir_lowering=False)
    S = Sched(nc)
    op = S.op

    def din(name, shape):
        return nc.dram_tensor(name, shape, F32, kind="ExternalInput").ap()

    xp_d = din("xp", [TP, D])
    xm_d = din("xm", [TM, D])
    pm_d = din("pm", [TM, 256])
    flag_d = din("flag", [128, 1])
    norm_w_d = din("norm_w", [D])
    w_in_d = din("w_in", [D, DIN])
    up_d = din("gla_gate_up", [16, 512])
    gate_b_d = din("gla_gate_b", [512])
    gla_nw_d = din("gla_norm_w", [256])
    conv_w_d = din("conv_w", [4, 1536])
    conv_b_d = din("conv_b", [1536])
    dt_bias_d = din("dt_bias", [16])
    a_log_d = din("a_log", [16])
    d_skip_d = din("d_skip", [16])
    ssd_nw_d = din("ssd_norm_w", [1024])
    w_out_d = din("w_out", [2048, D])
    w_pe_d = din("w_pe", [256, D])
    w_g_d = din("w_pe_gate", [D, D])
    pe_nw_d = din("pe_norm_w", [D])
    fin_w_d = din("final_norm_w", [D])
    out_d = nc.dram_tensor("out", [TM, D], F32, kind="ExternalOutput").ap()

    def sb(name, shape, dt=F32):
        return nc.alloc_sbuf_tensor(name, shape, dt)

    big = sb("big", [128, 8 * DIN], BF16)
    op("gpsimd", lambda e: e.memset(big[:, 0:8192], 0.0), w=["startgate"])

    ps = nc.alloc_psum_tensor("ps", [128, 8 * 512], F32)
    ps_i = [0]

    ps_l = [0]
    ps_ctr = {}

    def psb(long=False):
        cur = CUR[0]
        if long and not (cur is not None and cur.banks is not None and cur.banks[-1] >= 6):
            k = 6 + ps_l[0] % 2
            ps_l[0] += 1
        else:
            bs = tuple(cur.banks) if (cur is not None and cur.banks is not None) else (0, 1, 2, 3, 4, 5)
            c = ps_ctr.get(bs, 0)
            ps_ctr[bs] = c + 1
            k = bs[c % len(bs)]
        n = "ps%d" % k
        GEN[n] = GEN.get(n, 0) + 1
        return ps[:, k * 512:(k + 1) * 512], "%s#%d" % (n, GEN[n])

    ones_f = sb("ones_f", [128, 128])
    identf = sb("identf", [128, 128])
    ident = sb("ident", [128, 128], BF16)
    L_f = sb("L_f", [128, 128])
    SU_f = sb("SU_f", [128, 128])
    SU_b = sb("SU_b", [128, 128], BF16)
    nhalf = sb("nhalf", [128, 4])
    epsc = sb("epsc", [128, 1])
    rmask = sb("rmask", [128, NTOK])
    NV = 90
    vrows = sb("vrows", [NV, 128])
    vecs = sb("vecs", [128, NV])
    nw = vecs[:, 0:8]
    pnw = vecs[:, 8:16]
    snw = vecs[:, 16:24]
    gnw = vecs[:, 24:26]
    nb = vecs[:, 26:30]
    cb = vecs[:, 30:42]
    cw = vecs[:, 42:90].rearrange("p (j c) -> p j c", j=4)
    dtb_bc = sb("dtb_bc", [128, 16])
    aneg_bc = sb("aneg_bc", [128, 16])
    dsk_bc = sb("dsk_bc", [128, 16])
    flag = sb("flag_sb", [128, 1])
    up_b = sb("up_b", [16, 512], BF16)

    op("gpsimd", lambda e: e.memset(ones_f[:], 1.0), w=["ones_f"])
    op("gpsimd", lambda e: e.memset(nhalf[:], -0.5), w=["nhalf"])
    op("gpsimd", lambda e: e.memset(epsc[:], EPS), w=["epsc"])
    op("gpsimd", lambda e: e.affine_select(out=identf[:], in_=ones_f[:], pattern=[[-1, 128]], compare_op=ALU.is_equal,
                                           fill=0.0, base=0, channel_multiplier=1), r=["ones_f"], w=["identf"])
    op("gpsimd", lambda e: e.affine_select(out=L_f[:], in_=ones_f[:], pattern=[[1, 128]], compare_op=ALU.is_ge,
                                           fill=0.0, base=0, channel_multiplier=-1), r=["ones_f"], w=["L_f"])
    op("gpsimd", lambda e: e.affine_select(out=SU_f[:], in_=ones_f[:], pattern=[[-1, 128]], compare_op=ALU.is_gt,
                                           fill=0.0, base=0, channel_multiplier=1), r=["ones_f"], w=["SU_f"])
    op("vector", lambda e: e.tensor_copy(out=ident[:], in_=identf[:]), r=["identf"], w=["ident"])
    op("vector", lambda e: e.tensor_copy(out=SU_b[:], in_=SU_f[:]), r=["SU_f"], w=["SU_b"])
    op("gpsimd", lambda e: e.memset(rmask[:], 1.0), w=["rmask"])
    op("gpsimd", lambda e: e.memset(rmask[:].rearrange("p (t i) -> p t i", i=128)[:, :, 0], 0.0), w=["rmask"])

    def small_dma(dst, src, name, n=1):
        op("sync", lambda e: e.dma_start(out=dst, in_=src), w=[name], dma="c_" + name)

    def rows_dma(r0, src, nrow, tag):
        op("sync", lambda e: e.dma_start(out=vrows[r0:r0 + nrow, :], in_=src.rearrange("(k p) -> k p", p=128)), w=["vrows_" + tag], dma="c_" + tag)
    rows_dma(0, norm_w_d, 8, "nw")
    rows_dma(8, pe_nw_d, 8, "pnw")
    rows_dma(16, ssd_nw_d, 8, "snw")
    rows_dma(24, gla_nw_d, 2, "gnw")
    rows_dma(26, gate_b_d, 4, "nb")
    rows_dma(30, conv_b_d, 12, "cb")
    for j in range(4):
        rows_dma(42 + 12 * j, conv_w_d[j], 12, "cw%d" % j)
    VR = ["vrows_" + t for t in ("nw", "pnw", "snw", "gnw", "nb", "cb", "cw0", "cw1", "cw2", "cw3")]
    (pvv, pvvn) = psb()
    op("tensor", lambda e: e.matmul(out=pvv[:, 0:NV], lhsT=vrows[:, :], rhs=identf[0:NV, 0:NV], start=True, stop=True), r=VR + ["identf"], w=[pvvn])
    VN = ["nw", "pnw", "snw", "gnw", "nb", "cb", "cw"]
    op("vector", lambda e: e.tensor_copy(out=vecs[:, :], in_=pvv[:, 0:NV]), r=[pvvn], w=VN)
    small_dma(dtb_bc[:], dt_bias_d.partition_broadcast(128), "dtb_bc")
    small_dma(aneg_bc[:], a_log_d.partition_broadcast(128), "aneg_bc")
    small_dma(dsk_bc[:], d_skip_d.partition_broadcast(128), "dsk_bc")
    small_dma(flag[:], flag_d[:, :], "flag")
    op("gpsimd", lambda e: e.tensor_scalar(out=nb, in0=nb, scalar1=-1.0, scalar2=None, op0=ALU.mult), r=["nb"], w=["nb"])
    op("scalar", lambda e: e.activation(out=aneg_bc[:], in_=aneg_bc[:], func=AF.Exp), r=["aneg_bc"], w=["aneg_bc"])
    op("gpsimd", lambda e: e.tensor_scalar(out=aneg_bc[:], in0=aneg_bc[:], scalar1=-1.0, scalar2=None, op0=ALU.mult),
       r=["aneg_bc"], w=["aneg_bc"])


    Win = big[:, :].rearrange("p (k c) -> p k c", k=8)
    Wout = big[:, 0:16384].rearrange("p (k c) -> p k c", k=16)
    Wg = big[:, 16384:24576].rearrange("p (k c) -> p k c", k=8)
    Wpe = big[:, 24576:26624].rearrange("p (k c) -> p k c", k=2)

    Sg = sb("Sg", [128, 4, 256])
    Sgb = sb("Sgb", [128, 4, 256], BF16)
    Ss = sb("Ss", [128, 2, 512])
    Ssb = sb("Ssb", [128, 2, 512], BF16)
    halo = sb("halo", [128, 12, 3])
    for t_, n_ in ((Sg, ["Sg"]), (Sgb, ["Sgb"]), (Ss, ["Ss0", "Ss1"]), (Ssb, ["Ssb"]), (halo, ["halo"])):
        op("gpsimd", (lambda t_: (lambda e: e.memset(t_[:], 0.0)))(t_), w=n_)

    xt_pool = Rot(nc, "xt", 2, [128, D], F32)
    xn_pool = Rot(nc, "xn", NT, [128, D], BF16)
    uT = sb("uT", [128, 8, NTOK], BF16)
    glr = sb("glr", [16, NTOK], BF16)
    qdec2 = [sb("qdec%d" % i, [128, 4, NTOK], BF16) for i in range(2)]
    kinv = sb("kinv", [128, 4, NTOK], BF16)
    kdT = sb("kdT", [128, 4, NTOK], BF16)
    decg2 = [sb("decg%d" % i, [128, 4, NT]) for i in range(2)]
    xc = sb("xc", [128, 10, NTOK], BF16)
    xcC2 = [sb("xcC%d" % i, [128, 2, NTOK], BF16) for i in range(2)]
    gtmp = Rot(nc, "gtmp", 6, [128, NTOK], F32)
    ncl_pool = Rot(nc, "ncl", 2, [128, NT], F32)
    xpre_pool = Rot(nc, "xpre", 4, [128, NTOK + 3], F32)
    small = Rot(nc, "sm", 20, [128, 16], F32)
    V_pool = Rot(nc, "V", 2, [128, 1024], BF16)
    sg_pool = Rot(nc, "sg", 2, [128, 1024], BF16)
    sz_pool = Rot(nc, "sz", 2, [128, 1024], BF16)
    scm_pool = Rot(nc, "scm", 2, [128, 4, 128], BF16)
    kd_pool = Rot(nc, "kd", 2, [128, 512], BF16)
    xdt_pool = Rot(nc, "xdt", 2, [128, 1024], BF16)
    xdd_pool = Rot(nc, "xdd", 2, [128, 1024], BF16)
    xsD_pool = Rot(nc, "xsD", 2, [128, 1024], F32)
    Btm_pool = Rot(nc, "Btm", 2, [128, 256], BF16)
    ex_pool = Rot(nc, "ex", 2, [128, 48], F32)
    rhs2_pool = Rot(nc, "rhs2", 1, [128, 16, 128], BF16)
    dec_pool = Rot(nc, "dec", 2, [128, 8, 128], BF16)
    cbm_pool = Rot(nc, "cbm", 2, [128, 2, 128], BF16)
    MT_pool = Rot(nc, "MT", 2, [128, 8, 128], BF16)
    yt_pool = Rot(nc, "ytmp", 2, [128, 512], F32)
    mixed_pool = Rot(nc, "mixed", 1, [128, 2048], BF16)
    scr_d = nc.dram_tensor("mix_scratch", [max(TM, 128), 2048], BF16, kind="Internal").ap()

    stage_bufs = [(xdt_pool.bufs[0][0][:, :].bitcast(F32), "xdt0"), (xdt_pool.bufs[1][0][:, :].bitcast(F32), "xdt1"),
                  (xdd_pool.bufs[0][0][:, :].bitcast(F32), "xdd0"), (xdd_pool.bufs[1][0][:, :].bitcast(F32), "xdd1"),
                  (V_pool.bufs[0][0][:, :].bitcast(F32), "V0"), (V_pool.bufs[1][0][:, :].bitcast(F32), "V1")]
    stg_i = [0]
    cast_engs = ["vector", "gpsimd", "scalar", "vector", "scalar"]
    ci = [0]

    def load_w(dst3, kc, c0, c1, src_rows, scale_ap, scale_name, wname, extra_r=()):
        (st, stn) = stage_bufs[stg_i[0] % len(stage_bufs)]
        stg_i[0] += 1
        n = c1 - c0
        op("sync", lambda e: e.dma_start(out=st[:, 0:n], in_=src_rows[:, c0:c1]), w=[stn], dma="w_" + stn)
        eng = cast_engs[ci[0] % len(cast_engs)]
        ci[0] += 1
        rr = [stn] + list(extra_r) + ([scale_name] if scale_name else [])
        if eng == "scalar":
            if scale_ap is None:
                f = lambda e: e.activation(out=dst3[:, kc, c0:c1], in_=st[:, 0:n], func=AF.Copy)
            else:
                f = lambda e: e.activation(out=dst3[:, kc, c0:c1], in_=st[:, 0:n], func=AF.Copy, scale=scale_ap)
        else:
            if scale_ap is None:
                f = lambda e: e.tensor_copy(out=dst3[:, kc, c0:c1], in_=st[:, 0:n])
            elif eng == "gpsimd":
                f = lambda e: e.tensor_scalar(out=dst3[:, kc, c0:c1], in0=st[:, 0:n], scalar1=scale_ap, scalar2=0.0, op0=ALU.mult, op1=ALU.add)
            else:
                f = lambda e: e.tensor_scalar(out=dst3[:, kc, c0:c1], in0=st[:, 0:n], scalar1=scale_ap, scalar2=None, op0=ALU.mult)
        op(eng, f, r=rr, w=[wname])

    upst = xsD_pool.bufs[0][0]
    op("sync", lambda e: e.dma_start(out=upst[0:16, 0:512], in_=up_d[:, :]), w=["xsD0"], dma="c_up")
    op("vector", lambda e: e.tensor_copy(out=up_b[:], in_=upst[0:16, 0:512]), r=["xsD0"], w=["up_b"])

    def wn(c0, n):
        return ["W%d" % p for p in range(c0 // 512, (c0 + n - 1) // 512 + 1)]
    ALLW = ["W%d" % p for p in range((DIN + 511) // 512)]
    def load_pieces(pieces):
        for piece in pieces:
            c0 = piece * 512
            for kc in range(8):
                load_w(Win, kc, c0, min(DIN, c0 + 512), w_in_d[kc * 128:(kc + 1) * 128, :], nw[:, kc:kc + 1], "nw", "W%d" % piece)
    load_pieces((6, 0, 1, 8, 9, 10, 11))
    late_pieces = [(2, 3, 4, 5, 7)]

    def rstd_from(ssq_ap, ssq_name, n, inv_count):
        (t1, t1n) = small.get()
        (t2, t2n) = small.get()
        op("gpsimd", lambda e: e.tensor_scalar(out=t1[:, 0:n], in0=ssq_ap, scalar1=inv_count, scalar2=EPS, op0=ALU.mult, op1=ALU.add),
           r=[ssq_name], w=[t1n])
        op("gpsimd", lambda e: e.tensor_tensor(out=t2[:, 0:n], in0=t1[:, 0:n], in1=nhalf[:, 0:n], op=ALU.pow), r=[t1n, "nhalf"], w=[t2n])
        return t2, t2n

    def transposes(src_aps, src_names, dst_ap, dst_name, copy_eng, extra_r=()):
        n = len(src_aps)
        (pb_, pbn) = psb()
        pv = pb_.bitcast(BF16)

        def f(e):
            last = None
            for i, a in enumerate(src_aps):
                last = e.transpose(out=pv[:, i * 128:(i + 1) * 128], in_=a, identity=ident[:])
            return last
        op("tensor", f, r=list(src_names) + ["ident"], w=[pbn])
        src = pv[:, 0:n * 128]
        if dst_ap.ndim == 3:
            src = src.rearrange("p (k t) -> p k t", k=n)
        if copy_eng == "scalar":
            op("scalar", lambda e: e.activation(out=dst_ap, in_=src, func=AF.Copy), r=[pbn] + list(extra_r), w=[dst_name])
        else:
            op("vector", lambda e: e.tensor_copy(out=dst_ap, in_=src), r=[pbn] + list(extra_r), w=[dst_name])

    def inproj_fm(c0, m):
        (pb_, pbn) = psb()

        def f(e):
            last = None
            for kc in range(8):
                last = e.matmul(out=pb_[0:m, 0:NTOK], lhsT=Win[:, kc, c0:c0 + m], rhs=uT[:, kc, :], start=(kc == 0), stop=(kc == 7))
            return last
        op("tensor", f, r=wn(c0, m) + ["uT"], w=[pbn])
        return pb_, pbn

    def inproj_tm(t, c0, n):
        (pb_, pbn) = psb()

        def f(e):
            last = None
            for kc in range(8):
                last = e.matmul(out=pb_[:, 0:n], lhsT=uT[:, kc, t * 128:(t + 1) * 128], rhs=Win[:, kc, c0:c0 + n], start=(kc == 0), stop=(kc == 7))
            return last
        op("tensor", f, r=wn(c0, n) + ["uT"], w=[pbn])
        return pb_, pbn

    def x_loads(src_d, tok0):
        xts = []
        for t in range(NT):
            (xt, xtn) = xt_pool.get()
            r0 = tok0 + t * 128
            op("sync", (lambda xt, r0: lambda e: e.dma_start(out=xt[:], in_=src_d[r0:r0 + 128, :]))(xt, r0), w=[xtn], dma="x_" + xtn.split("#")[0])
            xts.append((xt, xtn))
        return xts

    def phaseA_pre(xts):
        tl = []
        for t in range(NT):
            (xt, xtn) = xts[t]
            (ssq, ssqn) = small.get()
            (xn, xnn) = xn_pool.get()
            op("scalar", (lambda xt, ssq, xn: lambda e: e.activation(out=xn[:], in_=xt[:], func=AF.Square, accum_out=ssq[:, 0:1]))(xt, ssq, xn),
               r=[xtn], w=[ssqn, xnn])
            tl.append((xt, xtn, ssq, ssqn, xn, xnn))
        rl = [rstd_from(ssq[:, 0:1], ssqn, 1, 1.0 / D) for (xt, xtn, ssq, ssqn, xn, xnn) in tl]
        xns = []
        for (xt, xtn, ssq, ssqn, xn, xnn), (rs, rsn) in zip(tl, rl):
            op("vector", (lambda xt, xn, rs: lambda e: e.tensor_scalar(out=xn[:], in0=xt[:], scalar1=rs[:, 0:1], scalar2=None, op0=ALU.mult))(xt, xn, rs),
               r=[xtn, rsn], w=[xnn])
            xns.append((xn, xnn))
        return xns

    def phaseAB(xns, main, par, all_chunks):
        qdec = qdec2[par]; qn = "qdec%d" % par
        decg = decg2[par]; dgn = "decg%d" % par
        xcC = xcC2[par]; xcn = "xcC%d" % par
        for t in range(NT):
            (xn, xnn) = xns[t]
            transposes([xn[:, k * 128:(k + 1) * 128] for k in range(8)], [xnn], uT[:, :, t * 128:(t + 1) * 128], "uT", "vector")

        pb_, pbn = inproj_fm(C_GLR, 16)
        op("scalar", lambda e: e.activation(out=glr[:, :], in_=pb_[0:16, 0:NTOK], func=AF.Copy), r=[pbn], w=["glr"])
        for hp in range(2):
            hs = (2 * hp, 2 * hp + 1)
            H = {}
            for h in hs:
                (pz, pzn) = psb()
                op("tensor", (lambda pz, h: lambda e: e.matmul(out=pz[:, 0:NTOK], lhsT=up_b[:, h * 128:(h + 1) * 128], rhs=glr[:, :], start=True, stop=True))(pz, h),
                   r=["up_b", "glr"], w=[pzn])
                H[h] = dict(pz=pz, pzn=pzn)
            for h in hs:
                d = H[h]
                (d["l"], d["ln"]) = gtmp.get()
                op("scalar", (lambda pz, l, h: lambda e: e.activation(out=l[:], in_=pz[:, 0:NTOK], func=AF.Exp, scale=-1.0, bias=nb[:, h:h + 1]))(d["pz"], d["l"], h),
                   r=[d["pzn"], "nb"], w=[d["ln"]])
            for h in hs:
                d = H[h]
                op("scalar", (lambda l: lambda e: e.activation(out=l[:], in_=l[:], func=AF.Ln, bias=1.0))(d["l"]), r=[d["ln"]], w=[d["ln"]])
            for h in hs:
                d = H[h]
                (d["cl"], d["cln"]) = gtmp.get()
                op("vector", (lambda l, cl: lambda e: e.tensor_tensor_scan(out=cl[:], data0=rmask[:], data1=l[:], initial=0.0, op0=ALU.mult, op1=ALU.add))(d["l"], d["cl"]),
                   r=[d["ln"], "rmask"], w=[d["cln"]])
            for h in hs:
                d = H[h]
                (d["e1"], d["e1n"]) = gtmp.get()
                op("scalar", (lambda cl, e1: lambda e: e.activation(out=e1[:], in_=cl[:], func=AF.Exp, scale=-1.0 / 16))(d["cl"], d["e1"]), r=[d["cln"]], w=[d["e1n"]])
            for h in hs:
                d = H[h]
                op("gpsimd", (lambda e1, h: lambda e: e.tensor_copy(out=decg[:, h, :], in_=e1[:].rearrange("p (t i) -> p t i", i=128)[:, :, 127]))(d["e1"], h),
                   r=[d["e1n"]], w=[dgn])
                (d["ncl"], d["ncln"]) = ncl_pool.get()
                op("gpsimd", (lambda cl, ncl: lambda e: e.tensor_scalar(out=ncl[:], in0=cl[:].rearrange("p (t i) -> p t i", i=128)[:, :, 127],
                                                                         scalar1=-1.0 / 16, scalar2=0.0, op0=ALU.mult, op1=ALU.add))(d["cl"], d["ncl"]), r=[d["cln"]], w=[d["ncln"]])
            for h in hs:
                d = H[h]
                (xb, xbn) = xpre_pool.get()
                d["ed"], d["edn"] = xb[:, 0:NTOK], [xbn, xbn + "h"]

                def fed(e, cl=d["cl"], ed=d["ed"], ncl=d["ncl"]):
                    last = None
                    for t in range(NT):
                        last = e.activation(out=ed[:, t * 128:(t + 1) * 128], in_=cl[:, t * 128:(t + 1) * 128], func=AF.Exp, scale=1.0 / 16, bias=ncl[:, t:t + 1])
                    return last
                op("scalar", fed, r=[d["cln"], d["ncln"]], w=d["edn"])
            if main:
                for h in hs:
                    d = H[h]
                    (xb, xbn) = xpre_pool.get()
                    d["e2"], d["e2n"] = xb[:, 0:NTOK], [xbn, xbn + "h"]
                    op("scalar", (lambda cl, e2: lambda e: e.activation(out=e2, in_=cl[:], func=AF.Exp, scale=1.0 / 16))(d["cl"], d["e2"]), r=[d["cln"]], w=d["e2n"])
                for h in hs:
                    d = H[h]
                    pq, pqn = inproj_fm(C_Q + h * 128, 128)
                    op("vector", (lambda pq, e1, h: lambda e: e.scalar_tensor_tensor(out=qdec[:, h, :], in0=pq[:, 0:NTOK], scalar=128.0 ** -0.5, in1=e1[:],
                                                                                      op0=ALU.mult, op1=ALU.mult))(pq, d["e1"], h), r=[pqn, d["e1n"]], w=[qn])
            for h in hs:
                d = H[h]
                pk, pkn = inproj_fm(C_K + h * 128, 128)
                if main:
                    op("vector", (lambda pk, e2, h: lambda e: e.tensor_tensor(out=kinv[:, h, :], in0=pk[:, 0:NTOK], in1=e2, op=ALU.mult))(pk, d["e2"], h),
                       r=[pkn] + d["e2n"], w=["kinv"])
                op("vector", (lambda pk, ed, h: lambda e: e.tensor_tensor(out=kdT[:, h, :], in0=pk[:, 0:NTOK], in1=ed, op=ALU.mult))(pk, d["ed"], h),
                   r=[pkn] + d["edn"], w=["kdT"])

        nchunks = 12 if all_chunks else 10
        pend = []
        for c0 in range(0, nchunks, 2):
            pair = []
            for c in (c0, c0 + 1):
                pc, pcn = inproj_fm(C_XBC + c * 128, 128)
                (xpre, xpren) = xpre_pool.get()
                op("gpsimd", (lambda xpre, c: lambda e: e.tensor_copy(out=xpre[:, 0:3], in_=halo[:, c, :]))(xpre, c), r=["halo"], w=[xpren + "h"])
                op("scalar", (lambda xpre, pc: lambda e: e.activation(out=xpre[:, 3:3 + NTOK], in_=pc[:, 0:NTOK], func=AF.Copy))(xpre, pc), r=[pcn], w=[xpren])
                op("gpsimd", (lambda xpre, c: lambda e: e.tensor_copy(out=halo[:, c, :], in_=xpre[:, NTOK:NTOK + 3]))(xpre, c), r=[xpren, xpren + "h"], w=["halo"])
                (acc, accn) = gtmp.get()
                op("scalar", (lambda pc, acc, c: lambda e: e.activation(out=acc[:], in_=pc[:, 0:NTOK], func=AF.Identity,
                                                                        scale=cw[:, 3, c:c + 1], bias=cb[:, c:c + 1]))(pc, acc, c),
                   r=[pcn, "cw", "cb"], w=[accn])
                pair.append((c, xpre, xpren, acc, accn))
            for j in range(3):
                for (c, xpre, xpren, acc, accn) in pair:
                    op("vector", (lambda xpre, acc, c, j: lambda e: e.scalar_tensor_tensor(out=acc[:], in0=xpre[:, j:j + NTOK], scalar=cw[:, j, c:c + 1], in1=acc[:],
                                                                                            op0=ALU.mult, op1=ALU.add))(xpre, acc, c, j),
                       r=[xpren, xpren + "h", accn, "cw"], w=[accn])
            for f in pend:
                f()
            pend = []
            for (c, xpre, xpren, acc, accn) in pair:
                if c < 10:
                    pend.append((lambda acc, accn, c: lambda: op("scalar", lambda e: e.activation(out=xc[:, c, :], in_=acc[:], func=AF.Silu), r=[accn], w=["xc"]))(acc, accn, c))
                else:
                    pend.append((lambda acc, accn, c: lambda: op("scalar", lambda e: e.activation(out=xcC[:, c - 10, :], in_=acc[:], func=AF.Silu), r=[accn], w=[xcn]))(acc, accn, c))
        for f in pend:
            f()

    def tile_gen(tok0, t, main, par):
        qdec = qdec2[par]; qn = "qdec%d" % par
        decg = decg2[par]; dgn = "decg%d" % par
        xcC = xcC2[par]; xcn = "xcC%d" % par
        tsl = slice(t * 128, (t + 1) * 128)
        if main:
            (sg, sgn) = sg_pool.get()
            (sz, szn) = sz_pool.get()
            for (dst, dstn, c0) in ((sg, sgn, C_G), (sz, szn, C_Z)):
                for half in range(2):
                    pg, pgn = inproj_tm(t, c0 + half * 512, 512)
                    op("scalar", (lambda pg, dst, half: lambda e: e.activation(out=dst[:, half * 512:(half + 1) * 512], in_=pg[:, :], func=AF.Silu))(pg, dst, half),
                       r=[pgn], w=[dstn])
        pd, pdn = inproj_tm(t, C_DT, 16)
        (dtr, dtrn) = small.get()
        op("vector", (lambda pd, dtr: lambda e: e.tensor_tensor(out=dtr[:], in0=pd[:, 0:16], in1=dtb_bc[:], op=ALU.add))(pd, dtr), r=[pdn, "dtb_bc"], w=[dtrn])
        op("scalar", (lambda dtr: lambda e: e.activation(out=dtr[:], in_=dtr[:], func=AF.Exp))(dtr), r=[dtrn], w=[dtrn])
        (dt, dtn) = small.get()
        op("scalar", (lambda dtr, dt: lambda e: e.activation(out=dt[:], in_=dtr[:], func=AF.Ln, bias=1.0))(dtr, dt), r=[dtrn], w=[dtn])
        (dtA, dtAn) = small.get()
        op("gpsimd", (lambda dt, dtA: lambda e: e.tensor_tensor(out=dtA[:], in0=dt[:], in1=aneg_bc[:], op=ALU.mult))(dt, dtA), r=[dtn, "aneg_bc"], w=[dtAn])
        (V, Vn) = V_pool.get()
        for half in range(2):
            pv, pvn = inproj_tm(t, C_V + half * 512, 512)
            if half == 0:
                op("scalar", (lambda pv, V: lambda e: e.activation(out=V[:, 0:512], in_=pv[:, :], func=AF.Copy))(pv, V), r=[pvn], w=[Vn])
            else:
                op("vector", (lambda pv, V: lambda e: e.tensor_copy(out=V[:, 512:1024], in_=pv[:, :]))(pv, V), r=[pvn], w=[Vn])
        (pcm, pcmn) = psb()

        def fcm(e, pcm=pcm, dtA=dtA):
            e.matmul(out=pcm[:, 0:16], lhsT=L_f[:], rhs=dtA[:], start=True, stop=True)
            e.matmul(out=pcm[:, 16:32], lhsT=SU_f[:], rhs=dtA[:], start=True, stop=True)
            return e.matmul(out=pcm[:, 32:48], lhsT=ones_f[:], rhs=dtA[:], start=True, stop=True)
        op("tensor", fcm, r=["L_f", "SU_f", "ones_f", dtAn], w=[pcmn])
        (ex, exn) = ex_pool.get()
        op("scalar", (lambda pcm, ex: lambda e: e.activation(out=ex[:], in_=pcm[:, 0:48], func=AF.Exp))(pcm, ex), r=[pcmn], w=[exn])
        if main:
            (rhs2, rhs2n) = rhs2_pool.get()
            op("gpsimd", (lambda rhs2, dtA: lambda e: e.tensor_tensor(out=rhs2[:], in0=dtA[:].unsqueeze(2).broadcast_to([128, 16, 128]),
                                                                       in1=L_f[:].unsqueeze(1).broadcast_to([128, 16, 128]), op=ALU.mult))(rhs2, dtA),
               r=[dtAn, "L_f"], w=[rhs2n])
        (pxa, pxan) = psb()
        pxav = pxa.bitcast(BF16)

        def ftx(e, pxav=pxav):
            last = None
            for c in range(8):
                last = e.transpose(out=pxav[:, c * 128:(c + 1) * 128], in_=xc[:, c, tsl], identity=ident[:])
            return last
        op("tensor", ftx, r=["xc", "ident"], w=[pxan])
        (xdt, xdtn) = xdt_pool.get()
        op("vector", (lambda pxav, xdt, dt: lambda e: e.tensor_tensor(out=xdt[:].rearrange("p (h q) -> p h q", h=16),
                                                                        in0=pxav[:, :].rearrange("p (h q) -> p h q", h=16),
                                                                        in1=dt[:].unsqueeze(2).broadcast_to([128, 16, 64]), op=ALU.mult))(pxav, xdt, dt),
           r=[pxan, dtn], w=[xdtn])
        if main:
            (xsD, xsDn) = xsD_pool.get()
            op("vector", (lambda pxav, xsD: lambda e: e.tensor_tensor(out=xsD[:].rearrange("p (h q) -> p h q", h=16),
                                                                        in0=pxav[:, :].rearrange("p (h q) -> p h q", h=16),
                                                                        in1=dsk_bc[:].unsqueeze(2).broadcast_to([128, 16, 64]), op=ALU.mult))(pxav, xsD),
               r=[pxan, "dsk_bc"], w=[xsDn])
        (Btm, Btmn) = Btm_pool.get()
        transposes([xc[:, 8 + g, tsl] for g in range(2)], ["xc"], Btm[:, :], Btmn, "vector")
        (xdd, xddn) = xdd_pool.get()
        op("gpsimd", (lambda xdt, xdd, ex: lambda e: e.tensor_tensor(out=xdd[:].rearrange("p (h q) -> p h q", h=16),
                                                                      in0=xdt[:].rearrange("p (h q) -> p h q", h=16),
                                                                      in1=ex[:, 16:32].unsqueeze(2).broadcast_to([128, 16, 64]), op=ALU.mult))(xdt, xdd, ex),
           r=[xdtn, exn], w=[xddn])
        (kd, kdn) = kd_pool.get()
        transposes([kdT[:, h, tsl] for h in range(4)], ["kdT"], kd[:, :], kdn, "vector")
        MTs = []
        if main:
            (psc, pscn) = psb()

            def fsc(e, psc=psc):
                last = None
                for h in range(4):
                    last = e.matmul(out=psc[:, h * 128:(h + 1) * 128], lhsT=kinv[:, h, tsl], rhs=qdec[:, h, tsl], start=True, stop=True)
                return last
            op("tensor", fsc, r=["kinv", qn], w=[pscn])
            (scm, scmn) = scm_pool.get()
            op("vector", (lambda psc, scm: lambda e: e.tensor_tensor(out=scm[:], in0=psc[:, :].rearrange("p (h i) -> p h i", h=4),
                                                                       in1=L_f[:].unsqueeze(1).broadcast_to([128, 4, 128]), op=ALU.mult))(psc, scm),
               r=[pscn, "L_f"], w=[scmn])
            (pcb, pcbn) = psb()

            def fcb(e, pcb=pcb):
                e.matmul(out=pcb[:, 0:128], lhsT=xc[:, 8, tsl], rhs=xcC[:, 0, tsl], start=True, stop=True)
                return e.matmul(out=pcb[:, 128:256], lhsT=xc[:, 9, tsl], rhs=xcC[:, 1, tsl], start=True, stop=True)
            op("tensor", fcb, r=["xc", xcn], w=[pcbn])
            (cbm, cbmn) = cbm_pool.get()
            op("vector", (lambda pcb, cbm: lambda e: e.tensor_tensor(out=cbm[:], in0=pcb[:, 0:256].rearrange("p (g i) -> p g i", g=2),
                                                                       in1=L_f[:].unsqueeze(1).broadcast_to([128, 2, 128]), op=ALU.mult))(pcb, cbm),
               r=[pcbn, "L_f"], w=[cbmn])
            decs = [dec_pool.get() for g in range(2)]
            psgs = []
            for g in range(2):
                for q in range(2):
                    (psg, psgn) = psb()
                    hb = g * 8 + q * 4
                    op("tensor", (lambda psg, rhs2, hb: lambda e: e.matmul(out=psg[:, :], lhsT=SU_b[:], rhs=rhs2[:, hb:hb + 4, :], start=True, stop=True))(psg, rhs2, hb),
                       r=["SU_b", rhs2n], w=[psgn])
                    psgs.append((g, q, psg, psgn))
            for (g, q, psg, psgn) in psgs:
                (dec, decn) = decs[g]
                op("scalar", (lambda psg, dec, q: lambda e: e.activation(out=dec[:, q * 4:(q + 1) * 4, :], in_=psg[:, :].rearrange("p (h i) -> p h i", h=4),
                                                                          func=AF.Exp))(psg, dec, q), r=[psgn], w=[decn])
            for g in range(2):
                (dec, decn) = decs[g]
                (MT, MTn) = MT_pool.get()
                op("vector", (lambda MT, dec, cbm, g: lambda e: e.tensor_tensor(out=MT[:], in0=dec[:], in1=cbm[:, g, :].unsqueeze(1).broadcast_to([128, 8, 128]),
                                                                                 op=ALU.mult))(MT, dec, cbm, g), r=[decn, cbmn], w=[MTn])
                MTs.append((MT, MTn))
        yield
        pos = []
        if main:
            for hp in range(2):
                (po, pon) = psb(long=True)

                def fo(e, po=po, hp=hp):
                    last = None
                    for hh in range(2):
                        h = hp * 2 + hh
                        e.matmul(out=po[:, hh * 256:(hh + 1) * 256], lhsT=scm[:, h, :], rhs=V[:, h * 256:(h + 1) * 256], start=True, stop=False)
                        last = e.matmul(out=po[:, hh * 256:(hh + 1) * 256], lhsT=qdec[:, h, tsl], rhs=Sgb[:, h, :], start=False, stop=True)
                    return last
                op("tensor", fo, r=[scmn, Vn, qn, "Sgb"], w=[pon])
                pos.append((po, pon))
        for hp in range(2):
            (pds, pdsn) = psb()

            def fds(e, pds=pds, hp=hp):
                last = None
                for hh in range(2):
                    h = hp * 2 + hh
                    last = e.matmul(out=pds[:, hh * 256:(hh + 1) * 256], lhsT=kd[:, h * 128:(h + 1) * 128], rhs=V[:, h * 256:(h + 1) * 256], start=True, stop=True)
                return last
            op("tensor", fds, r=[kdn, Vn], w=[pdsn])
            for hh in range(2):
                h = hp * 2 + hh
                op("vector", (lambda pds, h, hh: lambda e: e.scalar_tensor_tensor(out=Sg[:, h, :], in0=Sg[:, h, :], scalar=decg[:, h, t:t + 1],
                                                                                   in1=pds[:, hh * 256:(hh + 1) * 256], op0=ALU.mult, op1=ALU.add))(pds, h, hh),
                   r=["Sg", dgn, pdsn], w=["Sg"])
        if not main:
            op("scalar", lambda e: e.activation(out=Sgb[:], in_=Sg[:], func=AF.Copy), r=["Sg"], w=["Sgb"])
        if main:
            (mixed, mixedn) = mixed_pool.get()
            (ssqg, ssqgn) = small.get()
            for h in range(4):
                po, pon = pos[h // 2]
                hh = h % 2
                op("scalar", (lambda po, hh, h: lambda e: e.activation(out=mixed[:, h * 256:(h + 1) * 256], in_=po[:, hh * 256:(hh + 1) * 256], func=AF.Square,
                                                                        accum_out=ssqg[:, h:h + 1]))(po, hh, h), r=[pon], w=[ssqgn, mixedn + "g"])
            op("scalar", lambda e: e.activation(out=Sgb[:], in_=Sg[:], func=AF.Copy), r=["Sg"], w=["Sgb"])
            rg, rgn = rstd_from(ssqg[:, 0:4], ssqgn, 4, 1.0 / 256)
            for h in range(4):
                po, pon = pos[h // 2]
                hh = h % 2
                op("vector", (lambda po, hh, h: lambda e: e.scalar_tensor_tensor(
                    out=mixed[:, h * 256:(h + 1) * 256], in0=po[:, hh * 256:(hh + 1) * 256], scalar=rg[:, h:h + 1],
                    in1=sg[:, h * 256:(h + 1) * 256], op0=ALU.mult, op1=ALU.mult))(po, hh, h),
                   r=[pon, rgn, sgn], w=[mixedn + "g"])
        yield
        yzs = []
        saved_banks = None
        if CUR[0] is not None and CUR[0].banks == (4, 5):
            saved_banks = CUR[0]
            saved_banks.banks = (4, 5, 6, 7)
        if main:
            yl = []
            for g in range(2):
                (MT, MTn) = MTs[g]
                (py, pyn) = psb()

                def fy(e, py=py, MT=MT, g=g):
                    last = None
                    for hh in range(8):
                        h = g * 8 + hh
                        last = e.matmul(out=py[:, hh * 64:(hh + 1) * 64], lhsT=MT[:, hh, :], rhs=xdt[:, h * 64:(h + 1) * 64], start=True, stop=True)
                    return last
                op("tensor", fy, r=[MTn, xdtn], w=[pyn])
                (pyi, pyin) = psb()
                op("tensor", (lambda pyi, g: lambda e: e.matmul(out=pyi[:, :], lhsT=xcC[:, g, tsl], rhs=Ssb[:, g, :], start=True, stop=True))(pyi, g),
                   r=[xcn, "Ssb"], w=[pyin])
                (t1, t1n) = yt_pool.get()
                yl.append((py, pyn, pyi, pyin, t1, t1n))
            for g in range(2):
                (py, pyn, pyi, pyin, t1, t1n) = yl[g]
                op("vector", (lambda pyi, t1, g: lambda e: e.tensor_tensor(out=t1[:].rearrange("p (h q) -> p h q", h=8),
                                                                            in0=pyi[:, :].rearrange("p (h q) -> p h q", h=8),
                                                                            in1=ex[:, g * 8:(g + 1) * 8].unsqueeze(2).broadcast_to([128, 8, 64]), op=ALU.mult))(pyi, t1, g),
                   r=[pyin, exn], w=[t1n])
            for g in range(2):
                (py, pyn, pyi, pyin, t1, t1n) = yl[g]
                op("vector", (lambda py, t1: lambda e: e.tensor_tensor(out=t1[:], in0=py[:, :], in1=t1[:], op=ALU.add))(py, t1), r=[pyn, t1n], w=[t1n])
                yzs.append((t1, t1n))
        psl = []
        for g in range(2):
            (pss, pssn) = psb()
            op("tensor", (lambda pss, g: lambda e: e.matmul(out=pss[:, :], lhsT=Btm[:, g * 128:(g + 1) * 128], rhs=xdd[:, g * 512:(g + 1) * 512],
                                                             start=True, stop=True))(pss, g), r=[Btmn, xddn], w=[pssn])
            psl.append((pss, pssn))
        if saved_banks is not None:
            saved_banks.banks = (4, 5)
        for g in range(2):
            op("vector", (lambda g: lambda e: e.tensor_tensor(out=Ss[:, g, :].rearrange("p (h q) -> p h q", h=8),
                                                              in0=Ss[:, g, :].rearrange("p (h q) -> p h q", h=8),
                                                              in1=ex[:, 32 + g * 8:40 + g * 8].unsqueeze(2).broadcast_to([128, 8, 64]), op=ALU.mult))(g),
               r=["Ss%d" % g, exn, "Ssb"], w=["Ss%d" % g])
        for g in range(2):
            (pss, pssn) = psl[g]
            op("vector", (lambda pss, g: lambda e: e.tensor_tensor(out=Ss[:, g, :], in0=Ss[:, g, :], in1=pss[:, :], op=ALU.add))(pss, g),
               r=["Ss%d" % g, pssn], w=["Ss%d" % g])
        for g in range(2):
            op("scalar", (lambda g: lambda e: e.activation(out=Ssb[:, g, :], in_=Ss[:, g, :], func=AF.Copy))(g), r=["Ss%d" % g], w=["Ssb"])
        if main:
            (ssqs, ssqsn) = small.get()
            for g in range(2):
                (t1, t1n) = yzs[g]
                op("vector", (lambda t1, g: lambda e: e.tensor_tensor(out=t1[:], in0=t1[:], in1=xsD[:, g * 512:(g + 1) * 512], op=ALU.add))(t1, g),
                   r=[t1n, xsDn], w=[t1n])
            for g in range(2):
                (t1, t1n) = yzs[g]
                op("vector", (lambda t1, g: lambda e: e.tensor_tensor(out=t1[:], in0=t1[:], in1=sz[:, g * 512:(g + 1) * 512], op=ALU.mult))(t1, g),
                   r=[t1n, szn], w=[t1n])
            for g in range(2):
                (t1, t1n) = yzs[g]
                op("scalar", (lambda t1, g: lambda e: e.activation(out=mixed[:, 1024 + g * 512:1024 + (g + 1) * 512], in_=t1[:], func=AF.Square, accum_out=ssqs[:, g:g + 1]))(t1, g),
                   r=[t1n], w=[ssqsn, mixedn + "s%d" % g])
            rss, rssn = rstd_from(ssqs[:, 0:2], ssqsn, 2, 1.0 / 512)
            for g in range(2):
                t1, t1n = yzs[g]
                op("scalar", (lambda t1, g: lambda e: e.activation(out=mixed[:, 1024 + g * 512:1024 + (g + 1) * 512], in_=t1[:], func=AF.Copy,
                                                                    scale=rss[:, g:g + 1]))(t1, g), r=[t1n, rssn], w=[mixedn + "s%d" % g])
            r0 = tok0 + t * 128
            op("sync", (lambda r0: lambda e: e.dma_start(out=scr_d[r0:r0 + 128, :], in_=mixed[:]))(r0),
               r=[mixedn + "g", mixedn + "s0", mixedn + "s1"], w=["scr%d" % (r0 // 128)], dma="m_" + mixedn.split("#")[0])
        yield

    pendq = []

    def partner(nhalf):
        todo = []
        for a in pendq:
            while a[1] > 0 and len(todo) < nhalf:
                todo.append(a[0])
                a[1] -= 1
        pendq[:] = [a for a in pendq if a[1] > 0]
        if not todo:
            return None

        def f():
            for g in todo:
                next(g)
        return f

    def slot(main_fn, nhalf, extra=None, wmain=1):
        p = partner(nhalf)
        fns, bks, ws = [], [], []
        if p is not None:
            fns.append(p); bks.append((4, 5)); ws.append(1)
        fns.append(main_fn); bks.append((0, 1, 2, 3) if p is not None else (0, 1, 2, 3, 4, 5)); ws.append(wmain if p is not None else (2 if extra is not None else 1))
        if extra is not None:
            fns.append(extra); bks.append(None); ws.append(1)
        interleave(fns, bks, "tile", ws)

    def run_phase(src_d, ntok, main):
        nsup = ntok // NTOK
        if nsup == 0:
            return
        st = {"xns": phaseA_pre(x_loads(src_d, 0)), "xts": x_loads(src_d, NTOK) if nsup > 1 else None}
        for s in range(nsup):
            par = s % 2
            xns = st["xns"]
            extra0 = None
            if late_pieces:
                extra0 = (lambda ps_: lambda: load_pieces(ps_))(late_pieces.pop())
            slot(lambda: phaseAB(xns, main, par, main or s == nsup - 1), 1, extra0, wmain=6)
            for t in range(NT):
                g = tile_gen(s * NTOK, t, main, par)
                extra = None
                if t == NT - 1 and s + 1 < nsup:
                    def extra(s=s):
                        st["xns"] = phaseA_pre(st["xts"])
                        st["xts"] = x_loads(src_d, (s + 2) * NTOK) if s + 2 < nsup else None
                slot((lambda g: lambda: next(g))(g), 1 if t < NT - 1 else 99, extra, wmain=(2 if t < NT - 1 else 1))
                pendq.append([g, 2])

    def mask_gen():
        op("gpsimd", lambda e: e.tensor_scalar(out=Sg[:], in0=Sg[:], scalar1=flag[:, 0:1], scalar2=0.0, op0=ALU.mult, op1=ALU.add), r=["Sg", "flag"], w=["Sg"])
        op("gpsimd", lambda e: e.tensor_scalar(out=Ss[:], in0=Ss[:], scalar1=flag[:, 0:1], scalar2=0.0, op0=ALU.mult, op1=ALU.add),
           r=["Ss0", "Ss1", "flag"], w=["Ss0", "Ss1"])
        op("scalar", lambda e: e.activation(out=Sgb[:], in_=Sg[:], func=AF.Copy), r=["Sg"], w=["Sgb"])
        op("scalar", lambda e: e.activation(out=Ssb[:], in_=Ss[:], func=AF.Copy), r=["Ss0", "Ss1"], w=["Ssb"])
        yield

    if STOP != "setup":
        run_phase(xp_d, TP, False)
    if TP > 0 and STOP not in ("setup", "pre"):
        pendq.append([mask_gen(), 1])
    if STOP not in ("setup", "pre"):
        run_phase(xm_d, TM, True)
    p_last = partner(99)
    if p_last is not None:
        p_last()

    if TM > 0 and STOP in (None, "p2w", "p2s0", "p2s1"):
        bar = sb("bar", [128, 1])
        op("gpsimd", lambda e: e.memset(bar[:], 0.0), w=ALLW + ["bar"])
        BAR = ["bar"]

        def f32view(a, b_):
            return big[:, a:b_].bitcast(F32)
        o_ = 26624
        mixin = [(big[:, o_:o_ + 2048], "mixin0"), (big[:, o_ + 2048:o_ + 4096], "mixin1")]; o_ += 4096
        mixTs = [(big[:, o_:o_ + 2048].rearrange("p (k t) -> p k t", k=16), "mixT0"),
                 (big[:, o_ + 2048:o_ + 4096].rearrange("p (k t) -> p k t", k=16), "mixT1")]; o_ += 4096
        hns = [(big[:, o_:o_ + 1024], "hn0"), (big[:, o_ + 1024:o_ + 2048], "hn1")]; o_ += 2048
        hnTs = [(big[:, o_:o_ + 1024].rearrange("p (k t) -> p k t", k=8), "hnT0"),
                (big[:, o_ + 1024:o_ + 2048].rearrange("p (k t) -> p k t", k=8), "hnT1")]; o_ += 2048
        hbufs = [(f32view(o_, o_ + 2048), "h0"), (f32view(o_ + 2048, o_ + 4096), "h1")]; o_ += 4096
        assert o_ <= 8 * DIN
        tgs = [(xsD_pool.bufs[0][0][:, :], "xsD0"), (xsD_pool.bufs[0][0][:, :], "xsD0")]
        obs = [(xt_pool.bufs[0][0][:, :], "xt0"), (xt_pool.bufs[1][0][:, :], "xt1")]
        xrs = [(mixed_pool.bufs[0][0][:, :].bitcast(F32), "mixed0"), (xsD_pool.bufs[1][0][:, :], "xsD1")]
        pts = [(yt_pool.bufs[0][0][:, 0:256], "ytmp0"), (yt_pool.bufs[1][0][:, 0:256], "ytmp1")]
        pbfs = [(Btm_pool.bufs[0][0][:, :], "Btm0"), (Btm_pool.bufs[1][0][:, :], "Btm1")]
        pTs = [(kd_pool.bufs[0][0][:, 0:256].rearrange("p (k t) -> p k t", k=2), "kd0"),
               (kd_pool.bufs[1][0][:, 0:256].rearrange("p (k t) -> p k t", k=2), "kd1"),
               (scm_pool.bufs[0][0][:, 0:2, :], "scm0")]
        fw_bc = sb("fw_bc", [128, D])
        op("sync", lambda e: e.dma_start(out=fw_bc[:], in_=fin_w_d.partition_broadcast(128)), w=["fw_bc"], dma="c_fw")
        def load_wout():
            for fc in range(16):
                rows = w_out_d[fc * 128:(fc + 1) * 128, :]
                if fc < 8:
                    sc, scn = gnw[:, (fc % 2):(fc % 2) + 1], "gnw"
                else:
                    sc, scn = snw[:, fc - 8:fc - 7], "snw"
                for c0 in (0, 512):
                    load_w(Wout, fc, c0, c0 + 512, rows, sc, scn, "Wout", BAR)

        def load_wg_wpe():
            for kc in range(8):
                for c0 in (0, 512):
                    load_w(Wg, kc, c0, c0 + 512, w_g_d[kc * 128:(kc + 1) * 128, :], pnw[:, kc:kc + 1], "pnw", "Wg", BAR)
            for c in range(2):
                for c0 in (0, 512):
                    load_w(Wpe, c, c0, c0 + 512, w_pe_d[c * 128:(c + 1) * 128, :], None, None, "Wpe", BAR)
        p2_loaders = [load_wg_wpe, load_wout]

        def p2_gen(ti):
            r0 = ti * 128
            (mi, min_) = mixin[ti % 2]
            (mixT, mixTn) = mixTs[ti % 2]
            (xr, xrn) = xrs[ti % 2]
            (pt, ptn) = pts[ti % 2]
            (pbf, pbfn) = pbfs[ti % 2]
            (pT, pTn) = pTs[ti % 3]
            (hb_, hn_) = hbufs[ti % 2]
            (hn, hnn) = hns[ti % 2]
            (hnT, hnTn) = hnTs[ti % 2]
            (tg, tgn0) = tgs[ti % 2]
            (ob, obn) = obs[ti % 2]
            op("sync", lambda e: e.dma_start(out=mi, in_=scr_d[r0:r0 + 128, :]), r=["scr%d" % ti] + BAR, w=[min_], dma="mi_" + min_)
            xr_w = [xrn] + (["mixed0g", "mixed0s0", "mixed0s1"] if xrn == "mixed0" else [])
            op("sync", lambda e: e.dma_start(out=xr, in_=xm_d[r0:r0 + 128, :]), w=xr_w, dma="x2_" + xrn)
            op("sync", lambda e: e.dma_start(out=pt, in_=pm_d[r0:r0 + 128, :]), w=[ptn], dma="p_" + ptn)
            transposes([mi[:, fc * 128:(fc + 1) * 128] for fc in range(8)], [min_], mixT[:, 0:8, :], mixTn + "a", "scalar")
            transposes([mi[:, fc * 128:(fc + 1) * 128] for fc in range(8, 16)], [min_], mixT[:, 8:16, :], mixTn + "b", "vector")
            op("gpsimd", lambda e: e.tensor_copy(out=pbf, in_=pt), r=[ptn], w=[pbfn])
            transposes([pbf[:, c * 128:(c + 1) * 128] for c in range(2)], [pbfn], pT, pTn, "vector")
            yield
            if STOP == "p2s0":
                return
            for half in range(2):
                (ph, phn) = psb()

                def fh(e, ph=ph, half=half):
                    last = None
                    for fc in range(16):
                        last = e.matmul(out=ph[:, :], lhsT=mixT[:, fc, :], rhs=Wout[:, fc, half * 512:(half + 1) * 512], start=(fc == 0), stop=(fc == 15))
                    return last
                op("tensor", fh, r=[mixTn + "a", mixTn + "b", "Wout"], w=[phn])
                op("vector", (lambda ph, half: lambda e: e.tensor_tensor(out=hb_[:, half * 512:(half + 1) * 512], in0=ph[:, :],
                                                                          in1=xr[:, half * 512:(half + 1) * 512], op=ALU.add))(ph, half),
                   r=[phn, xrn] + BAR, w=[hn_ + "_%d" % half])
            (ssqh, ssqhn) = small.get()
            op("scalar", lambda e: e.activation(out=hn, in_=hb_, func=AF.Square, accum_out=ssqh[:, 0:1]),
               r=[hn_ + "_0", hn_ + "_1"] + BAR, w=[ssqhn, hnn])
            rh, rhn = rstd_from(ssqh[:, 0:1], ssqhn, 1, 1.0 / D)
            op("scalar", lambda e: e.activation(out=hn, in_=hb_, func=AF.Copy, scale=rh[:, 0:1]),
               r=[hn_ + "_0", hn_ + "_1", rhn] + BAR, w=[hnn])
            transposes([hn[:, k * 128:(k + 1) * 128] for k in range(8)], [hnn], hnT, hnTn, "vector")
            yield
            if STOP == "p2s1":
                return
            (ssqf, ssqfn) = small.get()
            for half in range(2):
                (pgt, pgtn) = psb()

                def fg(e, pgt=pgt, half=half):
                    last = None
                    for kc in range(8):
                        last = e.matmul(out=pgt[:, :], lhsT=hnT[:, kc, :], rhs=Wg[:, kc, half * 512:(half + 1) * 512], start=(kc == 0), stop=(kc == 7))
                    return last
                op("tensor", fg, r=[hnTn, "Wg"], w=[pgtn])
                (ppe, ppen) = psb()

                def fpe(e, ppe=ppe, half=half):
                    last = None
                    for c in range(2):
                        last = e.matmul(out=ppe[:, :], lhsT=pT[:, c, :], rhs=Wpe[:, c, half * 512:(half + 1) * 512], start=(c == 0), stop=(c == 1))
                    return last
                op("tensor", fpe, r=[pTn, "Wpe"], w=[ppen])
                hs = slice(half * 512, (half + 1) * 512)
                tgn = tgn0 + "_%d" % half
                op("scalar", (lambda pgt, hs: lambda e: e.activation(out=tg[:, hs], in_=pgt[:, :], func=AF.Tanh, scale=0.5))(pgt, hs), r=[pgtn], w=[tgn0, tgn])
                op("vector", (lambda ppe, hs: lambda e: e.scalar_tensor_tensor(out=tg[:, hs], in0=tg[:, hs], scalar=1.0, in1=ppe[:, :], op0=ALU.add, op1=ALU.mult))(ppe, hs),
                   r=[ppen, tgn], w=[tgn])
                op("vector", (lambda hs, half: lambda e: e.scalar_tensor_tensor(out=hb_[:, hs], in0=tg[:, hs], scalar=0.5, in1=hb_[:, hs], op0=ALU.mult, op1=ALU.add))(hs, half),
                   r=[tgn, hn_ + "_%d" % half], w=[hn_ + "_%d" % half])
            op("scalar", lambda e: e.activation(out=ob, in_=hb_, func=AF.Square, accum_out=ssqf[:, 0:1]),
               r=[hn_ + "_0", hn_ + "_1"], w=[ssqfn, obn])
            rf, rfn = rstd_from(ssqf[:, 0:1], ssqfn, 1, 1.0 / D)
            op("vector", lambda e: e.scalar_tensor_tensor(out=ob, in0=hb_, scalar=rf[:, 0:1], in1=fw_bc[:], op0=ALU.mult, op1=ALU.mult),
               r=[hn_ + "_0", hn_ + "_1", rfn, "fw_bc"], w=[obn])
            op("sync", lambda e: e.dma_start(out=out_d[r0:r0 + 128, :], in_=ob), r=[obn], dma="o_" + obn)
            yield

        active = []
        ntile = TM // 128 if STOP != "p2w" else 0
        nxt = 0
        stage_banks = {0: (0, 1), 1: (2, 3, 4), 2: (5, 6, 7)}
        while nxt < ntile or active:
            fns = []
            bks = []
            for a in active:
                fns.append((lambda g: lambda: next(g, None))(a[0]))
                bks.append(stage_banks[a[1]])
            if nxt < ntile:
                a = [p2_gen(nxt), 0]
                nxt += 1
                active.append(a)
                fns.append((lambda g: lambda: next(g, None))(a[0]))
                bks.append(stage_banks[0])
            if p2_loaders:
                fns.append(p2_loaders.pop())
                bks.append(None)
            interleave(fns, bks, "p2")
            for a in active:
                a[1] += 1
            active = [a for a in active if a[1] < 3]

    S.emit(final_waits=[k for k in S.dma_streams if k.startswith("o_")])
    return nc


WKEYS = ["norm_w", "w_in", "gla_gate_up", "gla_gate_b", "gla_norm_w", "conv_w", "conv_b", "dt_bias", "a_log",
         "d_skip", "ssd_norm_w", "w_out", "w_pe", "w_pe_gate", "pe_norm_w"]


def run_layer(x, p, w, final_norm_w):
    B, T, _ = x.shape
    half = T // 2
    nc = build_program(half, half)
    wmap = {k: np.ascontiguousarray(np.asarray(w[k], dtype=np.float32)) for k in WKEYS}
    wmap["final_norm_w"] = np.ascontiguousarray(np.asarray(final_norm_w, dtype=np.float32))
    in_maps = []
    for b in range(B):
        for s in range(2):
            m = dict(wmap)
            m["xm"] = np.ascontiguousarray(x[b, s * half:(s + 1) * half])
            m["xp"] = np.ascontiguousarray(x[b, 0:half]) if s == 1 else np.zeros((half, D), np.float32)
            m["pm"] = np.ascontiguousarray(p[b, s * half:(s + 1) * half])
            m["flag"] = np.full((128, 1), float(s), np.float32)
            in_maps.append(m)
    ncores = 2 * B
    res = run_bass_kernel_spmd(nc, in_maps, core_ids=list(range(ncores)))
    out = np.empty((B, T, D), np.float32)
    for b in range(B):
        for s in range(2):
            out[b, s * half:(s + 1) * half] = res.results[2 * b + s]["out"]
    return out


def kernel(**inputs):
    x = np.asarray(inputs["x"], dtype=np.float32)
    p = np.asarray(inputs["p"], dtype=np.float32)
    w = {k: np.asarray(inputs[k])[0] for k in WKEYS}
    return run_layer(x, p[0], w, inputs["final_norm_w"])
```

```python
import threading
import numpy as np
import concourse.bass as bass
import concourse.mybir as mybir
from concourse.bass_utils import run_bass_kernel_spmd

F32 = mybir.dt.float32
BF16 = mybir.dt.bfloat16
AF = mybir.ActivationFunctionType
ALU = mybir.AluOpType

ENGS = ["tensor", "vector", "scalar", "gpsimd", "sync"]

D = 1024
DIN = 5664
C_Q, C_K, C_V, C_G, C_GLR, C_Z, C_XBC, C_DT = 0, 512, 1024, 2048, 3072, 3088, 4112, 5648
EPS = 1e-6
STOP = None
NT = 2
NTOK = NT * 128


class Sched:
    def __init__(self, nc):
        self.nc = nc
        self.ops = []
        self.last_writer = {}
        self.readers = {}
        self.eng_count = {e: 0 for e in ENGS}
        self.dma_streams = {}

    def _chk(self, names, is_write=False):
        out = []
        for b in names:
            if "#" in b:
                base, rest = b.split("#", 1)
                i = 0
                while i < len(rest) and rest[i].isdigit():
                    i += 1
                gen, suf = int(rest[:i]), rest[i:]
                assert GEN.get(base, 0) == gen, "stale buffer use: %s (current gen %d)" % (b, GEN.get(base, 0))
                out.append(base + suf)
            else:
                out.append(b)
        return out

    def op(self, eng, fn, r=(), w=(), dma=None, ndma=1):
        r = self._chk(r)
        w = self._chk(w)
        idx = len(self.ops)
        deps = set()
        for b in r:
            if b in self.last_writer:
                deps.add(self.last_writer[b])
        for b in w:
            if b in self.last_writer:
                deps.add(self.last_writer[b])
            for q in self.readers.get(b, ()):
                deps.add(q)
        deps.discard(idx)
        o = dict(idx=idx, eng=eng, fn=fn, deps=sorted(deps), dma=dma, ndma=ndma)
        if dma is None:
            self.eng_count[eng] += 1
            o["seq"] = self.eng_count[eng]
        else:
            c = self.dma_streams.get(dma, 0) + ndma
            self.dma_streams[dma] = c
            o["seq"] = c * 16
        self.ops.append(o)
        for b in r:
            self.readers.setdefault(b, []).append(idx)
        for b in w:
            self.last_writer[b] = idx
            self.readers[b] = []
        if CUR[0] is not None:
            CUR[0].pause()
        return idx

    def emit(self, final_waits=()):
        nc = self.nc
        sems = {e: nc.alloc_semaphore("sem_" + e) for e in ENGS}
        dsems = {s: nc.alloc_semaphore("dsem_" + s) for s in self.dma_streams}
        ops = self.ops
        per_eng = {e: [o for o in ops if o["eng"] == e] for e in ENGS}

        def sem_of(o):
            return dsems[o["dma"]] if o["dma"] is not None else sems[o["eng"]]

        def key_of(o):
            return ("d", o["dma"]) if o["dma"] is not None else ("e", o["eng"])

        def run(eng_name, eng):
            waited = {}
            if eng_name != "gpsimd" and per_eng["gpsimd"]:
                eng.wait_ge(sems["gpsimd"], 1)
                waited[("e", "gpsimd")] = 1
            for o in per_eng[eng_name]:
                need = {}
                for d in o["deps"]:
                    p = ops[d]
                    k = key_of(p)
                    if eng_name == "tensor" and k == ("e", "tensor"):
                        continue
                    if need.get(k, (0, None))[0] < p["seq"]:
                        need[k] = (p["seq"], sem_of(p))
                for k, (v, s) in need.items():
                    if waited.get(k, 0) >= v:
                        continue
                    eng.wait_ge(s, v)
                    waited[k] = v
                res = o["fn"](eng)
                if o["dma"] is not None:
                    rs = res if isinstance(res, (list, tuple)) else [res]
                    assert len(rs) == o["ndma"], (len(rs), o["ndma"])
                    for ins in rs:
                        ins.then_inc(dsems[o["dma"]], 16)
                else:
                    res.then_inc(sems[eng_name], 1)
            if eng_name == "sync":
                for st in final_waits:
                    eng.wait_ge(dsems[st], self.dma_streams[st] * 16)

        with nc.Block() as block:
            @block.tensor
            def _(e):
                run("tensor", e)

            @block.vector
            def _(e):
                run("vector", e)

            @block.scalar
            def _(e):
                run("scalar", e)

            @block.gpsimd
            def _(e):
                run("gpsimd", e)

            @block.sync
            def _(e):
                run("sync", e)


GEN = {}
CUR = [None]


class Stream:
    def __init__(self, fn, banks=None):
        self.fn = fn
        self.banks = banks
        self.go = threading.Semaphore(0)
        self.done = threading.Semaphore(0)
        self.finished = False
        self.exc = None
        self.th = threading.Thread(target=self._run, daemon=True)
        self.started = False

    def _run(self):
        self.go.acquire()
        try:
            self.fn()
        except BaseException as e:
            self.exc = e
        self.finished = True
        self.done.release()

    def step(self):
        if not self.started:
            self.started = True
            self.th.start()
        CUR[0] = self
        self.go.release()
        self.done.acquire()
        CUR[0] = None
        if self.exc is not None:
            raise self.exc

    def pause(self):
        self.done.release()
        self.go.acquire()


ILV = {"AB": True, "tile": True, "p2": True}


def interleave(fns, banks=None, kind="tile", weights=None):
    if not ILV[kind]:
        for f in fns:
            f()
        return
    sts = [Stream(f, banks[i] if banks else None) for i, f in enumerate(fns)]
    while any(not st.finished for st in sts):
        for i, st in enumerate(sts):
            for _ in range(weights[i] if weights else 1):
                if not st.finished:
                    st.step()


class Rot:
    def __init__(self, nc, name, n, shape, dt):
        self.bufs = [(nc.alloc_sbuf_tensor("%s%d" % (name, i), shape, dt), "%s%d" % (name, i)) for i in range(n)]
        self.i = 0

    def get(self):
        (t, n) = self.bufs[self.i % len(self.bufs)]
        self.i += 1
        GEN[n] = GEN.get(n, 0) + 1
        return (t, "%s#%d" % (n, GEN[n]))


def build_program(TP, TM):
    GEN.clear()
    nc = bass.Bass("TRN2", target_bir_lowering=False)
    S = Sched(nc)
    op = S.op

    def din(name, shape):
        return nc.dram_tensor(name, shape, F32, kind="ExternalInput").ap()

    xp_d = din("xp", [TP, D])
    xm_d = din("xm", [TM, D])
    pm_d = din("pm", [TM, 256])
    flag_d = din("flag", [128, 1])
    norm_w_d = din("norm_w", [D])
    w_in_d = din("w_in", [D, DIN])
    up_d = din("gla_gate_up", [16, 512])
    gate_b_d = din("gla_gate_b", [512])
    gla_nw_d = din("gla_norm_w", [256])
    conv_w_d = din("conv_w", [4, 1536])
    conv_b_d = din("conv_b", [1536])
    dt_bias_d = din("dt_bias", [16])
    a_log_d = din("a_log", [16])
    d_skip_d = din("d_skip", [16])
    ssd_nw_d = din("ssd_norm_w", [1024])
    w_out_d = din("w_out", [2048, D])
    w_pe_d = din("w_pe", [256, D])
    w_g_d = din("w_pe_gate", [D, D])
    pe_nw_d = din("pe_norm_w", [D])
    fin_w_d = din("final_norm_w", [D])
    out_d = nc.dram_tensor("out", [TM, D], F32, kind="ExternalOutput").ap()

    def sb(name, shape, dt=F32):
        return nc.alloc_sbuf_tensor(name, shape, dt)

    big = sb("big", [128, 8 * DIN], BF16)
    op("gpsimd", lambda e: e.memset(big[:, 0:8192], 0.0), w=["startgate"])

    ps = nc.alloc_psum_tensor("ps", [128, 8 * 512], F32)
    ps_i = [0]

    ps_l = [0]
    ps_ctr = {}

    def psb(long=False):
        cur = CUR[0]
        if long and not (cur is not None and cur.banks is not None and cur.banks[-1] >= 6):
            k = 6 + ps_l[0] % 2
            ps_l[0] += 1
        else:
            bs = tuple(cur.banks) if (cur is not None and cur.banks is not None) else (0, 1, 2, 3, 4, 5)
            c = ps_ctr.get(bs, 0)
            ps_ctr[bs] = c + 1
            k = bs[c % len(bs)]
        n = "ps%d" % k
        GEN[n] = GEN.get(n, 0) + 1
        return ps[:, k * 512:(k + 1) * 512], "%s#%d" % (n, GEN[n])

    ones_f = sb("ones_f", [128, 128])
    identf = sb("identf", [128, 128])
    ident = sb("ident", [128, 128], BF16)
    L_f = sb("L_f", [128, 128])
    SU_f = sb("SU_f", [128, 128])
    SU_b = sb("SU_b", [128, 128], BF16)
    nhalf = sb("nhalf", [128, 4])
    epsc = sb("epsc", [128, 1])
    rmask = sb("rmask", [128, NTOK])
    NV = 90
    vrows = sb("vrows", [NV, 128])
    vecs = sb("vecs", [128, NV])
    nw = vecs[:, 0:8]
    pnw = vecs[:, 8:16]
    snw = vecs[:, 16:24]
    gnw = vecs[:, 24:26]
    nb = vecs[:, 26:30]
    cb = vecs[:, 30:42]
    cw = vecs[:, 42:90].rearrange("p (j c) -> p j c", j=4)
    dtb_bc = sb("dtb_bc", [128, 16])
    aneg_bc = sb("aneg_bc", [128, 16])
    dsk_bc = sb("dsk_bc", [128, 16])
    flag = sb("flag_sb", [128, 1])
    up_b = sb("up_b", [16, 512], BF16)

    op("gpsimd", lambda e: e.memset(ones_f[:], 1.0), w=["ones_f"])
    op("gpsimd", lambda e: e.memset(nhalf[:], -0.5), w=["nhalf"])
    op("gpsimd", lambda e: e.memset(epsc[:], EPS), w=["epsc"])
    op("gpsimd", lambda e: e.affine_select(out=identf[:], in_=ones_f[:], pattern=[[-1, 128]], compare_op=ALU.is_equal,
                                           fill=0.0, base=0, channel_multiplier=1), r=["ones_f"], w=["identf"])
    op("gpsimd", lambda e: e.affine_select(out=L_f[:], in_=ones_f[:], pattern=[[1, 128]], compare_op=ALU.is_ge,
                                           fill=0.0, base=0, channel_multiplier=-1), r=["ones_f"], w=["L_f"])
    op("gpsimd", lambda e: e.affine_select(out=SU_f[:], in_=ones_f[:], pattern=[[-1, 128]], compare_op=ALU.is_gt,
                                           fill=0.0, base=0, channel_multiplier=1), r=["ones_f"], w=["SU_f"])
    op("vector", lambda e: e.tensor_copy(out=ident[:], in_=identf[:]), r=["identf"], w=["ident"])
    op("vector", lambda e: e.tensor_copy(out=SU_b[:], in_=SU_f[:]), r=["SU_f"], w=["SU_b"])
    op("gpsimd", lambda e: e.memset(rmask[:], 1.0), w=["rmask"])
    op("gpsimd", lambda e: e.memset(rmask[:].rearrange("p (t i) -> p t i", i=128)[:, :, 0], 0.0), w=["rmask"])

    def small_dma(dst, src, name, n=1):
        op("sync", lambda e: e.dma_start(out=dst, in_=src), w=[name], dma="c_" + name)

    def rows_dma(r0, src, nrow, tag):
        op("sync", lambda e: e.dma_start(out=vrows[r0:r0 + nrow, :], in_=src.rearrange("(k p) -> k p", p=128)), w=["vrows_" + tag], dma="c_" + tag)
    rows_dma(0, norm_w_d, 8, "nw")
    rows_dma(8, pe_nw_d, 8, "pnw")
    rows_dma(16, ssd_nw_d, 8, "snw")
    rows_dma(24, gla_nw_d, 2, "gnw")
    rows_dma(26, gate_b_d, 4, "nb")
    rows_dma(30, conv_b_d, 12, "cb")
    for j in range(4):
        rows_dma(42 + 12 * j, conv_w_d[j], 12, "cw%d" % j)
    VR = ["vrows_" + t for t in ("nw", "pnw", "snw", "gnw", "nb", "cb", "cw0", "cw1", "cw2", "cw3")]
    (pvv, pvvn) = psb()
    op("tensor", lambda e: e.matmul(out=pvv[:, 0:NV], lhsT=vrows[:, :], rhs=identf[0:NV, 0:NV], start=True, stop=True), r=VR + ["identf"], w=[pvvn])
    VN = ["nw", "pnw", "snw", "gnw", "nb", "cb", "cw"]
    op("vector", lambda e: e.tensor_copy(out=vecs[:, :], in_=pvv[:, 0:NV]), r=[pvvn], w=VN)
    small_dma(dtb_bc[:], dt_bias_d.partition_broadcast(128), "dtb_bc")
    small_dma(aneg_bc[:], a_log_d.partition_broadcast(128), "aneg_bc")
    small_dma(dsk_bc[:], d_skip_d.partition_broadcast(128), "dsk_bc")
    small_dma(flag[:], flag_d[:, :], "flag")
    op("gpsimd", lambda e: e.tensor_scalar(out=nb, in0=nb, scalar1=-1.0, scalar2=None, op0=ALU.mult), r=["nb"], w=["nb"])
    op("scalar", lambda e: e.activation(out=aneg_bc[:], in_=aneg_bc[:], func=AF.Exp), r=["aneg_bc"], w=["aneg_bc"])
    op("gpsimd", lambda e: e.tensor_scalar(out=aneg_bc[:], in0=aneg_bc[:], scalar1=-1.0, scalar2=None, op0=ALU.mult),
       r=["aneg_bc"], w=["aneg_bc"])


    Win = big[:, :].rearrange("p (k c) -> p k c", k=8)
    Wout = big[:, 0:16384].rearrange("p (k c) -> p k c", k=16)
    Wg = big[:, 16384:24576].rearrange("p (k c) -> p k c", k=8)
    Wpe = big[:, 24576:26624].rearrange("p (k c) -> p k c", k=2)

    Sg = sb("Sg", [128, 4, 256])
    Sgb = sb("Sgb", [128, 4, 256], BF16)
    Ss = sb("Ss", [128, 2, 512])
    Ssb = sb("Ssb", [128, 2, 512], BF16)
    halo = sb("halo", [128, 12, 3])
    for t_, n_ in ((Sg, ["Sg0", "Sg1"]), (Sgb, ["Sgb"]), (Ss, ["Ss0", "Ss1"]), (Ssb, ["Ssb"]), (halo, ["halo"])):
        op("gpsimd", (lambda t_: (lambda e: e.memset(t_[:], 0.0)))(t_), w=n_)

    xt_pool = Rot(nc, "xt", 2, [128, D], F32)
    xn_pool = Rot(nc, "xn", NT, [128, D], BF16)
    uT = sb("uT", [128, 8, NTOK], BF16)
    glr = sb("glr", [16, NTOK], BF16)
    qdec2 = [sb("qdec%d" % i, [128, 4, NTOK], BF16) for i in range(2)]
    kinv = sb("kinv", [128, 4, NTOK], BF16)
    kdT = sb("kdT", [128, 4, NTOK], BF16)
    decg2 = [sb("decg%d" % i, [128, 4, NT]) for i in range(2)]
    xc = sb("xc", [128, 10, NTOK], BF16)
    xcC2 = [sb("xcC%d" % i, [128, 2, NTOK], BF16) for i in range(2)]
    gtmp = Rot(nc, "gtmp", 6, [128, NTOK], F32)
    ncl_pool = Rot(nc, "ncl", 2, [128, NT], F32)
    xpre_pool = Rot(nc, "xpre", 4, [128, NTOK + 3], F32)
    small = Rot(nc, "sm", 20, [128, 16], F32)
    V_pool = Rot(nc, "V", 2, [128, 1024], BF16)
    sg_pool = Rot(nc, "sg", 2, [128, 1024], BF16)
    sz_pool = Rot(nc, "sz", 2, [128, 1024], BF16)
    scm_pool = Rot(nc, "scm", 2, [128, 4, 128], BF16)
    kd_pool = Rot(nc, "kd", 2, [128, 512], BF16)
    xdt_pool = Rot(nc, "xdt", 2, [128, 1024], BF16)
    xdd_pool = Rot(nc, "xdd", 2, [128, 1024], BF16)
    xsD_pool = Rot(nc, "xsD", 2, [128, 1024], F32)
    Btm_pool = Rot(nc, "Btm", 2, [128, 256], BF16)
    ex_pool = Rot(nc, "ex", 2, [128, 48], F32)
    rhs2_pool = Rot(nc, "rhs2", 1, [128, 16, 128], BF16)
    dec_pool = Rot(nc, "dec", 2, [128, 8, 128], BF16)
    cbm_pool = Rot(nc, "cbm", 2, [128, 2, 128], BF16)
    MT_pool = Rot(nc, "MT", 2, [128, 8, 128], BF16)
    yt_pool = Rot(nc, "ytmp", 2, [128, 512], F32)
    mixed_pool = Rot(nc, "mixed", 1, [128, 2048], BF16)
    scr_d = nc.dram_tensor("mix_scratch", [max(TM, 128), 2048], BF16, kind="Internal").ap()

    stage_bufs = [(xdt_pool.bufs[0][0][:, :].bitcast(F32), "xdt0"), (xdt_pool.bufs[1][0][:, :].bitcast(F32), "xdt1"),
                  (xdd_pool.bufs[0][0][:, :].bitcast(F32), "xdd0"), (xdd_pool.bufs[1][0][:, :].bitcast(F32), "xdd1"),
                  (V_pool.bufs[0][0][:, :].bitcast(F32), "V0"), (V_pool.bufs[1][0][:, :].bitcast(F32), "V1")]
    stg_i = [0]
    cast_engs = ["vector", "gpsimd", "scalar", "vector", "scalar"]
    ci = [0]

    def load_w(dst3, kc, c0, c1, src_rows, scale_ap, scale_name, wname, extra_r=()):
        (st, stn) = stage_bufs[stg_i[0] % len(stage_bufs)]
        stg_i[0] += 1
        n = c1 - c0
        op("sync", lambda e: e.dma_start(out=st[:, 0:n], in_=src_rows[:, c0:c1]), w=[stn], dma="w_" + stn)
        eng = cast_engs[ci[0] % len(cast_engs)]
        ci[0] += 1
        rr = [stn] + list(extra_r) + ([scale_name] if scale_name else [])
        if eng == "scalar":
            if scale_ap is None:
                f = lambda e: e.activation(out=dst3[:, kc, c0:c1], in_=st[:, 0:n], func=AF.Copy)
            else:
                f = lambda e: e.activation(out=dst3[:, kc, c0:c1], in_=st[:, 0:n], func=AF.Copy, scale=scale_ap)
        else:
            if scale_ap is None:
                f = lambda e: e.tensor_copy(out=dst3[:, kc, c0:c1], in_=st[:, 0:n])
            elif eng == "gpsimd":
                f = lambda e: e.tensor_scalar(out=dst3[:, kc, c0:c1], in0=st[:, 0:n], scalar1=scale_ap, scalar2=0.0, op0=ALU.mult, op1=ALU.add)
            else:
                f = lambda e: e.tensor_scalar(out=dst3[:, kc, c0:c1], in0=st[:, 0:n], scalar1=scale_ap, scalar2=None, op0=ALU.mult)
        op(eng, f, r=rr, w=[wname])

    upst = xsD_pool.bufs[0][0]
    op("sync", lambda e: e.dma_start(out=upst[0:16, 0:512], in_=up_d[:, :]), w=["xsD0"], dma="c_up")
    op("vector", lambda e: e.tensor_copy(out=up_b[:], in_=upst[0:16, 0:512]), r=["xsD0"], w=["up_b"])

    def wn(c0, n):
        return ["W%d" % p for p in range(c0 // 512, (c0 + n - 1) // 512 + 1)]
    ALLW = ["W%d" % p for p in range((DIN + 511) // 512)]
    def load_pieces(pieces):
        for piece in pieces:
            c0 = piece * 512
            for kc in range(8):
                load_w(Win, kc, c0, min(DIN, c0 + 512), w_in_d[kc * 128:(kc + 1) * 128, :], nw[:, kc:kc + 1], "nw", "W%d" % piece)
    load_pieces((6, 0, 1, 8, 9, 10, 11))
    late_pieces = [(2, 3, 4, 5, 7)]

    def rstd_from(ssq_ap, ssq_name, n, inv_count):
        (t1, t1n) = small.get()
        (t2, t2n) = small.get()
        op("gpsimd", lambda e: e.tensor_scalar(out=t1[:, 0:n], in0=ssq_ap, scalar1=inv_count, scalar2=EPS, op0=ALU.mult, op1=ALU.add),
           r=[ssq_name], w=[t1n])
        op("gpsimd", lambda e: e.tensor_tensor(out=t2[:, 0:n], in0=t1[:, 0:n], in1=nhalf[:, 0:n], op=ALU.pow), r=[t1n, "nhalf"], w=[t2n])
        return t2, t2n

    def transposes(src_aps, src_names, dst_ap, dst_name, copy_eng, extra_r=()):
        n = len(src_aps)
        (pb_, pbn) = psb()
        pv = pb_.bitcast(BF16)

        def f(e):
            last = None
            for i, a in enumerate(src_aps):
                last = e.transpose(out=pv[:, i * 128:(i + 1) * 128], in_=a, identity=ident[:])
            return last
        op("tensor", f, r=list(src_names) + ["ident"], w=[pbn])
        src = pv[:, 0:n * 128]
        if dst_ap.ndim == 3:
            src = src.rearrange("p (k t) -> p k t", k=n)
        if copy_eng == "scalar":
            op("scalar", lambda e: e.activation(out=dst_ap, in_=src, func=AF.Copy), r=[pbn] + list(extra_r), w=[dst_name])
        else:
            op("vector", lambda e: e.tensor_copy(out=dst_ap, in_=src), r=[pbn] + list(extra_r), w=[dst_name])

    def inproj_fm(c0, m):
        (pb_, pbn) = psb()

        def f(e):
            last = None
            for kc in range(8):
                last = e.matmul(out=pb_[0:m, 0:NTOK], lhsT=Win[:, kc, c0:c0 + m], rhs=uT[:, kc, :], start=(kc == 0), stop=(kc == 7))
            return last
        op("tensor", f, r=wn(c0, m) + ["uT"], w=[pbn])
        return pb_, pbn

    def inproj_tm(t, c0, n):
        (pb_, pbn) = psb()

        def f(e):
            last = None
            for kc in range(8):
                last = e.matmul(out=pb_[:, 0:n], lhsT=uT[:, kc, t * 128:(t + 1) * 128], rhs=Win[:, kc, c0:c0 + n], start=(kc == 0), stop=(kc == 7))
            return last
        op("tensor", f, r=wn(c0, n) + ["uT"], w=[pbn])
        return pb_, pbn

    def x_loads(src_d, tok0):
        xts = []
        for t in range(NT):
            (xt, xtn) = xt_pool.get()
            r0 = tok0 + t * 128
            op("sync", (lambda xt, r0: lambda e: e.dma_start(out=xt[:], in_=src_d[r0:r0 + 128, :]))(xt, r0), w=[xtn], dma="x_" + xtn.split("#")[0])
            xts.append((xt, xtn))
        return xts

    def phaseA_pre(xts):
        tl = []
        for t in range(NT):
            (xt, xtn) = xts[t]
            (ssq, ssqn) = small.get()
            (xn, xnn) = xn_pool.get()
            op("scalar", (lambda xt, ssq, xn: lambda e: e.activation(out=xn[:], in_=xt[:], func=AF.Square, accum_out=ssq[:, 0:1]))(xt, ssq, xn),
               r=[xtn], w=[ssqn, xnn])
            tl.append((xt, xtn, ssq, ssqn, xn, xnn))
        rl = [rstd_from(ssq[:, 0:1], ssqn, 1, 1.0 / D) for (xt, xtn, ssq, ssqn, xn, xnn) in tl]
        xns = []
        for (xt, xtn, ssq, ssqn, xn, xnn), (rs, rsn) in zip(tl, rl):
            op("vector", (lambda xt, xn, rs: lambda e: e.tensor_scalar(out=xn[:], in0=xt[:], scalar1=rs[:, 0:1], scalar2=None, op0=ALU.mult))(xt, xn, rs),
               r=[xtn, rsn], w=[xnn])
            xns.append((xn, xnn))
        return xns

    def phaseAB(xns, main, par, all_chunks):
        qdec = qdec2[par]; qn = "qdec%d" % par
        decg = decg2[par]; dgn = "decg%d" % par
        xcC = xcC2[par]; xcn = "xcC%d" % par
        for t in range(NT):
            (xn, xnn) = xns[t]
            transposes([xn[:, k * 128:(k + 1) * 128] for k in range(8)], [xnn], uT[:, :, t * 128:(t + 1) * 128], "uT", "vector")

        pb_, pbn = inproj_fm(C_GLR, 16)
        op("scalar", lambda e: e.activation(out=glr[:, :], in_=pb_[0:16, 0:NTOK], func=AF.Copy), r=[pbn], w=["glr"])
        for hp in range(2):
            hs = (2 * hp, 2 * hp + 1)
            H = {}
            for h in hs:
                (pz, pzn) = psb()
                op("tensor", (lambda pz, h: lambda e: e.matmul(out=pz[:, 0:NTOK], lhsT=up_b[:, h * 128:(h + 1) * 128], rhs=glr[:, :], start=True, stop=True))(pz, h),
                   r=["up_b", "glr"], w=[pzn])
                H[h] = dict(pz=pz, pzn=pzn)
            for h in hs:
                d = H[h]
                (d["l"], d["ln"]) = gtmp.get()
                op("scalar", (lambda pz, l, h: lambda e: e.activation(out=l[:], in_=pz[:, 0:NTOK], func=AF.Exp, scale=-1.0, bias=nb[:, h:h + 1]))(d["pz"], d["l"], h),
                   r=[d["pzn"], "nb"], w=[d["ln"]])
            for h in hs:
                d = H[h]
                op("scalar", (lambda l: lambda e: e.activation(out=l[:], in_=l[:], func=AF.Ln, bias=1.0))(d["l"]), r=[d["ln"]], w=[d["ln"]])
            for h in hs:
                d = H[h]
                (d["cl"], d["cln"]) = gtmp.get()
                op("vector", (lambda l, cl: lambda e: e.tensor_tensor_scan(out=cl[:], data0=rmask[:], data1=l[:], initial=0.0, op0=ALU.mult, op1=ALU.add))(d["l"], d["cl"]),
                   r=[d["ln"], "rmask"], w=[d["cln"]])
            for h in hs:
                d = H[h]
                (d["e1"], d["e1n"]) = gtmp.get()
                op("scalar", (lambda cl, e1: lambda e: e.activation(out=e1[:], in_=cl[:], func=AF.Exp, scale=-1.0 / 16))(d["cl"], d["e1"]), r=[d["cln"]], w=[d["e1n"]])
            for h in hs:
                d = H[h]
                op("gpsimd", (lambda e1, h: lambda e: e.tensor_copy(out=decg[:, h, :], in_=e1[:].rearrange("p (t i) -> p t i", i=128)[:, :, 127]))(d["e1"], h),
                   r=[d["e1n"]], w=[dgn])
                (d["ncl"], d["ncln"]) = ncl_pool.get()
                op("gpsimd", (lambda cl, ncl: lambda e: e.tensor_scalar(out=ncl[:], in0=cl[:].rearrange("p (t i) -> p t i", i=128)[:, :, 127],
                                                                         scalar1=-1.0 / 16, scalar2=0.0, op0=ALU.mult, op1=ALU.add))(d["cl"], d["ncl"]), r=[d["cln"]], w=[d["ncln"]])
            for h in hs:
                d = H[h]
                (xb, xbn) = xpre_pool.get()
                d["ed"], d["edn"] = xb[:, 0:NTOK], [xbn, xbn + "h"]

                def fed(e, cl=d["cl"], ed=d["ed"], ncl=d["ncl"]):
                    last = None
                    for t in range(NT):
                        last = e.activation(out=ed[:, t * 128:(t + 1) * 128], in_=cl[:, t * 128:(t + 1) * 128], func=AF.Exp, scale=1.0 / 16, bias=ncl[:, t:t + 1])
                    return last
                op("scalar", fed, r=[d["cln"], d["ncln"]], w=d["edn"])
            if main:
                for h in hs:
                    d = H[h]
                    (xb, xbn) = xpre_pool.get()
                    d["e2"], d["e2n"] = xb[:, 0:NTOK], [xbn, xbn + "h"]
                    op("scalar", (lambda cl, e2: lambda e: e.activation(out=e2, in_=cl[:], func=AF.Exp, scale=1.0 / 16))(d["cl"], d["e2"]), r=[d["cln"]], w=d["e2n"])
                for h in hs:
                    d = H[h]
                    pq, pqn = inproj_fm(C_Q + h * 128, 128)
                    op("vector", (lambda pq, e1, h: lambda e: e.scalar_tensor_tensor(out=qdec[:, h, :], in0=pq[:, 0:NTOK], scalar=128.0 ** -0.5, in1=e1[:],
                                                                                      op0=ALU.mult, op1=ALU.mult))(pq, d["e1"], h), r=[pqn, d["e1n"]], w=[qn])
            for h in hs:
                d = H[h]
                pk, pkn = inproj_fm(C_K + h * 128, 128)
                if main:
                    op("vector", (lambda pk, e2, h: lambda e: e.tensor_tensor(out=kinv[:, h, :], in0=pk[:, 0:NTOK], in1=e2, op=ALU.mult))(pk, d["e2"], h),
                       r=[pkn] + d["e2n"], w=["kinv"])
                op("vector", (lambda pk, ed, h: lambda e: e.tensor_tensor(out=kdT[:, h, :], in0=pk[:, 0:NTOK], in1=ed, op=ALU.mult))(pk, d["ed"], h),
                   r=[pkn] + d["edn"], w=["kdT"])

        nchunks = 12 if all_chunks else 10
        pend = []
        for c0 in range(0, nchunks, 2):
            pair = []
            for c in (c0, c0 + 1):
                pc, pcn = inproj_fm(C_XBC + c * 128, 128)
                (xpre, xpren) = xpre_pool.get()
                op("gpsimd", (lambda xpre, c: lambda e: e.tensor_copy(out=xpre[:, 0:3], in_=halo[:, c, :]))(xpre, c), r=["halo"], w=[xpren + "h"])
                op("scalar", (lambda xpre, pc: lambda e: e.activation(out=xpre[:, 3:3 + NTOK], in_=pc[:, 0:NTOK], func=AF.Copy))(xpre, pc), r=[pcn], w=[xpren])
                op("gpsimd", (lambda xpre, c: lambda e: e.tensor_copy(out=halo[:, c, :], in_=xpre[:, NTOK:NTOK + 3]))(xpre, c), r=[xpren, xpren + "h"], w=["halo"])
                (acc, accn) = gtmp.get()
                op("scalar", (lambda pc, acc, c: lambda e: e.activation(out=acc[:], in_=pc[:, 0:NTOK], func=AF.Identity,
                                                                        scale=cw[:, 3, c:c + 1], bias=cb[:, c:c + 1]))(pc, acc, c),
                   r=[pcn, "cw", "cb"], w=[accn])
                pair.append((c, xpre, xpren, acc, accn))
            for j in range(3):
                for (c, xpre, xpren, acc, accn) in pair:
                    op("vector", (lambda xpre, acc, c, j: lambda e: e.scalar_tensor_tensor(out=acc[:], in0=xpre[:, j:j + NTOK], scalar=cw[:, j, c:c + 1], in1=acc[:],
                                                                                            op0=ALU.mult, op1=ALU.add))(xpre, acc, c, j),
                       r=[xpren, xpren + "h", accn, "cw"], w=[accn])
            for f in pend:
                f()
            pend = []
            for (c, xpre, xpren, acc, accn) in pair:
                if c < 10:
                    pend.append((lambda acc, accn, c: lambda: op("scalar", lambda e: e.activation(out=xc[:, c, :], in_=acc[:], func=AF.Silu), r=[accn], w=["xc"]))(acc, accn, c))
                else:
                    pend.append((lambda acc, accn, c: lambda: op("scalar", lambda e: e.activation(out=xcC[:, c - 10, :], in_=acc[:], func=AF.Silu), r=[accn], w=[xcn]))(acc, accn, c))
        for f in pend:
            f()

    def tile_gen(tok0, t, main, par):
        qdec = qdec2[par]; qn = "qdec%d" % par
        decg = decg2[par]; dgn = "decg%d" % par
        xcC = xcC2[par]; xcn = "xcC%d" % par
        tsl = slice(t * 128, (t + 1) * 128)
        if main:
            (sg, sgn) = sg_pool.get()
            (sz, szn) = sz_pool.get()
            for (dst, dstn, c0) in ((sg, sgn, C_G), (sz, szn, C_Z)):
                for half in range(2):
                    pg, pgn = inproj_tm(t, c0 + half * 512, 512)
                    op("scalar", (lambda pg, dst, half: lambda e: e.activation(out=dst[:, half * 512:(half + 1) * 512], in_=pg[:, :], func=AF.Silu))(pg, dst, half),
                       r=[pgn], w=[dstn])
        pd, pdn = inproj_tm(t, C_DT, 16)
        (dtr, dtrn) = small.get()
        op("vector", (lambda pd, dtr: lambda e: e.tensor_tensor(out=dtr[:], in0=pd[:, 0:16], in1=dtb_bc[:], op=ALU.add))(pd, dtr), r=[pdn, "dtb_bc"], w=[dtrn])
        op("scalar", (lambda dtr: lambda e: e.activation(out=dtr[:], in_=dtr[:], func=AF.Exp))(dtr), r=[dtrn], w=[dtrn])
        (dt, dtn) = small.get()
        op("scalar", (lambda dtr, dt: lambda e: e.activation(out=dt[:], in_=dtr[:], func=AF.Ln, bias=1.0))(dtr, dt), r=[dtrn], w=[dtn])
        (dtA, dtAn) = small.get()
        op("gpsimd", (lambda dt, dtA: lambda e: e.tensor_tensor(out=dtA[:], in0=dt[:], in1=aneg_bc[:], op=ALU.mult))(dt, dtA), r=[dtn, "aneg_bc"], w=[dtAn])
        (V, Vn) = V_pool.get()
        for half in range(2):
            pv, pvn = inproj_tm(t, C_V + half * 512, 512)
            if half == 0:
                op("scalar", (lambda pv, V: lambda e: e.activation(out=V[:, 0:512], in_=pv[:, :], func=AF.Copy))(pv, V), r=[pvn], w=[Vn])
            else:
                op("vector", (lambda pv, V: lambda e: e.tensor_copy(out=V[:, 512:1024], in_=pv[:, :]))(pv, V), r=[pvn], w=[Vn])
        (pcm, pcmn) = psb()

        def fcm(e, pcm=pcm, dtA=dtA):
            e.matmul(out=pcm[:, 0:16], lhsT=L_f[:], rhs=dtA[:], start=True, stop=True)
            e.matmul(out=pcm[:, 16:32], lhsT=SU_f[:], rhs=dtA[:], start=True, stop=True)
            return e.matmul(out=pcm[:, 32:48], lhsT=ones_f[:], rhs=dtA[:], start=True, stop=True)
        op("tensor", fcm, r=["L_f", "SU_f", "ones_f", dtAn], w=[pcmn])
        (ex, exn) = ex_pool.get()
        op("scalar", (lambda pcm, ex: lambda e: e.activation(out=ex[:], in_=pcm[:, 0:48], func=AF.Exp))(pcm, ex), r=[pcmn], w=[exn])
        if main:
            (rhs2, rhs2n) = rhs2_pool.get()
            op("gpsimd", (lambda rhs2, dtA: lambda e: e.tensor_tensor(out=rhs2[:], in0=dtA[:].unsqueeze(2).broadcast_to([128, 16, 128]),
                                                                       in1=L_f[:].unsqueeze(1).broadcast_to([128, 16, 128]), op=ALU.mult))(rhs2, dtA),
               r=[dtAn, "L_f"], w=[rhs2n])
        (pxa, pxan) = psb()
        pxav = pxa.bitcast(BF16)

        def ftx(e, pxav=pxav):
            last = None
            for c in range(8):
                last = e.transpose(out=pxav[:, c * 128:(c + 1) * 128], in_=xc[:, c, tsl], identity=ident[:])
            return last
        op("tensor", ftx, r=["xc", "ident"], w=[pxan])
        (xdt, xdtn) = xdt_pool.get()
        op("vector", (lambda pxav, xdt, dt: lambda e: e.tensor_tensor(out=xdt[:].rearrange("p (h q) -> p h q", h=16),
                                                                        in0=pxav[:, :].rearrange("p (h q) -> p h q", h=16),
                                                                        in1=dt[:].unsqueeze(2).broadcast_to([128, 16, 64]), op=ALU.mult))(pxav, xdt, dt),
           r=[pxan, dtn], w=[xdtn])
        if main:
            (xsD, xsDn) = xsD_pool.get()
            op("vector", (lambda pxav, xsD: lambda e: e.tensor_tensor(out=xsD[:].rearrange("p (h q) -> p h q", h=16),
                                                                        in0=pxav[:, :].rearrange("p (h q) -> p h q", h=16),
                                                                        in1=dsk_bc[:].unsqueeze(2).broadcast_to([128, 16, 64]), op=ALU.mult))(pxav, xsD),
               r=[pxan, "dsk_bc"], w=[xsDn])
        (Btm, Btmn) = Btm_pool.get()
        transposes([xc[:, 8 + g, tsl] for g in range(2)], ["xc"], Btm[:, :], Btmn, "vector")
        (xdd, xddn) = xdd_pool.get()
        op("gpsimd", (lambda xdt, xdd, ex: lambda e: e.tensor_tensor(out=xdd[:].rearrange("p (h q) -> p h q", h=16),
                                                                      in0=xdt[:].rearrange("p (h q) -> p h q", h=16),
                                                                      in1=ex[:, 16:32].unsqueeze(2).broadcast_to([128, 16, 64]), op=ALU.mult))(xdt, xdd, ex),
           r=[xdtn, exn], w=[xddn])
        (kd, kdn) = kd_pool.get()
        transposes([kdT[:, h, tsl] for h in range(4)], ["kdT"], kd[:, :], kdn, "vector")
        MTs = []
        if main:
            (psc, pscn) = psb()

            def fsc(e, psc=psc):
                last = None
                for h in range(4):
                    last = e.matmul(out=psc[:, h * 128:(h + 1) * 128], lhsT=kinv[:, h, tsl], rhs=qdec[:, h, tsl], start=True, stop=True)
                return last
            op("tensor", fsc, r=["kinv", qn], w=[pscn])
            (scm, scmn) = scm_pool.get()
            op("vector", (lambda psc, scm: lambda e: e.tensor_tensor(out=scm[:], in0=psc[:, :].rearrange("p (h i) -> p h i", h=4),
                                                                       in1=L_f[:].unsqueeze(1).broadcast_to([128, 4, 128]), op=ALU.mult))(psc, scm),
               r=[pscn, "L_f"], w=[scmn])
            (pcb, pcbn) = psb()

            def fcb(e, pcb=pcb):
                e.matmul(out=pcb[:, 0:128], lhsT=xc[:, 8, tsl], rhs=xcC[:, 0, tsl], start=True, stop=True)
                return e.matmul(out=pcb[:, 128:256], lhsT=xc[:, 9, tsl], rhs=xcC[:, 1, tsl], start=True, stop=True)
            op("tensor", fcb, r=["xc", xcn], w=[pcbn])
            (cbm, cbmn) = cbm_pool.get()
            op("vector", (lambda pcb, cbm: lambda e: e.tensor_tensor(out=cbm[:], in0=pcb[:, 0:256].rearrange("p (g i) -> p g i", g=2),
                                                                       in1=L_f[:].unsqueeze(1).broadcast_to([128, 2, 128]), op=ALU.mult))(pcb, cbm),
               r=[pcbn, "L_f"], w=[cbmn])
            decs = [dec_pool.get() for g in range(2)]
            psgs = []
            for g in range(2):
                for q in range(2):
                    (psg, psgn) = psb()
                    hb = g * 8 + q * 4
                    op("tensor", (lambda psg, rhs2, hb: lambda e: e.matmul(out=psg[:, :], lhsT=SU_b[:], rhs=rhs2[:, hb:hb + 4, :], start=True, stop=True))(psg, rhs2, hb),
                       r=["SU_b", rhs2n], w=[psgn])
                    psgs.append((g, q, psg, psgn))
            for (g, q, psg, psgn) in psgs:
                (dec, decn) = decs[g]
                op("scalar", (lambda psg, dec, q: lambda e: e.activation(out=dec[:, q * 4:(q + 1) * 4, :], in_=psg[:, :].rearrange("p (h i) -> p h i", h=4),
                                                                          func=AF.Exp))(psg, dec, q), r=[psgn], w=[decn])
            for g in range(2):
                (dec, decn) = decs[g]
                (MT, MTn) = MT_pool.get()
                op("vector", (lambda MT, dec, cbm, g: lambda e: e.tensor_tensor(out=MT[:], in0=dec[:], in1=cbm[:, g, :].unsqueeze(1).broadcast_to([128, 8, 128]),
                                                                                 op=ALU.mult))(MT, dec, cbm, g), r=[decn, cbmn], w=[MTn])
                MTs.append((MT, MTn))
        yield
        pos = []
        if main:
            for hp in range(2):
                (po, pon) = psb(long=True)

                def fo(e, po=po, hp=hp):
                    last = None
                    for hh in range(2):
                        h = hp * 2 + hh
                        e.matmul(out=po[:, hh * 256:(hh + 1) * 256], lhsT=scm[:, h, :], rhs=V[:, h * 256:(h + 1) * 256], start=True, stop=False)
                        last = e.matmul(out=po[:, hh * 256:(hh + 1) * 256], lhsT=qdec[:, h, tsl], rhs=Sgb[:, h, :], start=False, stop=True)
                    return last
                op("tensor", fo, r=[scmn, Vn, qn, "Sgb"], w=[pon])
                pos.append((po, pon))
        for hp in range(2):
            (pds, pdsn) = psb()

            def fds(e, pds=pds, hp=hp):
                last = None
                for hh in range(2):
                    h = hp * 2 + hh
                    last = e.matmul(out=pds[:, hh * 256:(hh + 1) * 256], lhsT=kd[:, h * 128:(h + 1) * 128], rhs=V[:, h * 256:(h + 1) * 256], start=True, stop=True)
                return last
            op("tensor", fds, r=[kdn, Vn], w=[pdsn])
            for hh in range(2):
                h = hp * 2 + hh
                op("vector", (lambda pds, h, hh: lambda e: e.scalar_tensor_tensor(out=Sg[:, h, :], in0=Sg[:, h, :], scalar=decg[:, h, t:t + 1],
                                                                                   in1=pds[:, hh * 256:(hh + 1) * 256], op0=ALU.mult, op1=ALU.add))(pds, h, hh),
                   r=["Sg%d" % hp, dgn, pdsn], w=["Sg%d" % hp])
        if not main:
            for hp_ in range(2):
                op("scalar", (lambda hp_: lambda e: e.activation(out=Sgb[:, 2 * hp_:2 * hp_ + 2, :], in_=Sg[:, 2 * hp_:2 * hp_ + 2, :], func=AF.Copy))(hp_),
                   r=["Sg%d" % hp_], w=["Sgb"])
        if main:
            (mixed, mixedn) = mixed_pool.get()
            (ssqg, ssqgn) = small.get()
            for h in range(4):
                po, pon = pos[h // 2]
                hh = h % 2
                op("scalar", (lambda po, hh, h: lambda e: e.activation(out=mixed[:, h * 256:(h + 1) * 256], in_=po[:, hh * 256:(hh + 1) * 256], func=AF.Square,
                                                                        accum_out=ssqg[:, h:h + 1]))(po, hh, h), r=[pon], w=[ssqgn, mixedn + "g"])
            for hp_ in range(2):
                op("scalar", (lambda hp_: lambda e: e.activation(out=Sgb[:, 2 * hp_:2 * hp_ + 2, :], in_=Sg[:, 2 * hp_:2 * hp_ + 2, :], func=AF.Copy))(hp_),
                   r=["Sg%d" % hp_], w=["Sgb"])
            rg, rgn = rstd_from(ssqg[:, 0:4], ssqgn, 4, 1.0 / 256)
            for h in range(4):
                po, pon = pos[h // 2]
                hh = h % 2
                op("vector", (lambda po, hh, h: lambda e: e.scalar_tensor_tensor(
                    out=mixed[:, h * 256:(h + 1) * 256], in0=po[:, hh * 256:(hh + 1) * 256], scalar=rg[:, h:h + 1],
                    in1=sg[:, h * 256:(h + 1) * 256], op0=ALU.mult, op1=ALU.mult))(po, hh, h),
                   r=[pon, rgn, sgn], w=[mixedn + "g"])
        yield
        yzs = []
        saved_banks = None
        if CUR[0] is not None and CUR[0].banks == (4, 5):
            saved_banks = CUR[0]
            saved_banks.banks = (4, 5, 6, 7)
        if main:
            yl = []
            for g in range(2):
                (MT, MTn) = MTs[g]
                (py, pyn) = psb()

                def fy(e, py=py, MT=MT, g=g):
                    last = None
                    for hh in range(8):
                        h = g * 8 + hh
                        last = e.matmul(out=py[:, hh * 64:(hh + 1) * 64], lhsT=MT[:, hh, :], rhs=xdt[:, h * 64:(h + 1) * 64], start=True, stop=True)
                    return last
                op("tensor", fy, r=[MTn, xdtn], w=[pyn])
                (pyi, pyin) = psb()
                op("tensor", (lambda pyi, g: lambda e: e.matmul(out=pyi[:, :], lhsT=xcC[:, g, tsl], rhs=Ssb[:, g, :], start=True, stop=True))(pyi, g),
                   r=[xcn, "Ssb"], w=[pyin])
                (t1, t1n) = yt_pool.get()
                yl.append((py, pyn, pyi, pyin, t1, t1n))
            for g in range(2):
                (py, pyn, pyi, pyin, t1, t1n) = yl[g]
                op("vector", (lambda pyi, t1, g: lambda e: e.tensor_tensor(out=t1[:].rearrange("p (h q) -> p h q", h=8),
                                                                            in0=pyi[:, :].rearrange("p (h q) -> p h q", h=8),
                                                                            in1=ex[:, g * 8:(g + 1) * 8].unsqueeze(2).broadcast_to([128, 8, 64]), op=ALU.mult))(pyi, t1, g),
                   r=[pyin, exn], w=[t1n])
            for g in range(2):
                (py, pyn, pyi, pyin, t1, t1n) = yl[g]
                op("vector", (lambda py, t1: lambda e: e.tensor_tensor(out=t1[:], in0=py[:, :], in1=t1[:], op=ALU.add))(py, t1), r=[pyn, t1n], w=[t1n])
                yzs.append((t1, t1n))
        psl = []
        for g in range(2):
            (pss, pssn) = psb()
            op("tensor", (lambda pss, g: lambda e: e.matmul(out=pss[:, :], lhsT=Btm[:, g * 128:(g + 1) * 128], rhs=xdd[:, g * 512:(g + 1) * 512],
                                                             start=True, stop=True))(pss, g), r=[Btmn, xddn], w=[pssn])
            psl.append((pss, pssn))
        if saved_banks is not None:
            saved_banks.banks = (4, 5)
        for g in range(2):
            op("vector", (lambda g: lambda e: e.tensor_tensor(out=Ss[:, g, :].rearrange("p (h q) -> p h q", h=8),
                                                              in0=Ss[:, g, :].rearrange("p (h q) -> p h q", h=8),
                                                              in1=ex[:, 32 + g * 8:40 + g * 8].unsqueeze(2).broadcast_to([128, 8, 64]), op=ALU.mult))(g),
               r=["Ss%d" % g, exn, "Ssb"], w=["Ss%d" % g])
        for g in range(2):
            (pss, pssn) = psl[g]
            op("vector", (lambda pss, g: lambda e: e.tensor_tensor(out=Ss[:, g, :], in0=Ss[:, g, :], in1=pss[:, :], op=ALU.add))(pss, g),
               r=["Ss%d" % g, pssn], w=["Ss%d" % g])
        for g in range(2):
            op("scalar", (lambda g: lambda e: e.activation(out=Ssb[:, g, :], in_=Ss[:, g, :], func=AF.Copy))(g), r=["Ss%d" % g], w=["Ssb"])
        if main:
            (ssqs, ssqsn) = small.get()
            for g in range(2):
                (t1, t1n) = yzs[g]
                op("vector", (lambda t1, g: lambda e: e.tensor_tensor(out=t1[:], in0=t1[:], in1=xsD[:, g * 512:(g + 1) * 512], op=ALU.add))(t1, g),
                   r=[t1n, xsDn], w=[t1n])
            for g in range(2):
                (t1, t1n) = yzs[g]
                op("vector", (lambda t1, g: lambda e: e.tensor_tensor(out=t1[:], in0=t1[:], in1=sz[:, g * 512:(g + 1) * 512], op=ALU.mult))(t1, g),
                   r=[t1n, szn], w=[t1n])
            for g in range(2):
                (t1, t1n) = yzs[g]
                op("scalar", (lambda t1, g: lambda e: e.activation(out=mixed[:, 1024 + g * 512:1024 + (g + 1) * 512], in_=t1[:], func=AF.Square, accum_out=ssqs[:, g:g + 1]))(t1, g),
                   r=[t1n], w=[ssqsn, mixedn + "s%d" % g])
            rss, rssn = rstd_from(ssqs[:, 0:2], ssqsn, 2, 1.0 / 512)
            for g in range(2):
                t1, t1n = yzs[g]
                op("scalar", (lambda t1, g: lambda e: e.activation(out=mixed[:, 1024 + g * 512:1024 + (g + 1) * 512], in_=t1[:], func=AF.Copy,
                                                                    scale=rss[:, g:g + 1]))(t1, g), r=[t1n, rssn], w=[mixedn + "s%d" % g])
            r0 = tok0 + t * 128
            op("sync", (lambda r0: lambda e: e.dma_start(out=scr_d[r0:r0 + 128, :], in_=mixed[:]))(r0),
               r=[mixedn + "g", mixedn + "s0", mixedn + "s1"], w=["scr%d" % (r0 // 128)], dma="m_" + mixedn.split("#")[0])
        yield

    pendq = []

    def partner(nhalf):
        todo = []
        for a in pendq:
            while a[1] > 0 and len(todo) < nhalf:
                todo.append(a[0])
                a[1] -= 1
        pendq[:] = [a for a in pendq if a[1] > 0]
        if not todo:
            return None

        def f():
            for g in todo:
                next(g)
        return f

    def slot(main_fn, nhalf, extra=None, wmain=1):
        p = partner(nhalf)
        fns, bks, ws = [], [], []
        if p is not None:
            fns.append(p); bks.append((4, 5)); ws.append(1)
        fns.append(main_fn); bks.append((0, 1, 2, 3) if p is not None else (0, 1, 2, 3, 4, 5)); ws.append(wmain if p is not None else (2 if extra is not None else 1))
        if extra is not None:
            fns.append(extra); bks.append(None); ws.append(1)
        interleave(fns, bks, "tile", ws)

    def run_phase(src_d, ntok, main):
        nsup = ntok // NTOK
        if nsup == 0:
            return
        st = {"xns": phaseA_pre(x_loads(src_d, 0)), "xts": x_loads(src_d, NTOK) if nsup > 1 else None}
        for s in range(nsup):
            par = s % 2
            xns = st["xns"]
            extra0 = None
            if late_pieces:
                extra0 = (lambda ps_: lambda: load_pieces(ps_))(late_pieces.pop())
            slot(lambda: phaseAB(xns, main, par, main or s == nsup - 1), 1, extra0, wmain=6)
            for t in range(NT):
                g = tile_gen(s * NTOK, t, main, par)
                extra = None
                if t == NT - 1 and s + 1 < nsup:
                    def extra(s=s):
                        st["xns"] = phaseA_pre(st["xts"])
                        st["xts"] = x_loads(src_d, (s + 2) * NTOK) if s + 2 < nsup else None
                slot((lambda g: lambda: next(g))(g), 1 if t < NT - 1 else 99, extra, wmain=(2 if t < NT - 1 else 1))
                pendq.append([g, 2])

    def mask_gen():
        op("gpsimd", lambda e: e.tensor_scalar(out=Sg[:], in0=Sg[:], scalar1=flag[:, 0:1], scalar2=0.0, op0=ALU.mult, op1=ALU.add), r=["Sg0", "Sg1", "flag"], w=["Sg0", "Sg1"])
        op("gpsimd", lambda e: e.tensor_scalar(out=Ss[:], in0=Ss[:], scalar1=flag[:, 0:1], scalar2=0.0, op0=ALU.mult, op1=ALU.add),
           r=["Ss0", "Ss1", "flag"], w=["Ss0", "Ss1"])
        op("scalar", lambda e: e.activation(out=Sgb[:], in_=Sg[:], func=AF.Copy), r=["Sg0", "Sg1"], w=["Sgb"])
        op("scalar", lambda e: e.activation(out=Ssb[:], in_=Ss[:], func=AF.Copy), r=["Ss0", "Ss1"], w=["Ssb"])
        yield

    if STOP != "setup":
        run_phase(xp_d, TP, False)
    if TP > 0 and STOP not in ("setup", "pre"):
        pendq.append([mask_gen(), 1])
    if STOP not in ("setup", "pre"):
        run_phase(xm_d, TM, True)
    p_last = partner(99)
    if p_last is not None:
        p_last()

    if TM > 0 and STOP in (None, "p2w", "p2s0", "p2s1"):
        bar = sb("bar", [128, 1])
        op("gpsimd", lambda e: e.memset(bar[:], 0.0), w=ALLW + ["bar"])
        BAR = ["bar"]

        def f32view(a, b_):
            return big[:, a:b_].bitcast(F32)
        o_ = 26624
        mixin = [(big[:, o_:o_ + 2048], "mixin0"), (big[:, o_ + 2048:o_ + 4096], "mixin1")]; o_ += 4096
        mixTs = [(big[:, o_:o_ + 2048].rearrange("p (k t) -> p k t", k=16), "mixT0"),
                 (big[:, o_ + 2048:o_ + 4096].rearrange("p (k t) -> p k t", k=16), "mixT1")]; o_ += 4096
        hns = [(big[:, o_:o_ + 1024], "hn0"), (big[:, o_ + 1024:o_ + 2048], "hn1")]; o_ += 2048
        hnTs = [(big[:, o_:o_ + 1024].rearrange("p (k t) -> p k t", k=8), "hnT0"),
                (big[:, o_ + 1024:o_ + 2048].rearrange("p (k t) -> p k t", k=8), "hnT1")]; o_ += 2048
        hbufs = [(f32view(o_, o_ + 2048), "h0"), (f32view(o_ + 2048, o_ + 4096), "h1")]; o_ += 4096
        assert o_ <= 8 * DIN
        tgs = [(xsD_pool.bufs[0][0][:, :], "xsD0"), (xsD_pool.bufs[0][0][:, :], "xsD0")]
        obs = [(xt_pool.bufs[0][0][:, :], "xt0"), (xt_pool.bufs[1][0][:, :], "xt1")]
        xrs = [(mixed_pool.bufs[0][0][:, :].bitcast(F32), "mixed0"), (xsD_pool.bufs[1][0][:, :], "xsD1")]
        pts = [(yt_pool.bufs[0][0][:, 0:256], "ytmp0"), (yt_pool.bufs[1][0][:, 0:256], "ytmp1")]
        pbfs = [(Btm_pool.bufs[0][0][:, :], "Btm0"), (Btm_pool.bufs[1][0][:, :], "Btm1")]
        pTs = [(kd_pool.bufs[0][0][:, 0:256].rearrange("p (k t) -> p k t", k=2), "kd0"),
               (kd_pool.bufs[1][0][:, 0:256].rearrange("p (k t) -> p k t", k=2), "kd1"),
               (scm_pool.bufs[0][0][:, 0:2, :], "scm0")]
        fw_bc = sb("fw_bc", [128, D])
        op("sync", lambda e: e.dma_start(out=fw_bc[:], in_=fin_w_d.partition_broadcast(128)), w=["fw_bc"], dma="c_fw")
        def load_wout():
            for fc in range(16):
                rows = w_out_d[fc * 128:(fc + 1) * 128, :]
                if fc < 8:
                    sc, scn = gnw[:, (fc % 2):(fc % 2) + 1], "gnw"
                else:
                    sc, scn = snw[:, fc - 8:fc - 7], "snw"
                for c0 in (0, 512):
                    load_w(Wout, fc, c0, c0 + 512, rows, sc, scn, "Wout", BAR)

        def load_wg_wpe():
            for kc in range(8):
                for c0 in (0, 512):
                    load_w(Wg, kc, c0, c0 + 512, w_g_d[kc * 128:(kc + 1) * 128, :], pnw[:, kc:kc + 1], "pnw", "Wg", BAR)
            for c in range(2):
                for c0 in (0, 512):
                    load_w(Wpe, c, c0, c0 + 512, w_pe_d[c * 128:(c + 1) * 128, :], None, None, "Wpe", BAR)
        p2_loaders = [load_wg_wpe, load_wout]

        def p2_gen(ti):
            r0 = ti * 128
            (mi, min_) = mixin[ti % 2]
            (mixT, mixTn) = mixTs[ti % 2]
            (xr, xrn) = xrs[ti % 2]
            (pt, ptn) = pts[ti % 2]
            (pbf, pbfn) = pbfs[ti % 2]
            (pT, pTn) = pTs[ti % 3]
            (hb_, hn_) = hbufs[ti % 2]
            (hn, hnn) = hns[ti % 2]
            (hnT, hnTn) = hnTs[ti % 2]
            (tg, tgn0) = tgs[ti % 2]
            (ob, obn) = obs[ti % 2]
            op("sync", lambda e: e.dma_start(out=mi, in_=scr_d[r0:r0 + 128, :]), r=["scr%d" % ti] + BAR, w=[min_], dma="mi_" + min_)
            xr_w = [xrn] + (["mixed0g", "mixed0s0", "mixed0s1"] if xrn == "mixed0" else [])
            op("sync", lambda e: e.dma_start(out=xr, in_=xm_d[r0:r0 + 128, :]), w=xr_w, dma="x2_" + xrn)
            op("sync", lambda e: e.dma_start(out=pt, in_=pm_d[r0:r0 + 128, :]), w=[ptn], dma="p_" + ptn)
            transposes([mi[:, fc * 128:(fc + 1) * 128] for fc in range(8)], [min_], mixT[:, 0:8, :], mixTn + "a", "scalar")
            transposes([mi[:, fc * 128:(fc + 1) * 128] for fc in range(8, 16)], [min_], mixT[:, 8:16, :], mixTn + "b", "vector")
            op("gpsimd", lambda e: e.tensor_copy(out=pbf, in_=pt), r=[ptn], w=[pbfn])
            transposes([pbf[:, c * 128:(c + 1) * 128] for c in range(2)], [pbfn], pT, pTn, "vector")
            yield
            if STOP == "p2s0":
                return
            for half in range(2):
                (ph, phn) = psb()

                def fh(e, ph=ph, half=half):
                    last = None
                    for fc in range(16):
                        last = e.matmul(out=ph[:, :], lhsT=mixT[:, fc, :], rhs=Wout[:, fc, half * 512:(half + 1) * 512], start=(fc == 0), stop=(fc == 15))
                    return last
                op("tensor", fh, r=[mixTn + "a", mixTn + "b", "Wout"], w=[phn])
                op("vector", (lambda ph, half: lambda e: e.tensor_tensor(out=hb_[:, half * 512:(half + 1) * 512], in0=ph[:, :],
                                                                          in1=xr[:, half * 512:(half + 1) * 512], op=ALU.add))(ph, half),
                   r=[phn, xrn] + BAR, w=[hn_ + "_%d" % half])
            (ssqh, ssqhn) = small.get()
            op("scalar", lambda e: e.activation(out=hn, in_=hb_, func=AF.Square, accum_out=ssqh[:, 0:1]),
               r=[hn_ + "_0", hn_ + "_1"] + BAR, w=[ssqhn, hnn])
            rh, rhn = rstd_from(ssqh[:, 0:1], ssqhn, 1, 1.0 / D)
            op("scalar", lambda e: e.activation(out=hn, in_=hb_, func=AF.Copy, scale=rh[:, 0:1]),
               r=[hn_ + "_0", hn_ + "_1", rhn] + BAR, w=[hnn])
            transposes([hn[:, k * 128:(k + 1) * 128] for k in range(8)], [hnn], hnT, hnTn, "vector")
            yield
            if STOP == "p2s1":
                return
            (ssqf, ssqfn) = small.get()
            for half in range(2):
                (pgt, pgtn) = psb()

                def fg(e, pgt=pgt, half=half):
                    last = None
                    for kc in range(8):
                        last = e.matmul(out=pgt[:, :], lhsT=hnT[:, kc, :], rhs=Wg[:, kc, half * 512:(half + 1) * 512], start=(kc == 0), stop=(kc == 7))
                    return last
                op("tensor", fg, r=[hnTn, "Wg"], w=[pgtn])
                (ppe, ppen) = psb()

                def fpe(e, ppe=ppe, half=half):
                    last = None
                    for c in range(2):
                        last = e.matmul(out=ppe[:, :], lhsT=pT[:, c, :], rhs=Wpe[:, c, half * 512:(half + 1) * 512], start=(c == 0), stop=(c == 1))
                    return last
                op("tensor", fpe, r=[pTn, "Wpe"], w=[ppen])
                hs = slice(half * 512, (half + 1) * 512)
                tgn = tgn0 + "_%d" % half
                op("scalar", (lambda pgt, hs: lambda e: e.activation(out=tg[:, hs], in_=pgt[:, :], func=AF.Tanh, scale=0.5))(pgt, hs), r=[pgtn], w=[tgn0, tgn])
                op("vector", (lambda ppe, hs: lambda e: e.scalar_tensor_tensor(out=tg[:, hs], in0=tg[:, hs], scalar=1.0, in1=ppe[:, :], op0=ALU.add, op1=ALU.mult))(ppe, hs),
                   r=[ppen, tgn], w=[tgn])
                op("vector", (lambda hs, half: lambda e: e.scalar_tensor_tensor(out=hb_[:, hs], in0=tg[:, hs], scalar=0.5, in1=hb_[:, hs], op0=ALU.mult, op1=ALU.add))(hs, half),
                   r=[tgn, hn_ + "_%d" % half], w=[hn_ + "_%d" % half])
            op("scalar", lambda e: e.activation(out=ob, in_=hb_, func=AF.Square, accum_out=ssqf[:, 0:1]),
               r=[hn_ + "_0", hn_ + "_1"], w=[ssqfn, obn])
            rf, rfn = rstd_from(ssqf[:, 0:1], ssqfn, 1, 1.0 / D)
            op("vector", lambda e: e.scalar_tensor_tensor(out=ob, in0=hb_, scalar=rf[:, 0:1], in1=fw_bc[:], op0=ALU.mult, op1=ALU.mult),
               r=[hn_ + "_0", hn_ + "_1", rfn, "fw_bc"], w=[obn])
            op("sync", lambda e: e.dma_start(out=out_d[r0:r0 + 128, :], in_=ob), r=[obn], dma="o_" + obn)
            yield

        active = []
        ntile = TM // 128 if STOP != "p2w" else 0
        nxt = 0
        stage_banks = {0: (0, 1), 1: (2, 3, 4), 2: (5, 6, 7)}
        while nxt < ntile or active:
            fns = []
            bks = []
            for a in active:
                fns.append((lambda g: lambda: next(g, None))(a[0]))
                bks.append(stage_banks[a[1]])
            if nxt < ntile:
                a = [p2_gen(nxt), 0]
                nxt += 1
                active.append(a)
                fns.append((lambda g: lambda: next(g, None))(a[0]))
                bks.append(stage_banks[0])
            if p2_loaders:
                fns.append(p2_loaders.pop())
                bks.append(None)
            interleave(fns, bks, "p2")
            for a in active:
                a[1] += 1
            active = [a for a in active if a[1] < 3]

    S.emit(final_waits=[k for k in S.dma_streams if k.startswith("o_")])
    return nc


WKEYS = ["norm_w", "w_in", "gla_gate_up", "gla_gate_b", "gla_norm_w", "conv_w", "conv_b", "dt_bias", "a_log",
         "d_skip", "ssd_norm_w", "w_out", "w_pe", "w_pe_gate", "pe_norm_w"]


def run_layer(x, p, w, final_norm_w):
    B, T, _ = x.shape
    half = T // 2
    nc = build_program(half, half)
    wmap = {k: np.ascontiguousarray(np.asarray(w[k], dtype=np.float32)) for k in WKEYS}
    wmap["final_norm_w"] = np.ascontiguousarray(np.asarray(final_norm_w, dtype=np.float32))
    in_maps = []
    for b in range(B):
        for s in range(2):
            m = dict(wmap)
            m["xm"] = np.ascontiguousarray(x[b, s * half:(s + 1) * half])
            m["xp"] = np.ascontiguousarray(x[b, 0:half]) if s == 1 else np.zeros((half, D), np.float32)
            m["pm"] = np.ascontiguousarray(p[b, s * half:(s + 1) * half])
            m["flag"] = np.full((128, 1), float(s), np.float32)
            in_maps.append(m)
    ncores = 2 * B
    res = run_bass_kernel_spmd(nc, in_maps, core_ids=list(range(ncores)))
    out = np.empty((B, T, D), np.float32)
    for b in range(B):
        for s in range(2):
            out[b, s * half:(s + 1) * half] = res.results[2 * b + s]["out"]
    return out


def kernel(**inputs):
    x = np.asarray(inputs["x"], dtype=np.float32)
    p = np.asarray(inputs["p"], dtype=np.float32)
    w = {k: np.asarray(inputs[k])[0] for k in WKEYS}
    return run_layer(x, p[0], w, inputs["final_norm_w"])
```
